# Optimizing a Trainium2 kernel written in Bass

```python
import math
import numpy as np
import jax
import jax.numpy as jnp
from jax import lax

D_MODEL = 1024
BATCH = 32
SEQ = 256
DEPTH = 4
DEC_BATCH = 2
DEC_SEQ = 4096
PAST_LEN = 256

GRID_W = 64
MIX_W = D_MODEL
GROUP_W = MIX_W // 4
HEAD_DIM = 64
POOL_GROUPS = 4
POOL_GC = GROUP_W // POOL_GROUPS
POOL_WINDOWS = (2, 4, 8, 16)
NA_HEADS = GROUP_W // HEAD_DIM
NA_ROWS = 8
NA_COLS = 16
ATTN_BLOCK = 128
RET_HEADS = 4
RET_DK = GROUP_W // RET_HEADS
RET_DV = GROUP_W // RET_HEADS
RET_CHUNK = 128
GLA_HEADS = 4
GLA_DK = GROUP_W // (2 * GLA_HEADS)
GLA_DV = GROUP_W // GLA_HEADS
GLA_LOWRANK = 16
GLA_TAU = 16.0
GLA_CHUNK = 64
D_FF = -(-8 * D_MODEL // (3 * 256)) * 256
P_IN = GROUP_W + 3 * NA_HEADS * HEAD_DIM + 2 * RET_HEADS * (RET_DK + RET_DV) + 2 * GLA_HEADS * (GLA_DK + GLA_DV) + GLA_LOWRANK
ROPE_BASE = 10000.0
RMS_EPS = 1e-6
GN_EPS = 1e-5

kernel_name = 'hybrid_pool_na_ret_gla_diffusion_step'


def _split_points():
    sizes = (GROUP_W,
             NA_HEADS * HEAD_DIM, NA_HEADS * HEAD_DIM, NA_HEADS * HEAD_DIM,
             RET_HEADS * RET_DK, RET_HEADS * RET_DK, RET_HEADS * RET_DV, RET_HEADS * RET_DV,
             GLA_HEADS * GLA_DK, GLA_HEADS * GLA_DK, GLA_HEADS * GLA_DV, GLA_HEADS * GLA_DV,
             GLA_LOWRANK)
    return [int(s) for s in np.cumsum(sizes)[:-1]]


def _rmsnorm(x, g):
    xf = x.astype(jnp.float32)
    y = xf * lax.rsqrt(jnp.mean(xf * xf, axis=-1, keepdims=True) + RMS_EPS)
    return (y * g.astype(jnp.float32)).astype(x.dtype)


def _head_groupnorm(o):
    of = o.astype(jnp.float32)
    mu = jnp.mean(of, axis=-1, keepdims=True)
    var = jnp.mean(jnp.square(of - mu), axis=-1, keepdims=True)
    return ((of - mu) * lax.rsqrt(var + GN_EPS)).astype(o.dtype)


def _heads(t, n):
    return t.reshape(t.shape[:-1] + (n, t.shape[-1] // n))


def _rope_axis(x, pos):
    nf = x.shape[-1] // 2
    inv = (ROPE_BASE ** (-np.arange(nf, dtype=np.float32) / nf)).astype(np.float32)
    ang = pos.astype(np.float32)[:, None] * inv[None, :]
    cos = jnp.asarray(np.cos(ang), dtype=x.dtype)[None, :, None, :]
    sin = jnp.asarray(np.sin(ang), dtype=x.dtype)[None, :, None, :]
    x1, x2 = x[..., :nf], x[..., nf:]
    return jnp.concatenate([x1 * cos - x2 * sin, x1 * sin + x2 * cos], axis=-1)


def _rope_2d(x):
    n = x.shape[1]
    t = np.arange(n)
    half = x.shape[-1] // 2
    return jnp.concatenate([_rope_axis(x[..., :half], t // GRID_W),
                            _rope_axis(x[..., half:], t % GRID_W)], axis=-1)


def _pool_mix(v, w_pool, scale):
    L = v.shape[-2]
    vf = v.astype(jnp.float32)
    cs = jnp.cumsum(vf, axis=-2)
    cs = jnp.concatenate([jnp.zeros_like(cs[..., :1, :]), cs], axis=-2)
    t = np.arange(L)
    means = []
    for gi, win in enumerate(POOL_WINDOWS):
        left = win // 2
        lo = np.clip(t - left, 0, L).astype(np.int32)
        hi = np.clip(t - left + win, 0, L).astype(np.int32)
        csg = cs[..., gi * POOL_GC:(gi + 1) * POOL_GC]
        cnt = jnp.asarray((hi - lo).astype(np.float32))[:, None]
        means.append((jnp.take(csg, hi, axis=-2) - jnp.take(csg, lo, axis=-2)) / cnt)
    d = (jnp.concatenate(means, axis=-1) - vf).astype(v.dtype)
    d = d.reshape(d.shape[:-1] + (POOL_GROUPS, POOL_GC))
    y = jnp.einsum('...gc,gcd->...gd', d, w_pool)
    return y.reshape(y.shape[:-2] + (GROUP_W,)) * scale


def _ctx_attention(q, k, v):
    B, L, H, dh = q.shape
    nb = L // ATTN_BLOCK
    qb = q.reshape(B, nb, ATTN_BLOCK, H, dh).transpose(1, 0, 2, 3, 4)
    scale = dh ** -0.5

    def block(qi):
        s = jnp.einsum('bqhd,bkhd->bhqk', qi, k).astype(jnp.float32) * scale
        p = jax.nn.softmax(s, axis=-1).astype(v.dtype)
        return jnp.einsum('bhqk,bkhd->bqhd', p, v)

    o = lax.map(block, qb)
    return o.transpose(1, 0, 2, 3, 4).reshape(B, L, H * dh)


def _na_latent(q, k, v, k_ctx, v_ctx, rpb):
    B, N, H, dh = q.shape
    rows = N // GRID_W
    kr = min(NA_ROWS, rows)
    kc = NA_COLS
    cols = np.arange(GRID_W)
    c0 = np.clip(cols - kc // 2, 0, GRID_W - kc)
    col_idx = (c0[:, None] + np.arange(kc)[None, :]).astype(np.int32)
    dc = (col_idx - cols[:, None] + (NA_COLS - 1)).astype(np.int32)
    scale = dh ** -0.5
    qr = q.reshape(B, rows, GRID_W, H, dh).transpose(1, 0, 2, 3, 4)

    def one_row(args):
        r, q_row = args
        r0 = jnp.clip(r - kr // 2, 0, rows - kr)
        key_rows = r0 + jnp.arange(kr, dtype=jnp.int32)
        tok = key_rows[None, :, None] * GRID_W + col_idx[:, None, :]
        kw = k[:, tok]
        vw = v[:, tok]
        dr = key_rows - r + (NA_ROWS - 1)
        bias = rpb[:, dr[None, :, None], dc[:, None, :]].astype(jnp.float32)
        s_loc = jnp.einsum('bwhd,bwrchd->bhwrc', q_row, kw).astype(jnp.float32) * scale + bias
        s_ctx = jnp.einsum('bwhd,bmhd->bhwm', q_row, k_ctx).astype(jnp.float32) * scale
        s = jnp.concatenate([s_loc.reshape(B, H, GRID_W, kr * kc), s_ctx], axis=-1)
        p = jax.nn.softmax(s, axis=-1).astype(v.dtype)
        p_loc = p[..., :kr * kc].reshape(B, H, GRID_W, kr, kc)
        return (jnp.einsum('bhwrc,bwrchd->bwhd', p_loc, vw)
                + jnp.einsum('bhwm,bmhd->bwhd', p[..., kr * kc:], v_ctx))

    o = lax.map(one_row, (jnp.arange(rows, dtype=jnp.int32), qr))
    return o.transpose(1, 0, 2, 3, 4).reshape(B, N, H * dh)


def _retention_scan(q, k, v, log_g, s0):
    B, L, H, dk = q.shape
    dv = v.shape[-1]
    C = RET_CHUNK
    n = L // C

    def chunks(t):
        return t.reshape(B, n, C, H, t.shape[-1]).transpose(1, 0, 3, 2, 4)

    idx = np.arange(C, dtype=np.float32)
    diff = idx[:, None] - idx[None, :]
    lg = log_g[:, None, None]
    d_intra = jnp.where(diff >= 0, jnp.exp(np.maximum(diff, 0.0) * lg), 0.0).astype(q.dtype)
    d_q = jnp.exp((idx + 1.0)[None, :, None] * lg).astype(q.dtype)
    d_k = jnp.exp((C - 1.0 - idx)[None, :, None] * lg).astype(q.dtype)
    d_c = jnp.exp(C * lg).astype(q.dtype)

    def step(S, inp):
        qc, kc, vc = inp
        a = jnp.einsum('bhid,bhjd->bhij', qc, kc) * d_intra
        o = jnp.einsum('bhij,bhjv->bhiv', a, vc) + jnp.einsum('bhid,bhdv->bhiv', qc, S) * d_q
        S = S * d_c + jnp.einsum('bhjd,bhjv->bhdv', kc * d_k, vc)
        return S.astype(s0.dtype), o

    S, o = lax.scan(step, s0, (chunks(q), chunks(k), chunks(v)))
    return o.transpose(1, 0, 3, 2, 4).reshape(B, L, H, dv), S


def _retention(q, k, v, g, decay_logit, s0):
    q = q * (RET_DK ** -0.5)
    log_g = jax.nn.log_sigmoid(decay_logit.astype(jnp.float32))
    o_f, s_f = _retention_scan(q, k, v, log_g[0], s0[:, 0])
    o_b, s_b = _retention_scan(q[:, ::-1], k[:, ::-1], v[:, ::-1], log_g[1], s0[:, 1])
    o = _head_groupnorm(o_f + o_b[:, ::-1])
    o = o.reshape(g.shape) * jax.nn.silu(g)
    return o, jnp.stack([s_f, s_b], axis=1)


def _gla_scan(q, k, v, log_a, s0):
    B, L, H, dk = q.shape
    dv = v.shape[-1]
    C = GLA_CHUNK
    n = L // C

    def chunks(t):
        return t.reshape(B, n, C, H, t.shape[-1]).transpose(1, 0, 3, 2, 4)

    mask = np.tril(np.ones((C, C), dtype=bool))[:, :, None]

    def step(S, inp):
        qc, kc, vc, ac = inp
        b = jnp.cumsum(ac, axis=2)
        rel = jnp.where(mask, b[:, :, :, None, :] - b[:, :, None, :, :], -jnp.inf)
        a = jnp.einsum('bhid,bhjd,bhijd->bhij', qc, kc, jnp.exp(rel).astype(qc.dtype))
        o = (jnp.einsum('bhij,bhjv->bhiv', a, vc)
             + jnp.einsum('bhid,bhdv->bhiv', qc * jnp.exp(b).astype(qc.dtype), S))
        bl = b[:, :, -1:, :]
        S = (S * jnp.exp(bl[:, :, 0, :, None]).astype(S.dtype)
             + jnp.einsum('bhjd,bhjv->bhdv', kc * jnp.exp(bl - b).astype(kc.dtype), vc))
        return S.astype(s0.dtype), o

    S, o = lax.scan(step, s0, (chunks(q), chunks(k), chunks(v), chunks(log_a)))
    return o.transpose(1, 0, 3, 2, 4).reshape(B, L, H, dv), S


def _gla(q, k, v, g, lr, gate_up, gate_b, norm_g, s0):
    q = q * (GLA_DK ** -0.5)

    def log_gate(d):
        z = (lr @ gate_up[d] + gate_b[d]).astype(jnp.float32)
        return _heads(jax.nn.log_sigmoid(z) / GLA_TAU, GLA_HEADS)

    o_f, s_f = _gla_scan(q, k, v, log_gate(0), s0[:, 0])
    o_b, s_b = _gla_scan(q[:, ::-1], k[:, ::-1], v[:, ::-1], log_gate(1)[:, ::-1], s0[:, 1])
    o = _rmsnorm(o_f + o_b[:, ::-1], norm_g)
    o = o.reshape(g.shape) * jax.nn.silu(g)
    return o, jnp.stack([s_f, s_b], axis=1)


def _token_mixers(h, latent, w_in_l, pool_w_l, pool_scale_l, rpb_l, ret_logit_l, gla_up_l, gla_b_l,
                  gla_ng_l, ctx_k, ctx_v, ret_s0, gla_s0):
    B, L, _ = h.shape
    parts = jnp.split(h @ w_in_l, _split_points(), axis=-1)
    v_pool, na_q, na_k, na_v, r_q, r_k, r_v, r_g, a_q, a_k, a_v, a_g, a_lr = parts
    na_q, na_k, na_v = _heads(na_q, NA_HEADS), _heads(na_k, NA_HEADS), _heads(na_v, NA_HEADS)
    r_q, r_k, r_v = _heads(r_q, RET_HEADS), _heads(r_k, RET_HEADS), _heads(r_v, RET_HEADS)
    a_q, a_k, a_v = _heads(a_q, GLA_HEADS), _heads(a_k, GLA_HEADS), _heads(a_v, GLA_HEADS)
    if latent:
        rows = L // GRID_W
        o_pool = _pool_mix(v_pool.reshape(B, rows, GRID_W, GROUP_W), pool_w_l, pool_scale_l).reshape(B, L, GROUP_W)
        o_na = _na_latent(na_q, na_k, na_v, ctx_k, ctx_v, rpb_l)
        r_q, r_k = _rope_2d(r_q), _rope_2d(r_k)
    else:
        o_pool = _pool_mix(v_pool, pool_w_l, pool_scale_l)
        o_na = _ctx_attention(na_q, na_k, na_v)
        ret_s0 = jnp.zeros((B, 2, RET_HEADS, RET_DK, RET_DV), h.dtype)
        gla_s0 = jnp.zeros((B, 2, GLA_HEADS, GLA_DK, GLA_DV), h.dtype)
    o_ret, s_ret = _retention(r_q, r_k, r_v, r_g, ret_logit_l, ret_s0)
    o_gla, s_gla = _gla(a_q, a_k, a_v, a_g, a_lr, gla_up_l, gla_b_l, gla_ng_l, gla_s0)
    o = jnp.concatenate([o_pool, o_na, o_ret, o_gla], axis=-1)
    if latent:
        return o, None, None, None, None
    return o, na_k, na_v, s_ret, s_gla


def setup_inputs(seed: int = 0) -> dict:
    key = jax.random.key(seed)
    ks = jax.random.split(key, 26)
    f32 = jnp.float32
    D = D_MODEL

    def nrm(k, shape, s):
        return jax.random.normal(k, shape, f32) * s

    ret_base = np.log(2.0 ** (5 + np.arange(RET_HEADS)) - 1.0).astype(np.float32)
    return {
        'x_prompt': nrm(ks[0], (BATCH, SEQ, D), 1.0),
        'x_sample': nrm(ks[1], (DEC_BATCH, DEC_SEQ, D), 1.0),
        'cache_na_k': nrm(ks[2], (DEC_BATCH, DEPTH, PAST_LEN, NA_HEADS, HEAD_DIM), 1.0),
        'cache_na_v': nrm(ks[3], (DEC_BATCH, DEPTH, PAST_LEN, NA_HEADS, HEAD_DIM), 1.0),
        'state_ret': nrm(ks[4], (DEC_BATCH, DEPTH, 2, RET_HEADS, RET_DK, RET_DV), 1.0),
        'state_gla': nrm(ks[5], (DEC_BATCH, DEPTH, 2, GLA_HEADS, GLA_DK, GLA_DV), 1.0),
        'c': nrm(ks[6], (DEC_BATCH, D), 1.0),
        'c_ctx': nrm(ks[7], (D,), 1.0),
        'w_mod': nrm(ks[8], (DEPTH, D, 6 * D), D ** -0.5),
        'b_mod': nrm(ks[9], (DEPTH, 6 * D), 0.01),
        'g_pre_mix': 1.0 + nrm(ks[10], (DEPTH, D), 0.05),
        'g_post_mix': 1.0 + nrm(ks[11], (DEPTH, D), 0.05),
        'g_pre_ffn': 1.0 + nrm(ks[12], (DEPTH, D), 0.05),
        'g_post_ffn': 1.0 + nrm(ks[13], (DEPTH, D), 0.05),
        'w_in': nrm(ks[14], (DEPTH, D, P_IN), D ** -0.5),
        'w_out': nrm(ks[15], (DEPTH, MIX_W, D), MIX_W ** -0.5),
        'pool_w': nrm(ks[16], (DEPTH, POOL_GROUPS, POOL_GC, POOL_GC), POOL_GC ** -0.5),
        'pool_scale': 1.0 + nrm(ks[17], (DEPTH, GROUP_W), 0.05),
        'na_rpb': nrm(ks[18], (DEPTH, NA_HEADS, 2 * NA_ROWS - 1, 2 * NA_COLS - 1), 0.02),
        'ret_decay_logit': jnp.asarray(ret_base)[None, None, :] + nrm(ks[19], (DEPTH, 2, RET_HEADS), 0.1),
        'gla_gate_up': nrm(ks[20], (DEPTH, 2, GLA_LOWRANK, GLA_HEADS * GLA_DK), GLA_LOWRANK ** -0.5),
        'gla_gate_b': nrm(ks[21], (DEPTH, 2, GLA_HEADS * GLA_DK), 0.01),
        'gla_norm_g': 1.0 + nrm(ks[22], (DEPTH, GLA_DV), 0.05),
        'w_ffn_gate': nrm(ks[23], (DEPTH, D, D_FF), D ** -0.5),
        'w_ffn_up': nrm(ks[24], (DEPTH, D, D_FF), D ** -0.5),
        'w_ffn_down': nrm(ks[25], (DEPTH, D_FF, D), D_FF ** -0.5),
    }


def reference(x_prompt, x_sample, cache_na_k, cache_na_v, state_ret, state_gla, c, c_ctx,
              w_mod, b_mod, g_pre_mix, g_post_mix, g_pre_ffn, g_post_ffn, w_in, w_out,
              pool_w, pool_scale, na_rpb, ret_decay_logit, gla_gate_up, gla_gate_b, gla_norm_g,
              w_ffn_gate, w_ffn_up, w_ffn_down):

    def run_layer(l, x, cvec, latent, ctx_k, ctx_v, ret_s0, gla_s0):
        mod = jax.nn.silu(cvec) @ w_mod[l] + b_mod[l]
        sh1, sc1, gt1, sh2, sc2, gt2 = jnp.split(mod[:, None, :], 6, axis=-1)
        h = _rmsnorm(x, g_pre_mix[l]) * (1.0 + sc1) + sh1
        o, k_na, v_na, s_ret, s_gla = _token_mixers(
            h, latent, w_in[l], pool_w[l], pool_scale[l], na_rpb[l], ret_decay_logit[l],
            gla_gate_up[l], gla_gate_b[l], gla_norm_g[l], ctx_k, ctx_v, ret_s0, gla_s0)
        x = x + gt1 * _rmsnorm(o @ w_out[l], g_post_mix[l])
        h = _rmsnorm(x, g_pre_ffn[l]) * (1.0 + sc2) + sh2
        y = (jax.nn.silu(h @ w_ffn_gate[l]) * (h @ w_ffn_up[l])) @ w_ffn_down[l]
        x = x + gt2 * _rmsnorm(y, g_post_ffn[l])
        return x, k_na, v_na, s_ret, s_gla

    xp = x_prompt
    cvec_ctx = c_ctx[None, :]
    ks_list, vs_list, sr_list, sg_list = [], [], [], []
    for l in range(DEPTH):
        xp, k_na, v_na, s_r, s_g = run_layer(l, xp, cvec_ctx, False, None, None, None, None)
        ks_list.append(k_na)
        vs_list.append(v_na)
        sr_list.append(s_r)
        sg_list.append(s_g)

    xs = x_sample
    for l in range(DEPTH):
        xs, _, _, _, _ = run_layer(l, xs, c, True, cache_na_k[:, l], cache_na_v[:, l],
                                   state_ret[:, l], state_gla[:, l])

    new_cache_na_k = jnp.stack(ks_list, axis=1)
    new_cache_na_v = jnp.stack(vs_list, axis=1)
    new_state_ret = jnp.stack(sr_list, axis=1)
    new_state_gla = jnp.stack(sg_list, axis=1)
    return (xp, xs, new_cache_na_k, new_cache_na_v, new_state_ret, new_state_gla)
```

```python
import numpy as np
from contextlib import ExitStack
import concourse.bass as bass
import concourse.mybir as mybir
from concourse.bass_utils import run_bass_kernel_spmd

F32 = mybir.dt.float32
F32R = mybir.dt.float32r
BF16 = mybir.dt.bfloat16
AF = mybir.ActivationFunctionType
ALU = mybir.AluOpType
AX = mybir.AxisListType

D = 1024
DEPTH = 4
NPR = 1024
NSM = 4096
NSP = NSM + 128
TT = 512
P_IN = 2832
DFF = 2816
COMPUTE = ("pe", "dve", "act", "pool")
NEG = -30000.0


class Sched:
    def __init__(self, nc, stack, n_dma=48):
        self.nc = nc
        self.eng = {"pe": nc.tensor, "dve": nc.vector, "act": nc.scalar,
                    "pool": nc.gpsimd, "sp": nc.sync}
        self.sem = {e: stack.enter_context(nc.semaphore("s_" + e)) for e in COMPUTE}
        self.cnt = {e: 0 for e in COMPUTE}
        self.dsem = [stack.enter_context(nc.semaphore("d%d" % i)) for i in range(n_dma)]
        self.dval = [0] * n_dma
        self.dq = [0, 0]
        self.ndma = n_dma
        self.seen = {e: {} for e in self.eng}
        self.last_w = {}
        self.readers = {}

    def _wait(self, e, tok):
        kind, key, val = tok
        if kind == "c" and key == e and e == "pe":
            return
        k = (kind, key)
        if self.seen[e].get(k, 0) >= val:
            return
        sem = self.sem[key] if kind == "c" else self.dsem[key]
        self.eng[e].wait_ge(sem, val)
        self.seen[e][k] = val

    def _deps(self, reads, writes):
        deps = {}
        for r in reads:
            t = self.last_w.get(r)
            if t is not None and deps.get((t[0], t[1]), 0) < t[2]:
                deps[(t[0], t[1])] = t[2]
        for w in writes:
            t = self.last_w.get(w)
            if t is not None and deps.get((t[0], t[1]), 0) < t[2]:
                deps[(t[0], t[1])] = t[2]
            for k, v in self.readers.get(w, {}).items():
                if deps.get(k, 0) < v:
                    deps[k] = v
        return deps

    def _commit(self, tok, reads, writes):
        for w in writes:
            self.last_w[w] = tok
            self.readers[w] = {}
        k = (tok[0], tok[1])
        for r in reads:
            d = self.readers.setdefault(r, {})
            if d.get(k, 0) < tok[2]:
                d[k] = tok[2]

    def op(self, e, fn, reads=(), writes=()):
        for k, v in self._deps(reads, writes).items():
            self._wait(e, (k[0], k[1], v))
        ins = fn(self.eng[e])
        self.cnt[e] += 1
        ins.then_inc(self.sem[e], 1)
        self._commit(("c", e, self.cnt[e]), reads, writes)

    def dma(self, q, out, in_, reads=(), writes=()):
        half = self.ndma // 2
        qi = 0 if q == "sp" else 1
        slot = qi * half + self.dq[qi]
        self.dq[qi] = (self.dq[qi] + 1) % half
        if self.dval[slot] > 0:
            self._wait(q, ("d", slot, self.dval[slot]))
        for k, v in self._deps(reads, writes).items():
            self._wait(q, (k[0], k[1], v))
        self.dval[slot] += 16
        self.eng[q].dma_start(out=out, in_=in_).then_inc(self.dsem[slot], 16)
        self._commit(("d", slot, self.dval[slot]), reads, writes)

    def finish(self, e=None):
        for en in (list(self.eng) if e is None else [e]):
            for slot in range(self.ndma):
                if self.dval[slot] > 0:
                    self._wait(en, ("d", slot, self.dval[slot]))
            for c in COMPUTE:
                if self.cnt[c] > 0 and c != en:
                    self._wait(en, ("c", c, self.cnt[c]))


class Rot:
    uid = [0]

    def __init__(self, nc, stack, name, shape, dtype, n):
        Rot.uid[0] += 1
        self.t = [stack.enter_context(nc.sbuf_tensor("%s_%d_r%d" % (name, i, Rot.uid[0]), list(shape), dtype)) for i in range(n)]
        self.k = ["%s_%d" % (name, i) for i in range(n)]
        self.i = 0

    def next(self):
        i = self.i
        self.i = (i + 1) % len(self.t)
        return self.t[i], self.k[i]


def _pool_consts():
    wins = (2, 4, 8, 16)

    def mat(L):
        t = np.arange(L)
        M = np.zeros((4, L, L), np.float64)
        for gi, win in enumerate(wins):
            left = win // 2
            lo = np.clip(t - left, 0, L)
            hi = np.clip(t - left + win, 0, L)
            for tt in range(L):
                M[gi, lo[tt]:hi[tt], tt] = 1.0 / (hi[tt] - lo[tt])
                M[gi, tt, tt] -= 1.0
        return M
    m64 = mat(64)
    m256 = mat(256)
    out = np.zeros((128, 4, 5, 128), np.float32)
    for g in range(4):
        out[0:64, g, 0, 0:64] = m64[g]
        out[64:128, g, 0, 64:128] = m64[g]
        for a in range(2):
            for b in range(2):
                out[:, g, 1 + a * 2 + b, :] = m256[g, b * 128:(b + 1) * 128, a * 128:(a + 1) * 128]
    return out


def _rope_consts():
    nf = 16
    inv = (10000.0 ** (-np.arange(nf, dtype=np.float32) / nf)).astype(np.float32)
    t = np.arange(NSM)
    cos = np.zeros((64, NSM), np.float32)
    sin = np.zeros((64, NSM), np.float32)
    perm = np.zeros((64, 64), np.float32)
    for d in range(64):
        blk = d // 32
        dd = d % 32
        pos = (t // 64) if blk == 0 else (t % 64)
        ang = pos.astype(np.float32) * inv[dd % nf]
        cos[d] = np.cos(ang).astype(np.float32)
        if dd < nf:
            sin[d] = -np.sin(ang).astype(np.float32)
            partner = d + nf
        else:
            sin[d] = np.sin(ang).astype(np.float32)
            partner = d - nf
        perm[partner, d] = 1.0
    return cos, sin, perm


NA_VARIANTS = (0, 1, 2, 30, 31)


def _na_base(l2):
    return int(np.clip(2 * l2 - 4, 0, 56))


def _na_var(l2):
    if l2 <= 1:
        return l2
    if l2 >= 30:
        return l2 - 27
    return 2


def _na_bias_tables(rpb):
    L = rpb.shape[0]
    out = np.full((L, 5, 4, 128, 576), NEG, np.float32)
    cols = np.arange(64)
    c0 = np.clip(cols - 8, 0, 48)
    for vi, l2 in enumerate(NA_VARIANTS):
        base = _na_base(l2)
        for a in range(2):
            r = 2 * l2 + a
            r0 = int(np.clip(r - 4, 0, 56))
            for kr in range(r0, r0 + 8):
                kl = kr - base
                assert 0 <= kl < 9
                dr = kr - r + 7
                for c in range(64):
                    d0 = int(c0[c]) - c + 15
                    out[:, vi, :, a * 64 + c, kl * 64 + int(c0[c]):kl * 64 + int(c0[c]) + 16] = rpb[:, :, dr, d0:d0 + 16]
    return out


class Builder:
    def __init__(self, depth=DEPTH):
        self.depth = depth
        nc = bass.Bass("TRN2", target_bir_lowering=False)
        self.nc = nc
        L = depth

        def din(name, shape):
            return nc.dram_tensor(name, list(shape), F32, kind="ExternalInput").ap()

        def dout(name, shape):
            return nc.dram_tensor(name, list(shape), F32, kind="ExternalOutput").ap()

        def dscr(name, shape):
            return nc.dram_tensor(name, list(shape), F32, kind="Internal").ap()

        self.xin = {"p": din("xp", (NPR, D)), "s": din("xs", (NSM, D))}
        self.cvec = din("cvec", (128, 8, 2))
        self.w_mod = din("w_mod", (L, D, 6 * D))
        self.b_mod = din("b_mod", (128, L, 48))
        self.gvec = [din(n, (128, L, 8)) for n in ("g_pre_mix", "g_post_mix", "g_pre_ffn", "g_post_ffn")]
        self.w_in = din("w_in", (L, D, P_IN))
        self.w_out = din("w_out", (L, D, D))
        self.pool_w = din("pool_w", (L, 4, 64, 64))
        self.pool_scale = din("pool_scale", (L, 256))
        self.ret_logit = din("ret_decay_logit", (L, 8))
        self.gate_up = din("gla_gate_up", (L, 2, 16, 128))
        self.gate_b = din("gla_gate_b", (L, 2, 128))
        self.norm_g = din("gla_norm_g", (L, 64))
        self.w_gate = din("w_ffn_gate", (L, D, DFF))
        self.w_up = din("w_ffn_up", (L, D, DFF))
        self.w_down = din("w_ffn_down", (L, DFF, D))
        self.ck = din("ck", (L, 256, 256))
        self.cv = din("cv", (L, 256, 256))
        self.sret = din("sret", (L, 2, 4, 64, 64))
        self.sgla = din("sgla", (L, 2, 4, 32, 64))
        self.nabias = din("nabias", (L, 5, 4, 128, 576))
        self.c_ident = din("c_ident", (128, 128))
        self.c_uf = din("c_uf", (128, 128))
        self.c_ub = din("c_ub", (128, 128))
        self.c_ufg = din("c_ufg", (128, 128))
        self.c_ubg = din("c_ubg", (128, 128))
        self.c_poolm = din("c_poolm", (128, 4, 5, 128))
        self.c_cos = din("c_cos", (64, NSM))
        self.c_sin = din("c_sin", (64, NSM))
        self.c_perm = din("c_perm", (64, 64))
        self.c_zero = din("c_zero", (128, 256))

        self.yout = {"p": dout("yp", (NPR, D)), "s": dout("ys", (NSM, D))}
        self.o_ck = dout("o_ck", (4, L, 256, 256))
        self.o_cv = dout("o_cv", (4, L, 256, 256))
        self.o_sret = dout("o_sret", (4, L, 2, 4, 64, 64))
        self.o_sgla = dout("o_sgla", (4, L, 2, 4, 32, 64))

        self.N = {"p": NPR, "s": NSM}
        self.NP = {"p": NPR, "s": NSP}
        self.xT = {g: dscr("xT_" + g, (8, 128, self.N[g])) for g in "ps"}
        self.fm = {}
        self.tm = {}
        for g in "ps":
            n = self.NP[g]
            for nm, rows in (("naq", 256), ("nak", 256), ("rq", 256), ("rk", 256), ("aq", 128), ("ak", 128), ("alr", 16)):
                self.fm[(nm, g)] = dscr("fm_%s_%s" % (nm, g), (rows, n))
            for nm, cols in (("vpool", 256), ("nav", 256), ("rvg", 512), ("av", 256), ("ag", 256), ("ofret", 256), ("ofgla", 256), ("o", 1024)):
                self.tm[(nm, g)] = dscr("tm_%s_%s" % (nm, g), (n, cols))

        with ExitStack() as st:
            self.st = st
            self.S = Sched(nc, st)
            self.build()

    def sb(self, stack, name, shape, dtype=F32):
        Rot.uid[0] += 1
        return stack.enter_context(self.nc.sbuf_tensor("%s_u%d" % (name, Rot.uid[0]), list(shape), dtype))

    def bank(self):
        i = self.bi
        self.bi = (i + 1) % 8
        return self.banks[i], "bank%d" % i

    def evac(self, out, in_, reads, writes):
        self.ev = 1 - self.ev
        if self.ev:
            self.S.op("act", lambda e: e.activation(out=out, in_=in_, func=AF.Copy), reads, writes)
        else:
            self.S.op("dve", lambda e: e.tensor_copy(out=out, in_=in_), reads, writes)

    def build(self):
        nc, S, st = self.nc, self.S, self.st
        L = self.depth
        self.bi = 0
        self.ev = 0
        self.banks = [st.enter_context(nc.psum_tensor("bank%d" % i, [128, 512], F32)) for i in range(8)]

        self.ident = self.sb(st, "ident", (128, 128))
        self.uf = self.sb(st, "uf", (128, 128))
        self.ub = self.sb(st, "ub", (128, 128))
        self.ones_f = self.sb(st, "ones_f", (128, 128))
        self.ones_r = self.sb(st, "ones_r", (128, 128), F32R)
        S.dma("sp", self.ident[:], self.c_ident[:, :], writes=["ident"])
        S.dma("sp", self.uf[:], self.c_uf[:, :], writes=["uf"])
        S.dma("sp", self.ub[:], self.c_ub[:, :], writes=["ub"])
        self.ufg = self.sb(st, "ufg", (128, 128))
        self.ubg = self.sb(st, "ubg", (128, 128))
        S.dma("sp", self.ufg[:], self.c_ufg[:, :], writes=["ufg"])
        S.dma("sp", self.ubg[:], self.c_ubg[:, :], writes=["ubg"])
        S.op("dve", lambda e: e.memset(self.ones_f[:], 1.0), writes=["ones_f"])
        self.zeros_f = self.sb(st, "zeros_f", (128, 256))
        S.op("dve", lambda e: e.memset(self.zeros_f[:], 0.0), writes=["zeros_f"])
        S.op("dve", lambda e: e.tensor_copy(out=self.ones_r[:], in_=self.ones_f[:]), reads=["ones_f"], writes=["ones_r"])
        self.ones_b = self.sb(st, "ones_b", (128, 128), BF16)
        S.op("dve", lambda e: e.tensor_copy(out=self.ones_b[:], in_=self.ones_f[:]), reads=["ones_f"], writes=["ones_b"])

        self.gv = []
        for i, gsrc in enumerate(self.gvec):
            t = self.sb(st, "gv%d" % i, (128, L, 8))
            S.dma("sp", t[:], gsrc[:, :, :], writes=["gv%d" % i])
            self.gv.append(t)
        self.modT = self.sb(st, "modT", (128, L, 48, 2))
        self.bmod = self.sb(st, "bmod", (128, L, 48))
        S.dma("sp", self.bmod[:], self.b_mod[:, :, :], writes=["bmod"])
        self.gm = [self.sb(st, "gm%d" % i, (128, L, 8, 2)) for i in range(4)]

        with ExitStack() as ph:
            z = self.sb(ph, "ztile", (128, 512))
            S.op("dve", lambda e: e.memset(z[:], 0.0), writes=["ztile"])
            for nm in ("nak",):
                S.dma("sp", self.fm[(nm, "s")][0:128, NSM:NSP], z[:, 0:128], reads=["ztile"], writes=[("fm", nm, "s", "pad")])
                S.dma("sp", self.fm[(nm, "s")][128:256, NSM:NSP], z[:, 0:128], reads=["ztile"], writes=[("fm", nm, "s", "pad")])
            S.dma("sp", self.tm[("nav", "s")][NSM:NSP, :], z[:, 0:256], reads=["ztile"], writes=[("tm", "nav", "s", "pad")])
            S.finish()

        import os
        stop = os.environ.get("KSTOP", "")
        grps = os.environ.get("KGRPS", "ps")
        self.phase_mod()
        if stop != "mod":
            self.phase_x()
        if stop not in ("mod", "x"):
            for l in range(L):
                self.phase_a(l, grps)
                if stop == "a":
                    break
                for g in grps:
                    self.mix_pool(l, g)
                    if stop == "pool":
                        continue
                    self.mix_attn(l, g)
                    if stop == "attn":
                        continue
                    self.mix_lin_pair(l, g)
                if stop in ("pool", "attn", "ret", "gla"):
                    break
                self.phase_b(l, grps, last=(l == L - 1))
        S.finish()

    def phase_mod(self):
        nc, S = self.nc, self.S
        L = self.depth
        with ExitStack() as ph:
            cv = self.sb(ph, "cvt", (128, 8, 2))
            scv = self.sb(ph, "scv", (128, 8, 2), F32R)
            S.dma("sp", cv[:], self.cvec[:, :, :], writes=["cvt"])
            S.op("act", lambda e: e.activation(out=scv[:], in_=cv[:], func=AF.Silu), reads=["cvt"], writes=["scv"])
            wr = Rot(nc, ph, "wm", (128, 8, 512), F32R, 3)
            for l in range(L):
                wv = self.w_mod[l].rearrange("(c p) f -> p c f", p=128)
                for jb in range(12):
                    wt, wk = wr.next()
                    S.dma("pool", wt[:], wv[:, :, jb * 512:(jb + 1) * 512], writes=[wk])
                    ps, pk = self.bank()
                    for j in range(4):
                        for k in range(8):
                            S.op("pe", lambda e, j=j, k=k: e.matmul(ps[:, j * 2:j * 2 + 2], lhsT=wt[:, k, j * 128:(j + 1) * 128],
                                                                   rhs=scv[:, k, :], start=(k == 0), stop=(k == 7)),
                                 reads=[wk, "scv"], writes=[pk])
                    S.op("dve", lambda e, jb=jb, l=l, ps=ps: e.tensor_tensor(
                        out=self.modT[:, l, jb * 4:(jb + 1) * 4, :], in0=ps[:, 0:8].rearrange("p (j v) -> p j v", v=2),
                        in1=self.bmod[:, l, jb * 4:(jb + 1) * 4].unsqueeze(2).to_broadcast([128, 4, 2]), op=ALU.add),
                        reads=[pk, "bmod"], writes=["modT"])
            for l in range(L):
                for i, (gi, sci) in enumerate(((0, 1), (1, 2), (2, 4), (3, 5))):
                    gsrc = self.gv[gi][:, l, :].unsqueeze(2).to_broadcast([128, 8, 2])
                    mod = self.modT[:, l, sci * 8:(sci + 1) * 8, :]
                    if i % 2 == 0:
                        S.op("dve", lambda e, mod=mod, gsrc=gsrc, i=i, l=l: e.scalar_tensor_tensor(
                            out=self.gm[i][:, l, :, :], in0=mod, scalar=1.0, in1=gsrc, op0=ALU.add, op1=ALU.mult),
                            reads=["modT", "gv%d" % gi], writes=["gm%d" % i])
                    else:
                        S.op("dve", lambda e, mod=mod, gsrc=gsrc, i=i, l=l: e.tensor_tensor(
                            out=self.gm[i][:, l, :, :], in0=mod, in1=gsrc, op=ALU.mult),
                            reads=["modT", "gv%d" % gi], writes=["gm%d" % i])
            S.finish()

    def phase_x(self):
        nc, S = self.nc, self.S
        with ExitStack() as ph:
            xi = Rot(nc, ph, "xi", (128, 1024), F32, 2)
            xo = Rot(nc, ph, "xo", (128, 8, 128), F32, 2)
            for g in "ps":
                xv = self.xT[g].rearrange("c p t -> p c t")
                for sub in range(self.N[g] // 128):
                    a, ak = xi.next()
                    S.dma("sp", a[:], self.xin[g][sub * 128:(sub + 1) * 128, :], writes=[ak])
                    o, ok = xo.next()
                    for half in range(2):
                        ps, pk = self.bank()
                        for j in range(4):
                            c = half * 4 + j
                            S.op("pe", lambda e, ps=ps, j=j, c=c, a=a: e.transpose(ps[:, j * 128:(j + 1) * 128], a[:, c * 128:(c + 1) * 128], self.ident[:]),
                                 reads=[ak, "ident"], writes=[pk])
                        self.evac(o[:, half * 4:(half + 1) * 4, :], ps[:, :].rearrange("p (j t) -> p j t", t=128), [pk], [ok + "h%d" % half])
                    S.dma("sp", xv[:, :, sub * 128:(sub + 1) * 128], o[:], reads=[ok + "h0", ok + "h1"],
                          writes=[("xT", g, sub // 4)])
            S.finish()

    def norm_mod(self, src, srck, dst, dstk, sq, rstd, tmp, l, gmi, shi, v):
        S = self.S
        S.op("act", lambda e: e.activation(out=sq[:], in_=src[:], func=AF.Square), reads=[srck], writes=["sq"])
        ps, pk = self.bank()
        for c in range(8):
            S.op("pe", lambda e, c=c: e.matmul(ps[:, :], lhsT=self.ones_b[:], rhs=sq[:, c, :], start=(c == 0), stop=(c == 7)),
                 reads=["sq", "ones_b"], writes=[pk])
        S.op("act", lambda e: e.activation(out=rstd[:], in_=ps[:, :], func=AF.Sqrt, bias=self.epsb[:, 0:1], scale=1.0 / D), reads=[pk, "epsb"], writes=["rstd0", "rstd"])
        S.op("dve", lambda e: e.reciprocal(out=rstd[:], in_=rstd[:]), reads=["rstd0"], writes=["rstd0", "rstd"])
        for c in range(8):
            t, tk = tmp.next()
            S.op("dve", lambda e, c=c, t=t: e.scalar_tensor_tensor(out=t[:], in0=src[:, c, :], scalar=self.gm[gmi][:, l, c, v:v + 1],
                                                               in1=rstd[:], op0=ALU.mult, op1=ALU.mult),
                 reads=[srck, "rstd", "gm%d" % gmi], writes=[tk])
            S.op("act", lambda e, c=c, t=t: e.activation(out=dst[:, c, :], in_=t[:], func=AF.Identity,
                                                         bias=self.modT[:, l, shi * 8 + c, v:v + 1], scale=1.0),
                 reads=[tk, "modT"], writes=[dstk])

    def post_norm_res(self, yraw, xt, sq, rstd, tmp, l, ggi, v, yk="yraw", xk="xt"):
        S = self.S
        S.op("act", lambda e: e.activation(out=sq[:], in_=yraw[:], func=AF.Square), reads=[yk], writes=["sq"])
        ps, pk = self.bank()
        for c in range(8):
            S.op("pe", lambda e, c=c: e.matmul(ps[:, :], lhsT=self.ones_b[:], rhs=sq[:, c, :], start=(c == 0), stop=(c == 7)),
                 reads=["sq", "ones_b"], writes=[pk])
        S.op("act", lambda e: e.activation(out=rstd[:], in_=ps[:, :], func=AF.Sqrt, bias=self.epsb[:, 0:1], scale=1.0 / D), reads=[pk, "epsb"], writes=["rstd0", "rstd"])
        S.op("dve", lambda e: e.reciprocal(out=rstd[:], in_=rstd[:]), reads=["rstd0"], writes=["rstd0", "rstd"])
        for c in range(8):
            t, tk = tmp.next()
            S.op("dve", lambda e, c=c, t=t: e.scalar_tensor_tensor(out=t[:], in0=yraw[:, c, :], scalar=self.gm[ggi][:, l, c, v:v + 1],
                                                               in1=rstd[:], op0=ALU.mult, op1=ALU.mult),
                 reads=[yk, "rstd", "gm%d" % ggi], writes=[tk])
            S.op("dve", lambda e, c=c, t=t: e.tensor_tensor(out=xt[:, c, :], in0=xt[:, c, :], in1=t[:], op=ALU.add),
                 reads=[tk, xk], writes=[xk])

    def dense_common(self, ph):
        nc = self.nc
        self.xt = self.sb(ph, "xt", (128, 8, TT))
        self.hT = self.sb(ph, "hT", (128, 8, TT), BF16)
        self.sq = self.sb(ph, "sq", (128, 8, TT), BF16)
        self.rstd = self.sb(ph, "rstd", (128, TT))
        self.tmp = Rot(nc, ph, "tmp", (128, TT), F32, 3)
        self.wr = Rot(nc, ph, "wr", (128, 4096), BF16, 4)
        self.stg = Rot(nc, ph, "stg", (128, TT), F32, 3)
        self.epsb = self.sb(ph, "epsb", (128, 1))
        self.S.op("dve", lambda e: e.memset(self.epsb[:], 1e-6), writes=["epsb"])

    def phase_a(self, l, groups):
        nc, S = self.nc, self.S

        def plan_for(g):
            return [
                (0, 512, [(256, 128, "naq", 0), (384, 128, "naq", 128)], [(0, 256, "vpool", 0)]),
                (512, 512, [(0, 128, "nak", 0), (128, 128, "nak", 128)], [(256, 256, "nav", 0)] + ([(0, 256, "ck", 0)] if g == "p" else [])),
                (1024, 512, [(0, 128, "rq", 0), (128, 128, "rq", 128), (256, 128, "rk", 0), (384, 128, "rk", 128)], []),
                (1536, 512, [], [(0, 512, "rvg", 0)]),
                (2048, 512, [(0, 128, "aq", 0), (128, 128, "ak", 0)], [(256, 256, "av", 0)]),
                (2560, 272, [(256, 16, "alr", 0)], [(0, 256, "ag", 0)]),
            ]
        wv = self.w_in[l].rearrange("(c p) f -> p c f", p=128)
        tiles = [(g, t) for g in groups for t in range(self.N[g] // TT)]
        with ExitStack() as ph:
            self.dense_common(ph)
            xts = [self.xt, self.sb(ph, "xtb", (128, 8, TT))]
            xks = ["xt", "xtb"]
            hTs = [self.hT, self.sb(ph, "hTb", (128, 8, TT), BF16)]
            hks = ["hT", "hTb"]

            def prep(i):
                g, t = tiles[i]
                v = 0 if g == "p" else 1
                t0 = t * TT
                xv = self.xT[g].rearrange("c p t -> p c t")
                xt, xk, hT, hk = xts[i % 2], xks[i % 2], hTs[i % 2], hks[i % 2]
                S.dma("sp", xt[:], xv[:, :, t0:t0 + TT], reads=[("xT", g, t)], writes=[xk])
                self.norm_mod(xt, xk, hT, hk, self.sq, self.rstd, self.tmp, l, 0, 0, v)
                return dict(g=g, t=t, t0=t0, hT=hT, hk=hk)

            def proj(c, blocks):
                g, t, t0, hT, hk = c["g"], c["t"], c["t0"], c["hT"], c["hk"]
                for (c0, ncol, fms, tms) in blocks:
                    wt, wk = self.wr.next()
                    w3 = wt[:, 0:8 * ncol].rearrange("p (c f) -> p c f", f=ncol)
                    S.dma("pool", w3, wv[:, :, c0:c0 + ncol], writes=[wk])
                    for (off, m, nm, r0) in fms:
                        ps, pk = self.bank()
                        for k in range(8):
                            S.op("pe", lambda e, k=k, off=off, m=m, ps=ps, w3=w3: e.matmul(ps[0:m, :], lhsT=w3[:, k, off:off + m], rhs=hT[:, k, :],
                                                                                         start=(k == 0), stop=(k == 7)),
                                 reads=[wk, hk], writes=[pk])
                        sg, sk = self.stg.next()
                        self.evac(sg[0:m, :], ps[0:m, :], [pk], [sk])
                        S.dma("sp", self.fm[(nm, g)][r0:r0 + m, t0:t0 + TT], sg[0:m, :], reads=[sk], writes=[("fm", nm, g, t)])
                    for (off, n, nm, cc0) in tms:
                        for sub in range(TT // 128):
                            ps, pk = self.bank()
                            for k in range(8):
                                S.op("pe", lambda e, k=k, off=off, n=n, ps=ps, w3=w3, sub=sub: e.matmul(
                                    ps[:, 0:n], lhsT=hT[:, k, sub * 128:(sub + 1) * 128], rhs=w3[:, k, off:off + n],
                                    start=(k == 0), stop=(k == 7)), reads=[wk, hk], writes=[pk])
                            sg, sk = self.stg.next()
                            self.evac(sg[:, 0:n], ps[:, 0:n], [pk], [sk])
                            tok0 = t0 + sub * 128
                            if nm == "ck":
                                seq, tt0 = tok0 // 256, tok0 % 256
                                S.dma("sp", self.o_ck[seq, l, tt0:tt0 + 128, :], sg[:, 0:n], reads=[sk])
                            else:
                                S.dma("sp", self.tm[(nm, g)][tok0:tok0 + 128, cc0:cc0 + n], sg[:, 0:n], reads=[sk],
                                      writes=[("tm", nm, g, tok0 // 128)])
                                if nm == "nav" and g == "p":
                                    seq, tt0 = tok0 // 256, tok0 % 256
                                    S.dma("sp", self.o_cv[seq, l, tt0:tt0 + 128, :], sg[:, 0:n], reads=[sk])

            ctx = prep(0)
            for i in range(len(tiles)):
                plan = plan_for(ctx["g"])
                proj(ctx, plan[0:3])
                nxt = prep(i + 1) if i + 1 < len(tiles) else None
                proj(ctx, plan[3:6])
                ctx = nxt
            S.finish()

    def phase_b(self, l, groups, last):
        nc, S = self.nc, self.S
        wo = self.w_out[l].rearrange("(c p) f -> p c f", p=128)
        wg = self.w_gate[l].rearrange("(c p) f -> p c f", p=128)
        wu = self.w_up[l].rearrange("(c p) f -> p c f", p=128)
        wd = self.w_down[l].rearrange("(c p) f -> p c f", p=128)
        tiles = [(g, t) for g in groups for t in range(self.N[g] // TT)]
        ntile = len(tiles)
        with ExitStack() as ph:
            self.dense_common(ph)
            xts = [self.xt, self.sb(ph, "xtb", (128, 8, TT))]
            xks = ["xt", "xtb"]
            yr1 = self.sb(ph, "yraw1", (128, 8, TT))
            yr2 = self.sb(ph, "yraw2", (128, 8, TT))
            hid = self.sb(ph, "hid", (128, 22, TT), BF16)
            oin = Rot(nc, ph, "oin", (128, 1024), F32, 4)
            yo = Rot(nc, ph, "yo", (128, 1024), F32, 2)
            sgt = Rot(nc, ph, "sgt", (128, TT), F32, 2)

            def load(i):
                g, t = tiles[i]
                t0 = t * TT
                xv = self.xT[g].rearrange("c p t -> p c t")
                xt, xk = xts[i % 2], xks[i % 2]
                S.dma("sp", xt[:], xv[:, :, t0:t0 + TT], reads=[("xT", g, t)], writes=[xk])
                os_ = []
                for sub in range(TT // 128):
                    a, ak = oin.next()
                    tok0 = t0 + sub * 128
                    S.dma("sp", a[:], self.tm[("o", g)][tok0:tok0 + 128, :], reads=[("tm", "o", g, tok0 // 128, q) for q in range(4)], writes=[ak])
                    os_.append((a, ak))
                return dict(g=g, v=(0 if g == "p" else 1), xv=xv, t=t, t0=t0, xt=xt, xk=xk, os=os_)

            def s1(c):
                for sub, (a, ak) in enumerate(c["os"]):
                    for half in range(2):
                        ps, pk = self.bank()
                        for j in range(4):
                            cc = half * 4 + j
                            S.op("pe", lambda e, ps=ps, j=j, cc=cc, a=a: e.transpose(ps[:, j * 128:(j + 1) * 128], a[:, cc * 128:(cc + 1) * 128], self.ident[:]),
                                 reads=[ak, "ident"], writes=[pk])
                        self.evac(self.hT[:, half * 4:(half + 1) * 4, sub * 128:(sub + 1) * 128],
                                  ps[:, :].rearrange("p (j t) -> p j t", t=128), [pk], ["hT"])
                for ob in range(2):
                    wt, wk = self.wr.next()
                    w3 = wt[:, :].rearrange("p (c f) -> p c f", f=512)
                    S.dma("pool", w3, wo[:, :, ob * 512:(ob + 1) * 512], writes=[wk])
                    for j in range(4):
                        ps, pk = self.bank()
                        for k in range(8):
                            S.op("pe", lambda e, k=k, j=j, ps=ps, w3=w3: e.matmul(ps[:, :], lhsT=w3[:, k, j * 128:(j + 1) * 128], rhs=self.hT[:, k, :],
                                                                              start=(k == 0), stop=(k == 7)), reads=[wk, "hT"], writes=[pk])
                        self.evac(yr1[:, ob * 4 + j, :], ps[:, :], [pk], ["yraw1"])

            def s2(c):
                self.post_norm_res(yr1, c["xt"], self.sq, self.rstd, self.tmp, l, 1, c["v"], yk="yraw1", xk=c["xk"])
                self.norm_mod(c["xt"], c["xk"], self.hT, "hT", self.sq, self.rstd, self.tmp, l, 2, 3, c["v"])

            def ffn_gu(c):
                for fb in range(6):
                    ncol = 512 if fb < 5 else 256
                    c0 = fb * 512
                    wtg, wkg = self.wr.next()
                    g3 = wtg[:, 0:8 * ncol].rearrange("p (c f) -> p c f", f=ncol)
                    S.dma("pool", g3, wg[:, :, c0:c0 + ncol], writes=[wkg])
                    wtu, wku = self.wr.next()
                    u3 = wtu[:, 0:8 * ncol].rearrange("p (c f) -> p c f", f=ncol)
                    S.dma("pool", u3, wu[:, :, c0:c0 + ncol], writes=[wku])
                    for j in range(ncol // 128):
                        psg, pkg = self.bank()
                        for k in range(8):
                            S.op("pe", lambda e, k=k, j=j, psg=psg, g3=g3: e.matmul(psg[:, :], lhsT=g3[:, k, j * 128:(j + 1) * 128], rhs=self.hT[:, k, :],
                                                                                 start=(k == 0), stop=(k == 7)), reads=[wkg, "hT"], writes=[pkg])
                        psu, pku = self.bank()
                        for k in range(8):
                            S.op("pe", lambda e, k=k, j=j, psu=psu, u3=u3: e.matmul(psu[:, :], lhsT=u3[:, k, j * 128:(j + 1) * 128], rhs=self.hT[:, k, :],
                                                                                 start=(k == 0), stop=(k == 7)), reads=[wku, "hT"], writes=[pku])
                        sg, sk = sgt.next()
                        S.op("act", lambda e, sg=sg, psg=psg: e.activation(out=sg[:], in_=psg[:, :], func=AF.Silu), reads=[pkg], writes=[sk])
                        S.op("dve", lambda e, sg=sg, psu=psu, idx=fb * 4 + j: e.tensor_tensor(out=hid[:, idx, :], in0=psu[:, :], in1=sg[:], op=ALU.mult),
                             reads=[pku, sk], writes=["hid"])

            def ffn_down(c):
                for ob in range(8):
                    wt, wk = self.wr.next()
                    d3 = wt[:, 0:22 * 128].rearrange("p (c f) -> p c f", f=128)
                    S.dma("pool", d3, wd[:, :, ob * 128:(ob + 1) * 128], writes=[wk])
                    ps, pk = self.bank()
                    for k in range(22):
                        S.op("pe", lambda e, k=k, ps=ps, d3=d3: e.matmul(ps[:, :], lhsT=d3[:, k, :], rhs=hid[:, k, :], start=(k == 0), stop=(k == 21)),
                             reads=[wk, "hid"], writes=[pk])
                    self.evac(yr2[:, ob, :], ps[:, :], [pk], ["yraw2"])

            def s4(c):
                xt, xk, t0, g, xv = c["xt"], c["xk"], c["t0"], c["g"], c["xv"]
                self.post_norm_res(yr2, xt, self.sq, self.rstd, self.tmp, l, 3, c["v"], yk="yraw2", xk=xk)
                if not last:
                    S.dma("sp", xv[:, :, t0:t0 + TT], xt[:], reads=[xk], writes=[("xT", g, c["t"])])
                else:
                    for sub in range(TT // 128):
                        a, ak = yo.next()
                        for half in range(2):
                            ps, pk = self.bank()
                            for j in range(4):
                                cc = half * 4 + j
                                S.op("pe", lambda e, ps=ps, j=j, cc=cc, sub=sub: e.transpose(ps[:, j * 128:(j + 1) * 128], xt[:, cc, sub * 128:(sub + 1) * 128], self.ident[:]),
                                     reads=[xk, "ident"], writes=[pk])
                            self.evac(a[:, half * 512:(half + 1) * 512], ps[:, :], [pk], [ak])
                        tok0 = t0 + sub * 128
                        S.dma("sp", self.yout[g][tok0:tok0 + 128, :], a[:], reads=[ak])

            ctx = load(0)
            s1(ctx)
            for t in range(ntile):
                s2(ctx)
                ffn_gu(ctx)
                nxt = load(t + 1) if t + 1 < ntile else None
                ffn_down(ctx)
                if nxt is not None:
                    s1(nxt)
                s4(ctx)
                ctx = nxt
            S.finish()

    def mixers(self, l, g):
        self.mix_pool(l, g)
        self.mix_attn(l, g)
        self.mix_lin_pair(l, g)

    def mix_pool(self, l, g):
        nc, S = self.nc, self.S
        N = self.N[g]
        with ExitStack() as ph:
            pm = self.sb(ph, "pm", (128, 4, 5, 128), F32R)
            wp = self.sb(ph, "wp", (64, 4, 64), F32R)
            psc = self.sb(ph, "psc", (128, 256))
            S.dma("pool", pm[:], self.c_poolm[:, :, :, :], writes=["pm"])
            S.dma("pool", wp[:], self.pool_w[l].rearrange("g c d -> c g d"), writes=["wp"])
            S.dma("sp", psc[:], self.pool_scale[l].partition_broadcast(128), writes=["psc"])
            vr = Rot(nc, ph, "pv", (128, 2, 256), F32R, 2)
            dtr = Rot(nc, ph, "pdt", (64, 4, 128), F32R, 2)
            opr = Rot(nc, ph, "pop", (128, 256), F32, 2)
            for pair in range(N // 256):
                tok0 = pair * 256
                vt, vk = vr.next()
                S.dma("pool", vt[:], self.tm[("vpool", g)][tok0:tok0 + 256, :].rearrange("(c p) f -> p c f", p=128),
                      reads=[("tm", "vpool", g, pair * 2), ("tm", "vpool", g, pair * 2 + 1)], writes=[vk])
                for a in range(2):
                    ps, pk = self.bank()
                    for gi in range(4):
                        if g == "s":
                            contrib = [(a, 0)]
                        else:
                            contrib = [(0, 1 + a * 2 + 0), (1, 1 + a * 2 + 1)]
                        for ci, (b, kind) in enumerate(contrib):
                            S.op("pe", lambda e, gi=gi, b=b, kind=kind, ci=ci, ps=ps, vt=vt, n=len(contrib): e.matmul(
                                ps[0:64, gi * 128:(gi + 1) * 128], lhsT=vt[:, b, gi * 64:(gi + 1) * 64], rhs=pm[:, gi, kind, :],
                                start=(ci == 0), stop=(ci == n - 1)), reads=[vk, "pm"], writes=[pk])
                    dt, dk_ = dtr.next()
                    self.evac(dt[:, :, :], ps[0:64, :].rearrange("p (g t) -> p g t", t=128), [pk], [dk_])
                    ps2, pk2 = self.bank()
                    for gi in range(4):
                        S.op("pe", lambda e, gi=gi, ps2=ps2, dt=dt: e.matmul(ps2[:, gi * 64:(gi + 1) * 64], lhsT=dt[:, gi, :], rhs=wp[:, gi, :],
                                                                         start=True, stop=True), reads=[dk_, "wp"], writes=[pk2])
                    op_, ok = opr.next()
                    S.op("dve", lambda e, op_=op_, ps2=ps2: e.tensor_tensor(out=op_[:], in0=ps2[:, 0:256], in1=psc[:], op=ALU.mult),
                         reads=[pk2, "psc"], writes=[ok])
                    tk0 = tok0 + a * 128
                    S.dma("sp", self.tm[("o", g)][tk0:tk0 + 128, 0:256], op_[:], reads=[ok], writes=[("tm", "o", g, tk0 // 128, 0)])
            S.finish()

    def attn_A(self, u, bufs):
        S = self.S
        sbt, sbks, nk = u["sbt"], u["sbks"], u["nk"]
        sm, smk = bufs["sm"].next()
        prob, prk = bufs["prob"].next()
        S.op("dve", lambda e: e.tensor_reduce(out=sm[:, 0:1], in_=sbt[:, 0:nk], axis=AX.X, op=ALU.max), reads=sbks, writes=[smk + "a"])
        S.op("dve", lambda e: e.tensor_scalar(out=sm[:, 1:2], in0=sm[:, 0:1], scalar1=-1.0, scalar2=None, op0=ALU.mult), reads=[smk + "a"], writes=[smk + "b"])
        S.op("dve", lambda e: e.memset(sm[:, 2:3], 0.0), writes=[smk + "c"])
        S.op("act", lambda e: e.activation(out=prob[:, 0:nk], in_=sbt[:, 0:nk], func=AF.Exp, bias=sm[:, 1:2], scale=1.0, accum_out=sm[:, 2:3]),
             reads=sbks + [smk + "b"], writes=[prk, smk + "c"])
        S.op("dve", lambda e: e.reciprocal(out=sm[:, 3:4], in_=sm[:, 2:3]), reads=[smk + "c"], writes=[smk + "d"])
        u.update(sm=sm, smk=smk, prob=prob, prk=prk)

    def attn_B(self, u, bufs):
        S = self.S
        sm, smk, prob, prk = u["sm"], u["smk"], u["prob"], u["prk"]
        chunks, vk_list, ona, onak, h = u["chunks"], u["vk_list"], u["ona"], u["onak"], u["h"]
        pt, ptk = bufs["pt"].next()
        nch = len(chunks)
        for c4 in range(0, nch, 4):
            ps, pk = self.bank()
            grp = chunks[c4:c4 + 4]
            for j, (col0, kc, vap) in enumerate(grp):
                S.op("pe", lambda e, j=j, col0=col0, kc=kc, ps=ps: e.transpose(ps[0:kc, j * 128:(j + 1) * 128], prob[:, col0:col0 + kc], self.ident[:]),
                     reads=[prk, "ident"], writes=[pk])
            self.evac(pt[:, c4:c4 + len(grp), :], ps[:, 0:128 * len(grp)].rearrange("p (j t) -> p j t", t=128), [pk], [ptk + "g%d" % (c4 // 4)])
        ps, pk = self.bank()
        for ci, (col0, kc, vap) in enumerate(chunks):
            S.op("pe", lambda e, ci=ci, kc=kc, vap=vap, ps=ps: e.matmul(ps[:, 0:64], lhsT=pt[0:kc, ci, :], rhs=vap, start=(ci == 0), stop=(ci == nch - 1)),
                 reads=[ptk + "g%d" % (ci // 4)] + vk_list, writes=[pk])
        S.op("act", lambda e, ps=ps: e.activation(out=ona[:, h * 64:(h + 1) * 64], in_=ps[:, 0:64], func=AF.Copy, scale=sm[:, 3:4]),
             reads=[pk, smk + "d"], writes=[onak])
        if u.get("post") is not None:
            u["post"]()

    def mix_attn(self, l, g):
        nc, S = self.nc, self.S
        sc = 0.125
        with ExitStack() as ph:
            bufs = {"sm": Rot(nc, ph, "asm", (128, 4), F32, 4), "pt": Rot(nc, ph, "apt", (128, 7, 128), F32R, 2),
                    "prob": Rot(nc, ph, "aprob", (128, 896), F32, 3)}
            sbr = Rot(nc, ph, "asb", (128, 896), F32, 3)
            for t_, k_ in zip(sbr.t, sbr.k):
                S.op("dve", lambda e, t_=t_: e.memset(t_[:], NEG), writes=[k_ + "pad"])
            onr = Rot(nc, ph, "aon", (128, 256), F32, 3)
            units = []

            if g == "p":
                qr = Rot(nc, ph, "aq", (64, 4, 256), F32R, 2)
                kr = Rot(nc, ph, "ak", (64, 4, 256), F32R, 2)
                vr = Rot(nc, ph, "av", (128, 2, 256), F32R, 2)

                def make_seq(s):
                    T0 = s * 256
                    qt, qk = qr.next()
                    kt, kk = kr.next()
                    vt, vk = vr.next()
                    S.dma("pool", qt[:], self.fm[("naq", g)][:, T0:T0 + 256].rearrange("(h d) t -> d h t", d=64), reads=[("fm", "naq", g, T0 // TT)], writes=[qk])
                    S.dma("pool", kt[:], self.fm[("nak", g)][:, T0:T0 + 256].rearrange("(h d) t -> d h t", d=64), reads=[("fm", "nak", g, T0 // TT)], writes=[kk])
                    S.dma("pool", vt[:], self.tm[("nav", g)][T0:T0 + 256, :].rearrange("(c p) f -> p c f", p=128),
                          reads=[("tm", "nav", g, T0 // 128), ("tm", "nav", g, T0 // 128 + 1)], writes=[vk])
                    return dict(qt=qt, qk=qk, kt=kt, kk=kk, vt=vt, vk=vk, T0=T0)

                def unit_p(sq, qb, h, onab):
                    def f():
                        if "c" not in sq:
                            sq["c"] = make_seq(sq["s"])
                        c = sq["c"]
                        if "o" not in onab:
                            onab["o"] = onr.next()
                        ona, onak = onab["o"]
                        ps, pk = self.bank()
                        S.op("pe", lambda e: e.matmul(ps[:, 0:256], lhsT=c["qt"][:, h, qb * 128:(qb + 1) * 128], rhs=c["kt"][:, h, :], start=True, stop=True),
                             reads=[c["qk"], c["kk"]], writes=[pk])
                        sbt, sbk = sbr.next()
                        S.op("act", lambda e: e.activation(out=sbt[:, 0:256], in_=ps[:, 0:256], func=AF.Copy, scale=sc), reads=[pk], writes=[sbk])
                        chunks = [(0, 128, c["vt"][:, 0, h * 64:(h + 1) * 64]), (128, 128, c["vt"][:, 1, h * 64:(h + 1) * 64])]
                        u = dict(sbt=sbt, sbks=[sbk], nk=256, chunks=chunks, vk_list=[c["vk"]], ona=ona, onak=onak, h=h, post=None)
                        if h == 3:
                            tk0 = c["T0"] + qb * 128
                            u["post"] = lambda: S.dma("sp", self.tm[("o", g)][tk0:tk0 + 128, 256:512], ona[:], reads=[onak], writes=[("tm", "o", g, tk0 // 128, 1)])
                        return u
                    return f
                for s in range(4):
                    sq = {"s": s}
                    for qb in range(2):
                        onab = {}
                        for h in range(4):
                            units.append(unit_p(sq, qb, h, onab))
            else:
                ckt = self.sb(ph, "ckt", (128, 2, 256))
                ckT = self.sb(ph, "ckT", (64, 4, 256), F32R)
                cvt = self.sb(ph, "cvt2", (128, 2, 256), F32R)
                S.dma("sp", ckt[:], self.ck[l].rearrange("(c p) f -> p c f", p=128), writes=["ckt"])
                S.dma("pool", cvt[:], self.cv[l].rearrange("(c p) f -> p c f", p=128), writes=["cvt2"])
                for h in range(4):
                    ps, pk = self.bank()
                    for c in range(2):
                        S.op("pe", lambda e, h=h, c=c, ps=ps: e.transpose(ps[0:64, c * 128:(c + 1) * 128], ckt[:, c, h * 64:(h + 1) * 64], self.ident[:]),
                             reads=["ckt", "ident"], writes=[pk])
                    self.evac(ckT[:, h, :], ps[0:64, 0:256], [pk], ["ckT"])
                qr = Rot(nc, ph, "aq", (64, 4, 128), F32R, 3)
                kr = Rot(nc, ph, "ak", (64, 4, 576), F32R, 3)
                vr = Rot(nc, ph, "av", (128, 5, 256), F32R, 3)
                br = Rot(nc, ph, "ab", (128, 4, 576), F32, 3)

                def make_l2(l2):
                    base = _na_base(l2)
                    var = _na_var(l2)
                    T0 = l2 * 128
                    K0 = base * 64
                    qt, qk = qr.next()
                    kt, kk = kr.next()
                    vt, vk = vr.next()
                    bt, bk = br.next()
                    kreads = [("fm", "nak", g, t) for t in range(K0 // TT, min((K0 + 575) // TT, NSM // TT - 1) + 1)] + [("fm", "nak", "s", "pad")]
                    vreads = [("tm", "nav", g, t) for t in range(K0 // 128, min((K0 + 639) // 128, NSM // 128 - 1) + 1)] + [("tm", "nav", "s", "pad")]
                    S.dma("pool", qt[:], self.fm[("naq", g)][:, T0:T0 + 128].rearrange("(h d) t -> d h t", d=64), reads=[("fm", "naq", g, T0 // TT)], writes=[qk])
                    S.dma("pool", kt[:], self.fm[("nak", g)][:, K0:K0 + 576].rearrange("(h d) t -> d h t", d=64), reads=kreads, writes=[kk])
                    S.dma("pool", vt[:], self.tm[("nav", g)][K0:K0 + 640, :].rearrange("(c p) f -> p c f", p=128), reads=vreads, writes=[vk])
                    S.dma("sp", bt[:], self.nabias[l, var].rearrange("h q k -> q h k"), writes=[bk])
                    return dict(qt=qt, qk=qk, kt=kt, kk=kk, vt=vt, vk=vk, bt=bt, bk=bk, T0=T0)

                def unit_s(lq, h, onab):
                    def f():
                        if "c" not in lq:
                            lq["c"] = make_l2(lq["l2"])
                        c = lq["c"]
                        if "o" not in onab:
                            onab["o"] = onr.next()
                        ona, onak = onab["o"]
                        qt, kt, vt, bt = c["qt"], c["kt"], c["vt"], c["bt"]
                        qk, kk, vk, bk = c["qk"], c["kk"], c["vk"], c["bk"]
                        psA, pkA = self.bank()
                        psB, pkB = self.bank()
                        S.op("pe", lambda e: e.matmul(psA[:, 0:512], lhsT=qt[:, h, :], rhs=kt[:, h, 0:512], start=True, stop=True), reads=[qk, kk], writes=[pkA])
                        S.op("pe", lambda e: e.matmul(psB[:, 0:64], lhsT=qt[:, h, :], rhs=kt[:, h, 512:576], start=True, stop=True), reads=[qk, kk], writes=[pkB])
                        S.op("pe", lambda e: e.matmul(psB[:, 64:320], lhsT=qt[:, h, :], rhs=ckT[:, h, :], start=True, stop=True), reads=[qk, "ckT"], writes=[pkB])
                        sbt, sbk = sbr.next()
                        S.op("dve", lambda e: e.scalar_tensor_tensor(out=sbt[:, 0:512], in0=psA[:, 0:512], scalar=sc, in1=bt[:, h, 0:512], op0=ALU.mult, op1=ALU.add),
                             reads=[pkA, bk], writes=[sbk + "a"])
                        S.op("dve", lambda e: e.scalar_tensor_tensor(out=sbt[:, 512:576], in0=psB[:, 0:64], scalar=sc, in1=bt[:, h, 512:576], op0=ALU.mult, op1=ALU.add),
                             reads=[pkB, bk], writes=[sbk + "b"])
                        S.op("dve", lambda e: e.tensor_scalar(out=sbt[:, 640:896], in0=psB[:, 64:320], scalar1=sc, scalar2=None, op0=ALU.mult), reads=[pkB], writes=[sbk + "c"])
                        chunks = [(cc * 128, 128, vt[:, cc, h * 64:(h + 1) * 64]) for cc in range(5)]
                        chunks.append((640, 128, cvt[:, 0, h * 64:(h + 1) * 64]))
                        chunks.append((768, 128, cvt[:, 1, h * 64:(h + 1) * 64]))
                        u = dict(sbt=sbt, sbks=[sbk + "a", sbk + "b", sbk + "c", sbk + "pad"], nk=896, chunks=chunks, vk_list=[vk, "cvt2"],
                                 ona=ona, onak=onak, h=h, post=None)
                        if h == 3:
                            T0 = c["T0"]
                            u["post"] = lambda: S.dma("sp", self.tm[("o", g)][T0:T0 + 128, 256:512], ona[:], reads=[onak], writes=[("tm", "o", g, T0 // 128, 1)])
                        return u
                    return f
                for l2 in range(32):
                    lq = {"l2": l2}
                    onab = {}
                    for h in range(4):
                        units.append(unit_s(lq, h, onab))
            prev = None
            for mk in units:
                u = mk()
                self.attn_A(u, bufs)
                if prev is not None:
                    self.attn_B(prev, bufs)
                prev = u
            self.attn_B(prev, bufs)
            S.finish()

    def mix_lin_gen(self, l, g, kind, ph):
        nc, S = self.nc, self.S
        ret = kind == "ret"
        dk = 64 if ret else 32
        HD = 4 * dk
        qscale = dk ** -0.5
        N = self.N[g]
        seqlen = 256 if g == "p" else NSM
        nseq = N // seqlen
        nblk = seqlen // 128
        qn, kn = ("rq", "rk") if ret else ("aq", "ak")
        rope = ret and g == "s"
        X = kind + "_"
        if True:
            if ret:
                lg = self.sb(ph, X + "lg", (128, 8))
                gconst = self.sb(ph, X + "gconst", (128, 2, 256))
                S.dma("sp", lg[:], self.ret_logit[l].partition_broadcast(128), writes=[X + "lg"])
                S.op("act", lambda e: e.activation(out=lg[:], in_=lg[:], func=AF.Exp, scale=-1.0), reads=[X + "lg"], writes=[X + "lg"])
                S.op("act", lambda e: e.activation(out=lg[:], in_=lg[:], func=AF.Ln, bias=self.ones_f[:, 0:1], scale=1.0), reads=[X + "lg", "ones_f"], writes=[X + "lg"])
                S.op("dve", lambda e: e.tensor_scalar(out=gconst[:].rearrange("p a (h d) -> p (a h) d", d=64), in0=lg[:].unsqueeze(2).to_broadcast([128, 8, 64]),
                                                      scalar1=-1.0, scalar2=None, op0=ALU.mult), reads=[X + "lg"], writes=[X + "gconst"])
            else:
                gup = self.sb(ph, X + "gup", (17, 2, 128), F32R)
                S.dma("pool", gup[0:16, :, :], self.gate_up[l].rearrange("a r f -> r a f"), writes=[X + "gup"])
                S.dma("pool", gup[16:17, :, :], self.gate_b[l:l + 1, :, :], writes=[X + "gup"])
                ng = self.sb(ph, X + "ng", (128, 64))
                S.dma("sp", ng[:], self.norm_g[l].partition_broadcast(128), writes=[X + "ng"])
                lrr = Rot(nc, ph, X + "lrr", (17, 128), F32R, 3)
                for t_, k_ in zip(lrr.t, lrr.k):
                    S.op("dve", lambda e, t_=t_: e.tensor_copy(out=t_[:], in_=self.ones_f[0:17, :]), reads=["ones_f"], writes=[k_])
                gtr = Rot(nc, ph, X + "gtr", (128, 128), F32, 3)
            if rope:
                cosT = self.sb(ph, X + "cosT", (64, NSM))
                sinT = self.sb(ph, X + "sinT", (64, NSM))
                S.dma("sp", cosT[:], self.c_cos[:, :], writes=[X + "cosT"])
                S.dma("sp", sinT[:], self.c_sin[:, :], writes=[X + "sinT"])
                rt = Rot(nc, ph, X + "rt", (64, 4, 128), F32, 6)
                qkr = Rot(nc, ph, X + "qkr", (64, 4, 128), F32, 4)
            epsg = self.sb(ph, X + "epsg", (128, 1))
            S.op("dve", lambda e: e.memset(epsg[:], 1e-5 if ret else 1e-6), writes=[X + "epsg"])
            sts = [self.sb(ph, "state%d" % d, (dk, 4, 64), F32R) for d in range(2)]
            qr = Rot(nc, ph, X + "lq", (dk, 4, 128), F32, 3)
            kr = Rot(nc, ph, X + "lk", (dk, 4, 128), F32, 3)
            vr = Rot(nc, ph, X + "lv", (128, 256), F32R, 4)
            gr = Rot(nc, ph, X + "lgt", (128, 256), F32, 2)
            ofr = Rot(nc, ph, X + "lof", (128, 256), F32, 3)
            epr = Rot(nc, ph, X + "lep", (dk, 512), F32, 4)
            enr = Rot(nc, ph, X + "len", (dk, 512), F32, 3)
            qgr = Rot(nc, ph, X + "lqg", (dk, 4, 128), F32R, 4)
            kgr = Rot(nc, ph, X + "lkg", (dk, 4, 128), F32R, 3)
            ktr = Rot(nc, ph, X + "lkt", (128, HD), F32R, 4)
            amr = Rot(nc, ph, X + "lam", (128, 512), F32R, 3)
            tsr = Rot(nc, ph, X + "lts", (dk, 256), F32, 2)
            osr = Rot(nc, ph, X + "los", (128, 256), F32, 2)
            o2r = Rot(nc, ph, X + "lo2", (128, 256), F32, 2)
            sqr = Rot(nc, ph, X + "lsq", (128, 256), F32, 2)
            str_ = Rot(nc, ph, X + "lst", (128, 16), F32, 2)
            sgr = Rot(nc, ph, X + "lsg", (128, 256), F32, 2)
            col0 = 512 if ret else 768

            def stage_a(s, d, n):
                U = self.uf if d == 0 else self.ub
                Uk = "uf" if d == 0 else "ub"
                if ret:
                    Ug, Ugk = U, Uk
                else:
                    Ug = self.ufg if d == 0 else self.ubg
                    Ugk = "ufg" if d == 0 else "ubg"
                T0 = s * seqlen + n * 128
                qt, qk = qr.next()
                kt, kk = kr.next()
                vt, vk = vr.next()
                S.dma("sp", qt[:], self.fm[(qn, g)][:, T0:T0 + 128].rearrange("(h d) t -> d h t", d=dk), reads=[("fm", qn, g, T0 // TT)], writes=[qk])
                S.dma("sp", kt[:], self.fm[(kn, g)][:, T0:T0 + 128].rearrange("(h d) t -> d h t", d=dk), reads=[("fm", kn, g, T0 // TT)], writes=[kk])
                vsrc = self.tm[("rvg", g)][T0:T0 + 128, 0:256] if ret else self.tm[("av", g)][T0:T0 + 128, :]
                S.dma("pool", vt[:], vsrc, reads=[("tm", "rvg" if ret else "av", g, T0 // 128)], writes=[vk])
                qf, kf, qfk, kfk = qt[:], kt[:], qk, kk
                if rope:
                    outs = []
                    pos0 = n * 128
                    cb = cosT[:, pos0:pos0 + 128].unsqueeze(1).to_broadcast([64, 4, 128])
                    sbb = sinT[:, pos0:pos0 + 128].unsqueeze(1).to_broadcast([64, 4, 128])
                    for (src_t, src_k, nm_) in ((qt, qk, qn), (kt, kk, kn)):
                        rr, rrk = rt.next()
                        srcv = self.fm[(nm_, g)][:, T0:T0 + 128].rearrange("(h b two s) t -> two b s h t", h=4, b=2, two=2, s=16)
                        for bb_ in range(2):
                            for tw in range(2):
                                p0 = bb_ * 32 + tw * 16
                                S.dma("sp", rr[p0:p0 + 16, :, :], srcv[1 - tw, bb_], reads=[("fm", nm_, g, T0 // TT)], writes=[rrk])
                        a1, a1k = rt.next()
                        S.op("pool", lambda e, a1=a1, src_t=src_t: e.tensor_tensor(out=a1[:], in0=src_t[:], in1=cb, op=ALU.mult), reads=[src_k, X + "cosT"], writes=[a1k])
                        S.op("dve", lambda e, rr=rr: e.tensor_tensor(out=rr[:], in0=rr[:], in1=sbb, op=ALU.mult), reads=[rrk, X + "sinT"], writes=[rrk])
                        o_, ok_ = qkr.next()
                        S.op("dve", lambda e, o_=o_, a1=a1, rr=rr: e.tensor_tensor(out=o_[:], in0=a1[:], in1=rr[:], op=ALU.add), reads=[a1k, rrk], writes=[ok_])
                        outs.append((o_, ok_))
                    qf, qfk = outs[0][0][:], outs[0][1]
                    kf, kfk = outs[1][0][:], outs[1][1]
                if ret:
                    gate_ap, gate_k = gconst[:, d, :], X + "gconst"
                else:
                    lrt, lrk = lrr.next()
                    S.dma("pool", lrt[0:16, :], self.fm[("alr", g)][:, T0:T0 + 128], reads=[("fm", "alr", g, T0 // TT)], writes=[lrk])
                    ps, pk = self.bank()
                    S.op("pe", lambda e, ps=ps, lrt=lrt: e.matmul(ps[:, 0:128], lhsT=lrt[:, :], rhs=gup[:, d, :], start=True, stop=True), reads=[lrk, X + "gup"], writes=[pk])
                    gt_, gtk = gtr.next()
                    S.op("act", lambda e, ps=ps, gt_=gt_: e.activation(out=gt_[:], in_=ps[:, 0:128], func=AF.Exp, scale=-1.0), reads=[pk], writes=[gtk])
                    S.op("act", lambda e, gt_=gt_: e.activation(out=gt_[:], in_=gt_[:], func=AF.Ln, bias=self.ones_f[:, 0:1], scale=1.0), reads=[gtk, "ones_f"], writes=[gtk])
                    gate_ap, gate_k = gt_[:], gtk
                bps, bpk = self.bank()
                for h in range(4):
                    S.op("pe", lambda e, h=h, bps=bps, gate_ap=gate_ap: e.matmul(bps[0:dk, h * 128:(h + 1) * 128], lhsT=gate_ap[:, h * dk:(h + 1) * dk], rhs=Ug[:], start=True, stop=True),
                         reads=[gate_k, Ugk], writes=[bpk])
                ep, epk = epr.next()
                en, enk = enr.next()
                S.op("act", lambda e, ep=ep, bps=bps: e.activation(out=ep[:], in_=bps[0:dk, :], func=AF.Exp), reads=[bpk], writes=[epk])
                S.op("act", lambda e, en=en, bps=bps: e.activation(out=en[:], in_=bps[0:dk, :], func=AF.Exp, scale=-1.0), reads=[bpk], writes=[enk])
                qg, qgk = qgr.next()
                kg, kgk = kgr.next()
                S.op("dve", lambda e, qg=qg, ep=ep, qf=qf: e.scalar_tensor_tensor(out=qg[:].rearrange("p h t -> p (h t)"), in0=qf.rearrange("p h t -> p (h t)"), scalar=qscale, in1=ep[:], op0=ALU.mult, op1=ALU.mult),
                     reads=[qfk, epk], writes=[qgk])
                S.op("dve", lambda e, kg=kg, en=en, kf=kf: e.tensor_tensor(out=kg[:].rearrange("p h t -> p (h t)"), in0=kf.rearrange("p h t -> p (h t)"), in1=en[:], op=ALU.mult),
                     reads=[kfk, enk], writes=[kgk])
                tps, tpk = self.bank()
                for h in range(4):
                    S.op("pe", lambda e, h=h, tps=tps, kg=kg: e.transpose(tps[:, h * dk:(h + 1) * dk], kg[:, h, :].bitcast(F32), self.ident[0:dk, 0:dk]),
                         reads=[kgk, "ident"], writes=[tpk])
                ktok, ktk = ktr.next()
                self.evac(ktok[:], tps[:, 0:HD], [tpk], [ktk])
                aps, apk = self.bank()
                for h in range(4):
                    S.op("pe", lambda e, h=h, aps=aps, kg=kg, qg=qg: e.matmul(aps[:, h * 128:(h + 1) * 128], lhsT=kg[:, h, :], rhs=qg[:, h, :], start=True, stop=True), reads=[kgk, qgk], writes=[apk])
                am, amk = amr.next()
                S.op("dve", lambda e, am=am, aps=aps: e.tensor_tensor(out=am[:].rearrange("p (h t) -> p h t", t=128), in0=aps[:, :].rearrange("p (h t) -> p h t", t=128),
                                                                      in1=U[:].unsqueeze(1).to_broadcast([128, 4, 128]), op=ALU.mult), reads=[apk, Uk], writes=[amk])
                ams = [(am, amk)] * 4
                return dict(s=s, d=d, n=n, T0=T0, vt=vt, vk=vk, ep=ep, epk=epk, qg=qg, qgk=qgk, ktok=ktok, ktk=ktk, ams=ams)

            def stage_b(c):
                s, d, n, T0 = c["s"], c["d"], c["n"], c["T0"]
                vt, vk, ep, epk, qg, qgk, ktok, ktk, ams = c["vt"], c["vk"], c["ep"], c["epk"], c["qg"], c["qgk"], c["ktok"], c["ktk"], c["ams"]
                st_ = sts[d]
                stk = X + "state%d" % d
                last = 127 if d == 0 else 0
                first_blk = (n == 0) if d == 0 else (n == nblk - 1)
                last_blk = (n == nblk - 1) if d == 0 else (n == 0)
                if first_blk:
                    if g == "p":
                        S.op("dve", lambda e: e.tensor_copy(out=st_[:], in_=self.zeros_f[0:dk, :].rearrange("p (h v) -> p h v", v=64)), reads=["zeros_f"], writes=[stk])
                    else:
                        src = (self.sret if ret else self.sgla)[l, d].rearrange("h d v -> d h v")
                        S.dma("pool", st_[:], src, writes=[stk])
                ops_, opk = self.bank()
                for h in range(4):
                    am, amk = ams[h]
                    S.op("pe", lambda e, h=h, am=am: e.matmul(ops_[:, h * 64:(h + 1) * 64], lhsT=am[:, h * 128:(h + 1) * 128], rhs=vt[:, h * 64:(h + 1) * 64], start=True, stop=False), reads=[amk, vk], writes=[opk])
                    S.op("pe", lambda e, h=h: e.matmul(ops_[:, h * 64:(h + 1) * 64], lhsT=qg[:, h, :], rhs=st_[:, h, :], start=False, stop=True), reads=[qgk, stk], writes=[opk])
                sps, spk = self.bank()
                for h in range(4):
                    S.op("pe", lambda e, h=h: e.matmul(sps[0:dk, h * 64:(h + 1) * 64], lhsT=ktok[:, h * dk:(h + 1) * dk], rhs=vt[:, h * 64:(h + 1) * 64], start=True, stop=True),
                         reads=[ktk, vk], writes=[spk])
                ts, tsk = tsr.next()
                S.op("dve", lambda e: e.tensor_tensor(out=ts[:], in0=sps[0:dk, 0:256], in1=st_[:].bitcast(F32).rearrange("p h v -> p (h v)"), op=ALU.add), reads=[spk, stk], writes=[tsk])
                eb = ep[:].rearrange("p (h t) -> p h t", t=128)[:, :, last:last + 1].to_broadcast([dk, 4, 64])
                S.op("dve", lambda e: e.tensor_tensor(out=st_[:], in0=ts[:].rearrange("p (h v) -> p h v", v=64), in1=eb, op=ALU.mult), reads=[tsk, epk], writes=[stk])
                if d == 0:
                    of, ofk = ofr.next()
                    self.evac(of[:], ops_[:, 0:256], [opk], [ofk])
                    S.dma("sp", self.tm[("of" + kind, g)][T0:T0 + 128, :], of[:], reads=[ofk], writes=[("tm", "of" + kind, g, T0 // 128)])
                else:
                    of, ofk = ofr.next()
                    S.dma("sp", of[:], self.tm[("of" + kind, g)][T0:T0 + 128, :], reads=[("tm", "of" + kind, g, T0 // 128)], writes=[ofk])
                    gt2, g2k = gr.next()
                    gsrc = self.tm[("rvg", g)][T0:T0 + 128, 256:512] if ret else self.tm[("ag", g)][T0:T0 + 128, :]
                    S.dma("sp", gt2[:], gsrc, reads=[("tm", "rvg" if ret else "ag", g, T0 // 128)], writes=[g2k])
                    osum, osk = osr.next()
                    S.op("dve", lambda e: e.tensor_tensor(out=osum[:], in0=ops_[:, 0:256], in1=of[:], op=ALU.add), reads=[opk, ofk], writes=[osk])
                    o3 = osum[:].rearrange("p (h v) -> p h v", v=64)
                    sq_, sqk = sqr.next()
                    stt, stk2 = str_.next()
                    S.op("act", lambda e: e.activation(out=sq_[:], in_=osum[:], func=AF.Square), reads=[osk], writes=[sqk])
                    S.op("dve", lambda e: e.tensor_reduce(out=stt[:, 4:8], in_=sq_[:].rearrange("p (h v) -> p h v", v=64), axis=AX.X, op=ALU.add), reads=[sqk], writes=[stk2 + "b"])
                    o2, o2k = o2r.next()
                    if ret:
                        S.op("dve", lambda e: e.tensor_reduce(out=stt[:, 0:4], in_=o3, axis=AX.X, op=ALU.add), reads=[osk], writes=[stk2 + "a"])
                        S.op("dve", lambda e: e.tensor_scalar(out=stt[:, 0:4], in0=stt[:, 0:4], scalar1=1.0 / 64, scalar2=None, op0=ALU.mult), reads=[stk2 + "a"], writes=[stk2 + "a"])
                        S.op("dve", lambda e: e.tensor_tensor(out=stt[:, 8:12], in0=stt[:, 0:4], in1=stt[:, 0:4], op=ALU.mult), reads=[stk2 + "a"], writes=[stk2 + "c"])
                        S.op("dve", lambda e: e.scalar_tensor_tensor(out=stt[:, 12:16], in0=stt[:, 4:8], scalar=1.0 / 64, in1=stt[:, 8:12], op0=ALU.mult, op1=ALU.subtract), reads=[stk2 + "b", stk2 + "c"], writes=[stk2 + "d"])
                    else:
                        S.op("dve", lambda e: e.tensor_scalar(out=stt[:, 12:16], in0=stt[:, 4:8], scalar1=1.0 / 64, scalar2=None, op0=ALU.mult), reads=[stk2 + "b"], writes=[stk2 + "d"])
                    S.op("act", lambda e: e.activation(out=stt[:, 12:16], in_=stt[:, 12:16], func=AF.Sqrt, bias=epsg[:, 0:1], scale=1.0), reads=[stk2 + "d", X + "epsg"], writes=[stk2 + "d"])
                    S.op("dve", lambda e: e.reciprocal(out=stt[:, 12:16], in_=stt[:, 12:16]), reads=[stk2 + "d"], writes=[stk2 + "d"])
                    rb = stt[:, 12:16].unsqueeze(2).to_broadcast([128, 4, 64])
                    o23 = o2[:].rearrange("p (h v) -> p h v", v=64)
                    if ret:
                        mb = stt[:, 0:4].unsqueeze(2).to_broadcast([128, 4, 64])
                        S.op("dve", lambda e: e.tensor_tensor(out=o23, in0=o3, in1=mb, op=ALU.subtract), reads=[osk, stk2 + "a"], writes=[o2k])
                        S.op("dve", lambda e: e.tensor_tensor(out=o23, in0=o23, in1=rb, op=ALU.mult), reads=[o2k, stk2 + "d"], writes=[o2k])
                    else:
                        S.op("dve", lambda e: e.tensor_tensor(out=o23, in0=o3, in1=rb, op=ALU.mult), reads=[osk, stk2 + "d"], writes=[o2k])
                        nb = ng[:].unsqueeze(1).to_broadcast([128, 4, 64])
                        S.op("dve", lambda e: e.tensor_tensor(out=o23, in0=o23, in1=nb, op=ALU.mult), reads=[o2k, X + "ng"], writes=[o2k])
                    sg_, sgk = sgr.next()
                    S.op("act", lambda e: e.activation(out=sg_[:], in_=gt2[:], func=AF.Silu), reads=[g2k], writes=[sgk])
                    S.op("pool", lambda e: e.tensor_tensor(out=o2[:], in0=o2[:], in1=sg_[:], op=ALU.mult), reads=[o2k, sgk], writes=[o2k])
                    S.dma("sp", self.tm[("o", g)][T0:T0 + 128, col0:col0 + 256], o2[:], reads=[o2k], writes=[("tm", "o", g, T0 // 128, 2 if ret else 3)])
                if last_blk and g == "p":
                    dst = (self.o_sret if ret else self.o_sgla)[s, l, d].rearrange("h d v -> d h v")
                    S.dma("sp", dst, st_[:].bitcast(F32), reads=[stk])

            seqn = []
            for s in range(nseq):
                for d in range(2):
                    order = range(nblk) if d == 0 else range(nblk - 1, -1, -1)
                    for n in order:
                        seqn.append((s, d, n))
            prev = None
            for (s, d, n) in seqn:
                c = stage_a(s, d, n)
                if prev is not None:
                    stage_b(prev)
                prev = c
                yield
            stage_b(prev)
            yield


    def mix_lin_pair(self, l, g):
        with ExitStack() as ph:
            gens = [self.mix_lin_gen(l, g, "ret", ph), self.mix_lin_gen(l, g, "gla", ph)]
            while gens:
                for gen in list(gens):
                    try:
                        next(gen)
                    except StopIteration:
                        gens.remove(gen)
            self.S.finish()


def _alloc_zeros(b):
    pass


_CACHE = {}


def _build(depth=DEPTH):
    if depth not in _CACHE:
        _CACHE[depth] = Builder(depth)
    return _CACHE[depth]


def kernel(x_prompt, x_sample, cache_na_k, cache_na_v, state_ret, state_gla, c, c_ctx,
           w_mod, b_mod, g_pre_mix, g_post_mix, g_pre_ffn, g_post_ffn, w_in, w_out,
           pool_w, pool_scale, na_rpb, ret_decay_logit, gla_gate_up, gla_gate_b, gla_norm_g,
           w_ffn_gate, w_ffn_up, w_ffn_down, _depth=DEPTH):
    f = lambda a: np.ascontiguousarray(np.asarray(a, dtype=np.float32))
    L = _depth
    b = _build(L)

    def fm_(a, nch):
        return np.ascontiguousarray(a.reshape(a.shape[0], nch, 128).transpose(2, 0, 1))
    cos, sin, perm = _rope_consts()
    idx = np.arange(128)
    shared = {
        "w_mod": f(w_mod)[:L], "b_mod": fm_(f(b_mod)[:L], 48),
        "g_pre_mix": fm_(f(g_pre_mix)[:L], 8), "g_post_mix": fm_(f(g_post_mix)[:L], 8), "g_pre_ffn": fm_(f(g_pre_ffn)[:L], 8), "g_post_ffn": fm_(f(g_post_ffn)[:L], 8),
        "w_in": f(w_in)[:L], "w_out": f(w_out)[:L], "pool_w": f(pool_w)[:L], "pool_scale": f(pool_scale)[:L],
        "ret_decay_logit": f(ret_decay_logit)[:L].reshape(L, 8), "gla_gate_up": f(gla_gate_up)[:L], "gla_gate_b": f(gla_gate_b)[:L],
        "gla_norm_g": f(gla_norm_g)[:L], "w_ffn_gate": f(w_ffn_gate)[:L], "w_ffn_up": f(w_ffn_up)[:L], "w_ffn_down": f(w_ffn_down)[:L],
        "nabias": _na_bias_tables(f(na_rpb)[:L]),
        "c_ident": np.eye(128, dtype=np.float32),
        "c_uf": (idx[:, None] <= idx[None, :]).astype(np.float32),
        "c_ub": (idx[:, None] >= idx[None, :]).astype(np.float32),
        "c_ufg": (idx[:, None] <= idx[None, :]).astype(np.float32) * np.float32(-1.0 / 16.0),
        "c_ubg": (idx[:, None] >= idx[None, :]).astype(np.float32) * np.float32(-1.0 / 16.0),
        "c_poolm": _pool_consts(), "c_cos": cos, "c_sin": sin, "c_perm": perm,
        "c_zero": np.zeros((128, 256), np.float32),
    }
    xp = f(x_prompt)
    xs = f(x_sample)
    cc = f(c)
    cctx = f(c_ctx)
    in_maps = []
    for core in range(8):
        bb = core % 2
        m = dict(shared)
        m["xp"] = xp[core * 4:(core + 1) * 4].reshape(NPR, D)
        m["xs"] = xs[bb]
        m["cvec"] = np.ascontiguousarray(np.stack([cctx, cc[bb]], axis=0).reshape(2, 8, 128).transpose(2, 1, 0))
        m["ck"] = f(cache_na_k)[bb, :L].reshape(L, 256, 256)
        m["cv"] = f(cache_na_v)[bb, :L].reshape(L, 256, 256)
        m["sret"] = f(state_ret)[bb, :L]
        m["sgla"] = f(state_gla)[bb, :L]
        in_maps.append(m)
    res = run_bass_kernel_spmd(b.nc, in_maps, core_ids=list(range(8)))
    R = res.results
    y_prompt = np.concatenate([R[i]["yp"].reshape(4, 256, D) for i in range(8)], axis=0)
    y_sample = np.stack([R[0]["ys"], R[1]["ys"]], axis=0)
    nk = np.concatenate([R[i]["o_ck"].reshape(4, L, 256, 4, 64) for i in range(8)], axis=0)
    nv = np.concatenate([R[i]["o_cv"].reshape(4, L, 256, 4, 64) for i in range(8)], axis=0)
    sr = np.concatenate([R[i]["o_sret"] for i in range(8)], axis=0)
    sg = np.concatenate([R[i]["o_sgla"] for i in range(8)], axis=0)
    return (y_prompt.astype(np.float32), y_sample.astype(np.float32), nk.astype(np.float32), nv.astype(np.float32),
            sr.astype(np.float32), sg.astype(np.float32))
```

```python
import numpy as np
from contextlib import ExitStack
import concourse.bass as bass
import concourse.mybir as mybir
from concourse.bass_utils import run_bass_kernel_spmd

F32 = mybir.dt.float32
F32R = mybir.dt.float32r
BF16 = mybir.dt.bfloat16
AF = mybir.ActivationFunctionType
ALU = mybir.AluOpType
AX = mybir.AxisListType

D = 1024
DEPTH = 4
NPR = 1024
NSM = 4096
NSP = NSM + 128
TT = 512
P_IN = 2832
DFF = 2816
COMPUTE = ("pe", "dve", "act", "pool")
NEG = -30000.0


class Sched:
    def __init__(self, nc, stack, n_dma=60):
        self.nc = nc
        self.eng = {"pe": nc.tensor, "dve": nc.vector, "act": nc.scalar,
                    "pool": nc.gpsimd, "sp": nc.sync}
        self.sem = {e: stack.enter_context(nc.semaphore("s_" + e)) for e in COMPUTE}
        self.cnt = {e: 0 for e in COMPUTE}
        self.dsem = [stack.enter_context(nc.semaphore("d%d" % i)) for i in range(n_dma)]
        self.dval = [0] * n_dma
        self.dq = [0, 0, 0]
        self.ndma = n_dma
        self.seen = {e: {} for e in self.eng}
        self.last_w = {}
        self.readers = {}

    def _wait(self, e, tok):
        kind, key, val = tok
        if kind == "c" and key == e and e == "pe":
            return
        k = (kind, key)
        if self.seen[e].get(k, 0) >= val:
            return
        sem = self.sem[key] if kind == "c" else self.dsem[key]
        self.eng[e].wait_ge(sem, val)
        self.seen[e][k] = val

    def _deps(self, reads, writes):
        deps = {}
        for r in reads:
            t = self.last_w.get(r)
            if t is not None and deps.get((t[0], t[1]), 0) < t[2]:
                deps[(t[0], t[1])] = t[2]
        for w in writes:
            t = self.last_w.get(w)
            if t is not None and deps.get((t[0], t[1]), 0) < t[2]:
                deps[(t[0], t[1])] = t[2]
            for k, v in self.readers.get(w, {}).items():
                if deps.get(k, 0) < v:
                    deps[k] = v
        return deps

    def _commit(self, tok, reads, writes):
        for w in writes:
            self.last_w[w] = tok
            self.readers[w] = {}
        k = (tok[0], tok[1])
        for r in reads:
            d = self.readers.setdefault(r, {})
            if d.get(k, 0) < tok[2]:
                d[k] = tok[2]

    def op(self, e, fn, reads=(), writes=()):
        for k, v in self._deps(reads, writes).items():
            self._wait(e, (k[0], k[1], v))
        ins = fn(self.eng[e])
        self.cnt[e] += 1
        ins.then_inc(self.sem[e], 1)
        self._commit(("c", e, self.cnt[e]), reads, writes)

    def dma(self, q, out, in_, reads=(), writes=()):
        half = self.ndma // 3
        qi = {"sp": 0, "pool": 1, "act": 2}[q]
        slot = qi * half + self.dq[qi]
        self.dq[qi] = (self.dq[qi] + 1) % half
        if self.dval[slot] > 0:
            self._wait(q, ("d", slot, self.dval[slot]))
        for k, v in self._deps(reads, writes).items():
            self._wait(q, (k[0], k[1], v))
        self.dval[slot] += 16
        self.eng[q].dma_start(out=out, in_=in_).then_inc(self.dsem[slot], 16)
        self._commit(("d", slot, self.dval[slot]), reads, writes)

    def finish(self, e=None):
        for en in (list(self.eng) if e is None else [e]):
            for slot in range(self.ndma):
                if self.dval[slot] > 0:
                    self._wait(en, ("d", slot, self.dval[slot]))
            for c in COMPUTE:
                if self.cnt[c] > 0 and c != en:
                    self._wait(en, ("c", c, self.cnt[c]))


class Rot:
    uid = [0]

    def __init__(self, nc, stack, name, shape, dtype, n):
        Rot.uid[0] += 1
        self.t = [stack.enter_context(nc.sbuf_tensor("%s_%d_r%d" % (name, i, Rot.uid[0]), list(shape), dtype)) for i in range(n)]
        self.k = ["%s_%d" % (name, i) for i in range(n)]
        self.i = 0

    def next(self):
        i = self.i
        self.i = (i + 1) % len(self.t)
        return self.t[i], self.k[i]


def _pool_consts():
    wins = (2, 4, 8, 16)

    def mat(L):
        t = np.arange(L)
        M = np.zeros((4, L, L), np.float64)
        for gi, win in enumerate(wins):
            left = win // 2
            lo = np.clip(t - left, 0, L)
            hi = np.clip(t - left + win, 0, L)
            for tt in range(L):
                M[gi, lo[tt]:hi[tt], tt] = 1.0 / (hi[tt] - lo[tt])
                M[gi, tt, tt] -= 1.0
        return M
    m64 = mat(64)
    m256 = mat(256)
    out = np.zeros((128, 4, 5, 128), np.float32)
    for g in range(4):
        out[0:64, g, 0, 0:64] = m64[g]
        out[64:128, g, 0, 64:128] = m64[g]
        for a in range(2):
            for b in range(2):
                out[:, g, 1 + a * 2 + b, :] = m256[g, b * 128:(b + 1) * 128, a * 128:(a + 1) * 128]
    return out


def _rope_consts():
    nf = 16
    inv = (10000.0 ** (-np.arange(nf, dtype=np.float32) / nf)).astype(np.float32)
    t = np.arange(NSM)
    cos = np.zeros((64, NSM), np.float32)
    sin = np.zeros((64, NSM), np.float32)
    perm = np.zeros((64, 64), np.float32)
    for d in range(64):
        blk = d // 32
        dd = d % 32
        pos = (t // 64) if blk == 0 else (t % 64)
        ang = pos.astype(np.float32) * inv[dd % nf]
        cos[d] = np.cos(ang).astype(np.float32)
        if dd < nf:
            sin[d] = -np.sin(ang).astype(np.float32)
            partner = d + nf
        else:
            sin[d] = np.sin(ang).astype(np.float32)
            partner = d - nf
        perm[partner, d] = 1.0
    return cos, sin, perm


NA_VARIANTS = (0, 1, 2, 30, 31)


def _na_base(l2):
    return int(np.clip(2 * l2 - 4, 0, 56))


def _na_var(l2):
    if l2 <= 1:
        return l2
    if l2 >= 30:
        return l2 - 27
    return 2


def _na_bias_tables(rpb):
    L = rpb.shape[0]
    out = np.full((L, 5, 4, 128, 576), NEG, np.float32)
    cols = np.arange(64)
    c0 = np.clip(cols - 8, 0, 48)
    for vi, l2 in enumerate(NA_VARIANTS):
        base = _na_base(l2)
        for a in range(2):
            r = 2 * l2 + a
            r0 = int(np.clip(r - 4, 0, 56))
            for kr in range(r0, r0 + 8):
                kl = kr - base
                assert 0 <= kl < 9
                dr = kr - r + 7
                for c in range(64):
                    d0 = int(c0[c]) - c + 15
                    out[:, vi, :, a * 64 + c, kl * 64 + int(c0[c]):kl * 64 + int(c0[c]) + 16] = rpb[:, :, dr, d0:d0 + 16]
    return out


class Builder:
    def __init__(self, depth=DEPTH):
        self.depth = depth
        nc = bass.Bass("TRN2", target_bir_lowering=False)
        self.nc = nc
        L = depth

        def din(name, shape):
            return nc.dram_tensor(name, list(shape), F32, kind="ExternalInput").ap()

        def dout(name, shape):
            return nc.dram_tensor(name, list(shape), F32, kind="ExternalOutput").ap()

        def dscr(name, shape):
            return nc.dram_tensor(name, list(shape), F32, kind="Internal").ap()

        self.xin = {"p": din("xp", (NPR, D)), "s": din("xs", (NSM, D))}
        self.cvec = din("cvec", (128, 8, 2))
        self.w_mod = din("w_mod", (L, D, 6 * D))
        self.b_mod = din("b_mod", (128, L, 48))
        self.gvec = [din(n, (128, L, 8)) for n in ("g_pre_mix", "g_post_mix", "g_pre_ffn", "g_post_ffn")]
        self.w_in = din("w_in", (L, D, P_IN))
        self.w_out = din("w_out", (L, D, D))
        self.pool_w = din("pool_w", (L, 4, 64, 64))
        self.pool_scale = din("pool_scale", (L, 256))
        self.ret_logit = din("ret_decay_logit", (L, 8))
        self.gate_up = din("gla_gate_up", (L, 2, 16, 128))
        self.gate_b = din("gla_gate_b", (L, 2, 128))
        self.norm_g = din("gla_norm_g", (L, 64))
        self.w_gate = din("w_ffn_gate", (L, D, DFF))
        self.w_up = din("w_ffn_up", (L, D, DFF))
        self.w_down = din("w_ffn_down", (L, DFF, D))
        self.ck = din("ck", (L, 256, 256))
        self.cv = din("cv", (L, 256, 256))
        self.sret = din("sret", (L, 2, 4, 64, 64))
        self.sgla = din("sgla", (L, 2, 4, 32, 64))
        self.nabias = din("nabias", (L, 5, 4, 128, 576))
        self.c_ident = din("c_ident", (128, 128))
        self.c_uf = din("c_uf", (128, 128))
        self.c_ub = din("c_ub", (128, 128))
        self.c_ufg = din("c_ufg", (128, 128))
        self.c_ubg = din("c_ubg", (128, 128))
        self.c_poolm = din("c_poolm", (128, 4, 5, 128))
        self.c_cos = din("c_cos", (64, NSM))
        self.c_sin = din("c_sin", (64, NSM))
        self.c_perm = din("c_perm", (64, 64))
        self.c_zero = din("c_zero", (128, 256))

        self.yout = {"p": dout("yp", (NPR, D)), "s": dout("ys", (NSM, D))}
        self.o_ck = dout("o_ck", (4, L, 256, 256))
        self.o_cv = dout("o_cv", (4, L, 256, 256))
        self.o_sret = dout("o_sret", (4, L, 2, 4, 64, 64))
        self.o_sgla = dout("o_sgla", (4, L, 2, 4, 32, 64))

        self.N = {"p": NPR, "s": NSM}
        self.NP = {"p": NPR, "s": NSP}
        self.xT = {g: dscr("xT_" + g, (8, 128, self.N[g])) for g in "ps"}
        self.fm = {}
        self.tm = {}
        for g in "ps":
            n = self.NP[g]
            for nm, rows in (("naq", 256), ("nak", 256), ("rq", 256), ("rk", 256), ("aq", 128), ("ak", 128), ("alr", 16)):
                self.fm[(nm, g)] = dscr("fm_%s_%s" % (nm, g), (rows, n))
            for nm, cols in (("vpool", 256), ("nav", 256), ("rvg", 512), ("av", 256), ("ag", 256), ("ofret", 256), ("ofgla", 256), ("o", 1024)):
                self.tm[(nm, g)] = dscr("tm_%s_%s" % (nm, g), (n, cols))

        with ExitStack() as st:
            self.st = st
            self.S = Sched(nc, st)
            self.build()

    def sb(self, stack, name, shape, dtype=F32):
        Rot.uid[0] += 1
        return stack.enter_context(self.nc.sbuf_tensor("%s_u%d" % (name, Rot.uid[0]), list(shape), dtype))

    def bank(self):
        i = self.bi
        self.bi = (i + 1) % 8
        return self.banks[i], "bank%d" % i

    def evac(self, out, in_, reads, writes):
        self.ev = 1 - self.ev
        if self.ev:
            self.S.op("act", lambda e: e.activation(out=out, in_=in_, func=AF.Copy), reads, writes)
        else:
            self.S.op("dve", lambda e: e.tensor_copy(out=out, in_=in_), reads, writes)

    def build(self):
        nc, S, st = self.nc, self.S, self.st
        L = self.depth
        self.bi = 0
        self.ev = 0
        self.banks = [st.enter_context(nc.psum_tensor("bank%d" % i, [128, 512], F32)) for i in range(8)]

        self.ident = self.sb(st, "ident", (128, 128))
        self.uf = self.sb(st, "uf", (128, 128))
        self.ub = self.sb(st, "ub", (128, 128))
        self.ones_f = self.sb(st, "ones_f", (128, 128))
        self.ones_r = self.sb(st, "ones_r", (128, 128), F32R)
        S.dma("sp", self.ident[:], self.c_ident[:, :], writes=["ident"])
        S.dma("sp", self.uf[:], self.c_uf[:, :], writes=["uf"])
        S.dma("sp", self.ub[:], self.c_ub[:, :], writes=["ub"])
        self.ufg = self.sb(st, "ufg", (128, 128))
        self.ubg = self.sb(st, "ubg", (128, 128))
        S.dma("sp", self.ufg[:], self.c_ufg[:, :], writes=["ufg"])
        S.dma("sp", self.ubg[:], self.c_ubg[:, :], writes=["ubg"])
        S.op("dve", lambda e: e.memset(self.ones_f[:], 1.0), writes=["ones_f"])
        self.zeros_f = self.sb(st, "zeros_f", (128, 256))
        S.op("dve", lambda e: e.memset(self.zeros_f[:], 0.0), writes=["zeros_f"])
        S.op("dve", lambda e: e.tensor_copy(out=self.ones_r[:], in_=self.ones_f[:]), reads=["ones_f"], writes=["ones_r"])
        self.ones_b = self.sb(st, "ones_b", (128, 128), BF16)
        S.op("dve", lambda e: e.tensor_copy(out=self.ones_b[:], in_=self.ones_f[:]), reads=["ones_f"], writes=["ones_b"])

        self.gv = []
        for i, gsrc in enumerate(self.gvec):
            t = self.sb(st, "gv%d" % i, (128, L, 8))
            S.dma("sp", t[:], gsrc[:, :, :], writes=["gv%d" % i])
            self.gv.append(t)
        self.modT = self.sb(st, "modT", (128, L, 48, 2))
        self.bmod = self.sb(st, "bmod", (128, L, 48))
        S.dma("sp", self.bmod[:], self.b_mod[:, :, :], writes=["bmod"])
        self.gm = [self.sb(st, "gm%d" % i, (128, L, 8, 2)) for i in range(4)]

        with ExitStack() as ph:
            z = self.sb(ph, "ztile", (128, 512))
            S.op("dve", lambda e: e.memset(z[:], 0.0), writes=["ztile"])
            for nm in ("nak",):
                S.dma("sp", self.fm[(nm, "s")][0:128, NSM:NSP], z[:, 0:128], reads=["ztile"], writes=[("fm", nm, "s", "pad")])
                S.dma("sp", self.fm[(nm, "s")][128:256, NSM:NSP], z[:, 0:128], reads=["ztile"], writes=[("fm", nm, "s", "pad")])
            S.dma("sp", self.tm[("nav", "s")][NSM:NSP, :], z[:, 0:256], reads=["ztile"], writes=[("tm", "nav", "s", "pad")])
            S.finish()

        import os
        stop = os.environ.get("KSTOP", "")
        grps = os.environ.get("KGRPS", "ps")
        self.phase_mod()
        if stop != "mod":
            self.phase_x()
        if stop not in ("mod", "x"):
            for l in range(L):
                self.phase_a(l, grps)
                if stop == "a":
                    break
                for g in grps:
                    self.mix_pool(l, g)
                    if stop == "pool":
                        continue
                    self.mix_attn(l, g)
                    if stop == "attn":
                        continue
                    self.mix_lin_pair(l, g)
                if stop in ("pool", "attn", "ret", "gla"):
                    break
                self.phase_b(l, grps, last=(l == L - 1))
        S.finish()

    def phase_mod(self):
        nc, S = self.nc, self.S
        L = self.depth
        with ExitStack() as ph:
            cv = self.sb(ph, "cvt", (128, 8, 2))
            scv = self.sb(ph, "scv", (128, 8, 2), F32R)
            S.dma("sp", cv[:], self.cvec[:, :, :], writes=["cvt"])
            S.op("act", lambda e: e.activation(out=scv[:], in_=cv[:], func=AF.Silu), reads=["cvt"], writes=["scv"])
            wr = Rot(nc, ph, "wm", (128, 8, 512), F32R, 3)
            for l in range(L):
                wv = self.w_mod[l].rearrange("(c p) f -> p c f", p=128)
                for jb in range(12):
                    wt, wk = wr.next()
                    S.dma("pool", wt[:], wv[:, :, jb * 512:(jb + 1) * 512], writes=[wk])
                    ps, pk = self.bank()
                    for j in range(4):
                        for k in range(8):
                            S.op("pe", lambda e, j=j, k=k: e.matmul(ps[:, j * 2:j * 2 + 2], lhsT=wt[:, k, j * 128:(j + 1) * 128],
                                                                   rhs=scv[:, k, :], start=(k == 0), stop=(k == 7)),
                                 reads=[wk, "scv"], writes=[pk])
                    S.op("dve", lambda e, jb=jb, l=l, ps=ps: e.tensor_tensor(
                        out=self.modT[:, l, jb * 4:(jb + 1) * 4, :], in0=ps[:, 0:8].rearrange("p (j v) -> p j v", v=2),
                        in1=self.bmod[:, l, jb * 4:(jb + 1) * 4].unsqueeze(2).to_broadcast([128, 4, 2]), op=ALU.add),
                        reads=[pk, "bmod"], writes=["modT"])
            for l in range(L):
                for i, (gi, sci) in enumerate(((0, 1), (1, 2), (2, 4), (3, 5))):
                    gsrc = self.gv[gi][:, l, :].unsqueeze(2).to_broadcast([128, 8, 2])
                    mod = self.modT[:, l, sci * 8:(sci + 1) * 8, :]
                    if i % 2 == 0:
                        S.op("dve", lambda e, mod=mod, gsrc=gsrc, i=i, l=l: e.scalar_tensor_tensor(
                            out=self.gm[i][:, l, :, :], in0=mod, scalar=1.0, in1=gsrc, op0=ALU.add, op1=ALU.mult),
                            reads=["modT", "gv%d" % gi], writes=["gm%d" % i])
                    else:
                        S.op("dve", lambda e, mod=mod, gsrc=gsrc, i=i, l=l: e.tensor_tensor(
                            out=self.gm[i][:, l, :, :], in0=mod, in1=gsrc, op=ALU.mult),
                            reads=["modT", "gv%d" % gi], writes=["gm%d" % i])
            S.finish()

    def phase_x(self):
        nc, S = self.nc, self.S
        with ExitStack() as ph:
            xi = Rot(nc, ph, "xi", (128, 1024), F32, 2)
            xo = Rot(nc, ph, "xo", (128, 8, 128), F32, 2)
            for g in "ps":
                xv = self.xT[g].rearrange("c p t -> p c t")
                for sub in range(self.N[g] // 128):
                    a, ak = xi.next()
                    S.dma("sp", a[:], self.xin[g][sub * 128:(sub + 1) * 128, :], writes=[ak])
                    o, ok = xo.next()
                    for half in range(2):
                        ps, pk = self.bank()
                        for j in range(4):
                            c = half * 4 + j
                            S.op("pe", lambda e, ps=ps, j=j, c=c, a=a: e.transpose(ps[:, j * 128:(j + 1) * 128], a[:, c * 128:(c + 1) * 128], self.ident[:]),
                                 reads=[ak, "ident"], writes=[pk])
                        self.evac(o[:, half * 4:(half + 1) * 4, :], ps[:, :].rearrange("p (j t) -> p j t", t=128), [pk], [ok + "h%d" % half])
                    S.dma("sp", xv[:, :, sub * 128:(sub + 1) * 128], o[:], reads=[ok + "h0", ok + "h1"],
                          writes=[("xT", g, sub // 4)])
            S.finish()

    def norm_mod(self, src, srck, dst, dstk, sq, rstd, tmp, l, gmi, shi, v):
        S = self.S
        S.op("act", lambda e: e.activation(out=sq[:], in_=src[:], func=AF.Square), reads=[srck], writes=["sq"])
        ps, pk = self.bank()
        for c in range(8):
            S.op("pe", lambda e, c=c: e.matmul(ps[:, :], lhsT=self.ones_b[:], rhs=sq[:, c, :], start=(c == 0), stop=(c == 7)),
                 reads=["sq", "ones_b"], writes=[pk])
        S.op("act", lambda e: e.activation(out=rstd[:], in_=ps[:, :], func=AF.Sqrt, bias=self.epsb[:, 0:1], scale=1.0 / D), reads=[pk, "epsb"], writes=["rstd0", "rstd"])
        S.op("dve", lambda e: e.reciprocal(out=rstd[:], in_=rstd[:]), reads=["rstd0"], writes=["rstd0", "rstd"])
        for c in range(8):
            t, tk = tmp.next()
            S.op("dve", lambda e, c=c, t=t: e.scalar_tensor_tensor(out=t[:], in0=src[:, c, :], scalar=self.gm[gmi][:, l, c, v:v + 1],
                                                               in1=rstd[:], op0=ALU.mult, op1=ALU.mult),
                 reads=[srck, "rstd", "gm%d" % gmi], writes=[tk])
            S.op("act", lambda e, c=c, t=t: e.activation(out=dst[:, c, :], in_=t[:], func=AF.Identity,
                                                         bias=self.modT[:, l, shi * 8 + c, v:v + 1], scale=1.0),
                 reads=[tk, "modT"], writes=[dstk])

    def post_norm_res(self, yraw, xt, sq, rstd, tmp, l, ggi, v, yk="yraw", xk="xt"):
        S = self.S
        S.op("act", lambda e: e.activation(out=sq[:], in_=yraw[:], func=AF.Square), reads=[yk], writes=["sq"])
        ps, pk = self.bank()
        for c in range(8):
            S.op("pe", lambda e, c=c: e.matmul(ps[:, :], lhsT=self.ones_b[:], rhs=sq[:, c, :], start=(c == 0), stop=(c == 7)),
                 reads=["sq", "ones_b"], writes=[pk])
        S.op("act", lambda e: e.activation(out=rstd[:], in_=ps[:, :], func=AF.Sqrt, bias=self.epsb[:, 0:1], scale=1.0 / D), reads=[pk, "epsb"], writes=["rstd0", "rstd"])
        S.op("dve", lambda e: e.reciprocal(out=rstd[:], in_=rstd[:]), reads=["rstd0"], writes=["rstd0", "rstd"])
        for c in range(8):
            t, tk = tmp.next()
            S.op("dve", lambda e, c=c, t=t: e.scalar_tensor_tensor(out=t[:], in0=yraw[:, c, :], scalar=self.gm[ggi][:, l, c, v:v + 1],
                                                               in1=rstd[:], op0=ALU.mult, op1=ALU.mult),
                 reads=[yk, "rstd", "gm%d" % ggi], writes=[tk])
            S.op("dve", lambda e, c=c, t=t: e.tensor_tensor(out=xt[:, c, :], in0=xt[:, c, :], in1=t[:], op=ALU.add),
                 reads=[tk, xk], writes=[xk])

    def dense_common(self, ph):
        nc = self.nc
        self.xt = self.sb(ph, "xt", (128, 8, TT))
        self.hT = self.sb(ph, "hT", (128, 8, TT), BF16)
        self.sq = self.sb(ph, "sq", (128, 8, TT), BF16)
        self.rstd = self.sb(ph, "rstd", (128, TT))
        self.tmp = Rot(nc, ph, "tmp", (128, TT), F32, 3)
        self.wr = Rot(nc, ph, "wr", (128, 4096), BF16, 4)
        self.stg = Rot(nc, ph, "stg", (128, TT), F32, 3)
        self.epsb = self.sb(ph, "epsb", (128, 1))
        self.S.op("dve", lambda e: e.memset(self.epsb[:], 1e-6), writes=["epsb"])

    def phase_a(self, l, groups):
        nc, S = self.nc, self.S

        def plan_for(g):
            return [
                (0, 512, [(256, 128, "naq", 0), (384, 128, "naq", 128)], [(0, 256, "vpool", 0)]),
                (512, 512, [(0, 128, "nak", 0), (128, 128, "nak", 128)], [(256, 256, "nav", 0)] + ([(0, 256, "ck", 0)] if g == "p" else [])),
                (1024, 512, [(0, 128, "rq", 0), (128, 128, "rq", 128), (256, 128, "rk", 0), (384, 128, "rk", 128)], []),
                (1536, 512, [], [(0, 512, "rvg", 0)]),
                (2048, 512, [(0, 128, "aq", 0), (128, 128, "ak", 0)], [(256, 256, "av", 0)]),
                (2560, 272, [(256, 16, "alr", 0)], [(0, 256, "ag", 0)]),
            ]
        wv = self.w_in[l].rearrange("(c p) f -> p c f", p=128)
        tiles = [(g, t) for g in groups for t in range(self.N[g] // TT)]
        with ExitStack() as ph:
            self.dense_common(ph)
            xts = [self.xt, self.sb(ph, "xtb", (128, 8, TT))]
            xks = ["xt", "xtb"]
            hTs = [self.hT, self.sb(ph, "hTb", (128, 8, TT), BF16)]
            hks = ["hT", "hTb"]

            def prep(i):
                g, t = tiles[i]
                v = 0 if g == "p" else 1
                t0 = t * TT
                xv = self.xT[g].rearrange("c p t -> p c t")
                xt, xk, hT, hk = xts[i % 2], xks[i % 2], hTs[i % 2], hks[i % 2]
                S.dma("sp", xt[:], xv[:, :, t0:t0 + TT], reads=[("xT", g, t)], writes=[xk])
                self.norm_mod(xt, xk, hT, hk, self.sq, self.rstd, self.tmp, l, 0, 0, v)
                return dict(g=g, t=t, t0=t0, hT=hT, hk=hk)

            def proj(c, blocks):
                g, t, t0, hT, hk = c["g"], c["t"], c["t0"], c["hT"], c["hk"]
                for (c0, ncol, fms, tms) in blocks:
                    wt, wk = self.wr.next()
                    w3 = wt[:, 0:8 * ncol].rearrange("p (c f) -> p c f", f=ncol)
                    S.dma("pool", w3, wv[:, :, c0:c0 + ncol], writes=[wk])
                    for (off, m, nm, r0) in fms:
                        ps, pk = self.bank()
                        for k in range(8):
                            S.op("pe", lambda e, k=k, off=off, m=m, ps=ps, w3=w3: e.matmul(ps[0:m, :], lhsT=w3[:, k, off:off + m], rhs=hT[:, k, :],
                                                                                         start=(k == 0), stop=(k == 7)),
                                 reads=[wk, hk], writes=[pk])
                        sg, sk = self.stg.next()
                        self.evac(sg[0:m, :], ps[0:m, :], [pk], [sk])
                        S.dma("sp", self.fm[(nm, g)][r0:r0 + m, t0:t0 + TT], sg[0:m, :], reads=[sk], writes=[("fm", nm, g, t)])
                    for (off, n, nm, cc0) in tms:
                        for sub in range(TT // 128):
                            ps, pk = self.bank()
                            for k in range(8):
                                S.op("pe", lambda e, k=k, off=off, n=n, ps=ps, w3=w3, sub=sub: e.matmul(
                                    ps[:, 0:n], lhsT=hT[:, k, sub * 128:(sub + 1) * 128], rhs=w3[:, k, off:off + n],
                                    start=(k == 0), stop=(k == 7)), reads=[wk, hk], writes=[pk])
                            sg, sk = self.stg.next()
                            self.evac(sg[:, 0:n], ps[:, 0:n], [pk], [sk])
                            tok0 = t0 + sub * 128
                            if nm == "ck":
                                seq, tt0 = tok0 // 256, tok0 % 256
                                S.dma("sp", self.o_ck[seq, l, tt0:tt0 + 128, :], sg[:, 0:n], reads=[sk])
                            else:
                                S.dma("sp", self.tm[(nm, g)][tok0:tok0 + 128, cc0:cc0 + n], sg[:, 0:n], reads=[sk],
                                      writes=[("tm", nm, g, tok0 // 128)])
                                if nm == "nav" and g == "p":
                                    seq, tt0 = tok0 // 256, tok0 % 256
                                    S.dma("sp", self.o_cv[seq, l, tt0:tt0 + 128, :], sg[:, 0:n], reads=[sk])

            ctx = prep(0)
            for i in range(len(tiles)):
                plan = plan_for(ctx["g"])
                proj(ctx, plan[0:3])
                nxt = prep(i + 1) if i + 1 < len(tiles) else None
                proj(ctx, plan[3:6])
                ctx = nxt
            S.finish()

    def phase_b(self, l, groups, last):
        nc, S = self.nc, self.S
        wo = self.w_out[l].rearrange("(c p) f -> p c f", p=128)
        wg = self.w_gate[l].rearrange("(c p) f -> p c f", p=128)
        wu = self.w_up[l].rearrange("(c p) f -> p c f", p=128)
        wd = self.w_down[l].rearrange("(c p) f -> p c f", p=128)
        tiles = [(g, t) for g in groups for t in range(self.N[g] // TT)]
        ntile = len(tiles)
        with ExitStack() as ph:
            self.dense_common(ph)
            xts = [self.xt, self.sb(ph, "xtb", (128, 8, TT))]
            xks = ["xt", "xtb"]
            yr1 = self.sb(ph, "yraw1", (128, 8, TT))
            yr2 = self.sb(ph, "yraw2", (128, 8, TT))
            hid = self.sb(ph, "hid", (128, 22, TT), BF16)
            oin = Rot(nc, ph, "oin", (128, 1024), F32, 4)
            yo = Rot(nc, ph, "yo", (128, 1024), F32, 2)
            sgt = Rot(nc, ph, "sgt", (128, TT), F32, 2)

            def load(i):
                g, t = tiles[i]
                t0 = t * TT
                xv = self.xT[g].rearrange("c p t -> p c t")
                xt, xk = xts[i % 2], xks[i % 2]
                S.dma("sp", xt[:], xv[:, :, t0:t0 + TT], reads=[("xT", g, t)], writes=[xk])
                os_ = []
                for sub in range(TT // 128):
                    a, ak = oin.next()
                    tok0 = t0 + sub * 128
                    S.dma("sp", a[:], self.tm[("o", g)][tok0:tok0 + 128, :], reads=[("tm", "o", g, tok0 // 128, q) for q in range(4)], writes=[ak])
                    os_.append((a, ak))
                return dict(g=g, v=(0 if g == "p" else 1), xv=xv, t=t, t0=t0, xt=xt, xk=xk, os=os_)

            def s1(c):
                for sub, (a, ak) in enumerate(c["os"]):
                    for half in range(2):
                        ps, pk = self.bank()
                        for j in range(4):
                            cc = half * 4 + j
                            S.op("pe", lambda e, ps=ps, j=j, cc=cc, a=a: e.transpose(ps[:, j * 128:(j + 1) * 128], a[:, cc * 128:(cc + 1) * 128], self.ident[:]),
                                 reads=[ak, "ident"], writes=[pk])
                        self.evac(self.hT[:, half * 4:(half + 1) * 4, sub * 128:(sub + 1) * 128],
                                  ps[:, :].rearrange("p (j t) -> p j t", t=128), [pk], ["hT"])
                for ob in range(2):
                    wt, wk = self.wr.next()
                    w3 = wt[:, :].rearrange("p (c f) -> p c f", f=512)
                    S.dma("pool", w3, wo[:, :, ob * 512:(ob + 1) * 512], writes=[wk])
                    for j in range(4):
                        ps, pk = self.bank()
                        for k in range(8):
                            S.op("pe", lambda e, k=k, j=j, ps=ps, w3=w3: e.matmul(ps[:, :], lhsT=w3[:, k, j * 128:(j + 1) * 128], rhs=self.hT[:, k, :],
                                                                              start=(k == 0), stop=(k == 7)), reads=[wk, "hT"], writes=[pk])
                        self.evac(yr1[:, ob * 4 + j, :], ps[:, :], [pk], ["yraw1"])

            def s2(c):
                self.post_norm_res(yr1, c["xt"], self.sq, self.rstd, self.tmp, l, 1, c["v"], yk="yraw1", xk=c["xk"])
                self.norm_mod(c["xt"], c["xk"], self.hT, "hT", self.sq, self.rstd, self.tmp, l, 2, 3, c["v"])

            def ffn_gu(c):
                for fb in range(6):
                    ncol = 512 if fb < 5 else 256
                    c0 = fb * 512
                    wtg, wkg = self.wr.next()
                    g3 = wtg[:, 0:8 * ncol].rearrange("p (c f) -> p c f", f=ncol)
                    S.dma("pool", g3, wg[:, :, c0:c0 + ncol], writes=[wkg])
                    wtu, wku = self.wr.next()
                    u3 = wtu[:, 0:8 * ncol].rearrange("p (c f) -> p c f", f=ncol)
                    S.dma("pool", u3, wu[:, :, c0:c0 + ncol], writes=[wku])
                    for j in range(ncol // 128):
                        psg, pkg = self.bank()
                        for k in range(8):
                            S.op("pe", lambda e, k=k, j=j, psg=psg, g3=g3: e.matmul(psg[:, :], lhsT=g3[:, k, j * 128:(j + 1) * 128], rhs=self.hT[:, k, :],
                                                                                 start=(k == 0), stop=(k == 7)), reads=[wkg, "hT"], writes=[pkg])
                        psu, pku = self.bank()
                        for k in range(8):
                            S.op("pe", lambda e, k=k, j=j, psu=psu, u3=u3: e.matmul(psu[:, :], lhsT=u3[:, k, j * 128:(j + 1) * 128], rhs=self.hT[:, k, :],
                                                                                 start=(k == 0), stop=(k == 7)), reads=[wku, "hT"], writes=[pku])
                        sg, sk = sgt.next()
                        S.op("act", lambda e, sg=sg, psg=psg: e.activation(out=sg[:], in_=psg[:, :], func=AF.Silu), reads=[pkg], writes=[sk])
                        S.op("dve", lambda e, sg=sg, psu=psu, idx=fb * 4 + j: e.tensor_tensor(out=hid[:, idx, :], in0=psu[:, :], in1=sg[:], op=ALU.mult),
                             reads=[pku, sk], writes=["hid"])

            def ffn_down(c):
                for ob in range(8):
                    wt, wk = self.wr.next()
                    d3 = wt[:, 0:22 * 128].rearrange("p (c f) -> p c f", f=128)
                    S.dma("pool", d3, wd[:, :, ob * 128:(ob + 1) * 128], writes=[wk])
                    ps, pk = self.bank()
                    for k in range(22):
                        S.op("pe", lambda e, k=k, ps=ps, d3=d3: e.matmul(ps[:, :], lhsT=d3[:, k, :], rhs=hid[:, k, :], start=(k == 0), stop=(k == 21)),
                             reads=[wk, "hid"], writes=[pk])
                    self.evac(yr2[:, ob, :], ps[:, :], [pk], ["yraw2"])

            def s4(c):
                xt, xk, t0, g, xv = c["xt"], c["xk"], c["t0"], c["g"], c["xv"]
                self.post_norm_res(yr2, xt, self.sq, self.rstd, self.tmp, l, 3, c["v"], yk="yraw2", xk=xk)
                if not last:
                    S.dma("sp", xv[:, :, t0:t0 + TT], xt[:], reads=[xk], writes=[("xT", g, c["t"])])
                else:
                    for sub in range(TT // 128):
                        a, ak = yo.next()
                        for half in range(2):
                            ps, pk = self.bank()
                            for j in range(4):
                                cc = half * 4 + j
                                S.op("pe", lambda e, ps=ps, j=j, cc=cc, sub=sub: e.transpose(ps[:, j * 128:(j + 1) * 128], xt[:, cc, sub * 128:(sub + 1) * 128], self.ident[:]),
                                     reads=[xk, "ident"], writes=[pk])
                            self.evac(a[:, half * 512:(half + 1) * 512], ps[:, :], [pk], [ak])
                        tok0 = t0 + sub * 128
                        S.dma("sp", self.yout[g][tok0:tok0 + 128, :], a[:], reads=[ak])

            ctx = load(0)
            s1(ctx)
            for t in range(ntile):
                s2(ctx)
                ffn_gu(ctx)
                nxt = load(t + 1) if t + 1 < ntile else None
                ffn_down(ctx)
                if nxt is not None:
                    s1(nxt)
                s4(ctx)
                ctx = nxt
            S.finish()

    def mixers(self, l, g):
        self.mix_pool(l, g)
        self.mix_attn(l, g)
        self.mix_lin_pair(l, g)

    def mix_pool(self, l, g):
        nc, S = self.nc, self.S
        N = self.N[g]
        with ExitStack() as ph:
            pm = self.sb(ph, "pm", (128, 4, 5, 128), F32R)
            wp = self.sb(ph, "wp", (64, 4, 64), F32R)
            psc = self.sb(ph, "psc", (128, 256))
            S.dma("pool", pm[:], self.c_poolm[:, :, :, :], writes=["pm"])
            S.dma("pool", wp[:], self.pool_w[l].rearrange("g c d -> c g d"), writes=["wp"])
            S.dma("sp", psc[:], self.pool_scale[l].partition_broadcast(128), writes=["psc"])
            vr = Rot(nc, ph, "pv", (128, 2, 256), F32R, 2)
            dtr = Rot(nc, ph, "pdt", (64, 4, 128), F32R, 2)
            opr = Rot(nc, ph, "pop", (128, 256), F32, 2)
            for pair in range(N // 256):
                tok0 = pair * 256
                vt, vk = vr.next()
                S.dma("pool", vt[:], self.tm[("vpool", g)][tok0:tok0 + 256, :].rearrange("(c p) f -> p c f", p=128),
                      reads=[("tm", "vpool", g, pair * 2), ("tm", "vpool", g, pair * 2 + 1)], writes=[vk])
                for a in range(2):
                    ps, pk = self.bank()
                    for gi in range(4):
                        if g == "s":
                            contrib = [(a, 0)]
                        else:
                            contrib = [(0, 1 + a * 2 + 0), (1, 1 + a * 2 + 1)]
                        for ci, (b, kind) in enumerate(contrib):
                            S.op("pe", lambda e, gi=gi, b=b, kind=kind, ci=ci, ps=ps, vt=vt, n=len(contrib): e.matmul(
                                ps[0:64, gi * 128:(gi + 1) * 128], lhsT=vt[:, b, gi * 64:(gi + 1) * 64], rhs=pm[:, gi, kind, :],
                                start=(ci == 0), stop=(ci == n - 1)), reads=[vk, "pm"], writes=[pk])
                    dt, dk_ = dtr.next()
                    self.evac(dt[:, :, :], ps[0:64, :].rearrange("p (g t) -> p g t", t=128), [pk], [dk_])
                    ps2, pk2 = self.bank()
                    for gi in range(4):
                        S.op("pe", lambda e, gi=gi, ps2=ps2, dt=dt: e.matmul(ps2[:, gi * 64:(gi + 1) * 64], lhsT=dt[:, gi, :], rhs=wp[:, gi, :],
                                                                         start=True, stop=True), reads=[dk_, "wp"], writes=[pk2])
                    op_, ok = opr.next()
                    S.op("dve", lambda e, op_=op_, ps2=ps2: e.tensor_tensor(out=op_[:], in0=ps2[:, 0:256], in1=psc[:], op=ALU.mult),
                         reads=[pk2, "psc"], writes=[ok])
                    tk0 = tok0 + a * 128
                    S.dma("sp", self.tm[("o", g)][tk0:tk0 + 128, 0:256], op_[:], reads=[ok], writes=[("tm", "o", g, tk0 // 128, 0)])
            S.finish()

    def attn_A(self, u, bufs):
        S = self.S
        sbt, sbks, nk = u["sbt"], u["sbks"], u["nk"]
        sm, smk = bufs["sm"].next()
        prob, prk = bufs["prob"].next()
        S.op("dve", lambda e: e.tensor_reduce(out=sm[:, 0:1], in_=sbt[:, 0:nk], axis=AX.X, op=ALU.max), reads=sbks, writes=[smk + "a"])
        S.op("dve", lambda e: e.tensor_scalar(out=sm[:, 1:2], in0=sm[:, 0:1], scalar1=-1.0, scalar2=None, op0=ALU.mult), reads=[smk + "a"], writes=[smk + "b"])
        S.op("dve", lambda e: e.memset(sm[:, 2:3], 0.0), writes=[smk + "c"])
        S.op("act", lambda e: e.activation(out=prob[:, 0:nk], in_=sbt[:, 0:nk], func=AF.Exp, bias=sm[:, 1:2], scale=1.0, accum_out=sm[:, 2:3]),
             reads=sbks + [smk + "b"], writes=[prk, smk + "c"])
        S.op("dve", lambda e: e.reciprocal(out=sm[:, 3:4], in_=sm[:, 2:3]), reads=[smk + "c"], writes=[smk + "d"])
        u.update(sm=sm, smk=smk, prob=prob, prk=prk)

    def attn_B(self, u, bufs):
        S = self.S
        sm, smk, prob, prk = u["sm"], u["smk"], u["prob"], u["prk"]
        chunks, vk_list, ona, onak, h = u["chunks"], u["vk_list"], u["ona"], u["onak"], u["h"]
        pt, ptk = bufs["pt"].next()
        nch = len(chunks)
        for c4 in range(0, nch, 4):
            ps, pk = self.bank()
            grp = chunks[c4:c4 + 4]
            for j, (col0, kc, vap) in enumerate(grp):
                S.op("pe", lambda e, j=j, col0=col0, kc=kc, ps=ps: e.transpose(ps[0:kc, j * 128:(j + 1) * 128], prob[:, col0:col0 + kc], self.ident[:]),
                     reads=[prk, "ident"], writes=[pk])
            self.evac(pt[:, c4:c4 + len(grp), :], ps[:, 0:128 * len(grp)].rearrange("p (j t) -> p j t", t=128), [pk], [ptk + "g%d" % (c4 // 4)])
        ps, pk = self.bank()
        for ci, (col0, kc, vap) in enumerate(chunks):
            S.op("pe", lambda e, ci=ci, kc=kc, vap=vap, ps=ps: e.matmul(ps[:, 0:64], lhsT=pt[0:kc, ci, :], rhs=vap, start=(ci == 0), stop=(ci == nch - 1)),
                 reads=[ptk + "g%d" % (ci // 4)] + vk_list, writes=[pk])
        S.op("act", lambda e, ps=ps: e.activation(out=ona[:, h * 64:(h + 1) * 64], in_=ps[:, 0:64], func=AF.Copy, scale=sm[:, 3:4]),
             reads=[pk, smk + "d"], writes=[onak])
        if u.get("post") is not None:
            u["post"]()

    def mix_attn(self, l, g):
        nc, S = self.nc, self.S
        sc = 0.125
        with ExitStack() as ph:
            bufs = {"sm": Rot(nc, ph, "asm", (128, 4), F32, 4), "pt": Rot(nc, ph, "apt", (128, 7, 128), F32R, 2),
                    "prob": Rot(nc, ph, "aprob", (128, 896), F32, 3)}
            sbr = Rot(nc, ph, "asb", (128, 896), F32, 3)
            for t_, k_ in zip(sbr.t, sbr.k):
                S.op("dve", lambda e, t_=t_: e.memset(t_[:], NEG), writes=[k_ + "pad"])
            onr = Rot(nc, ph, "aon", (128, 256), F32, 3)
            units = []

            if g == "p":
                qr = Rot(nc, ph, "aq", (64, 4, 256), F32R, 2)
                kr = Rot(nc, ph, "ak", (64, 4, 256), F32R, 2)
                vr = Rot(nc, ph, "av", (128, 2, 256), F32R, 2)

                def make_seq(s):
                    T0 = s * 256
                    qt, qk = qr.next()
                    kt, kk = kr.next()
                    vt, vk = vr.next()
                    S.dma("pool", qt[:], self.fm[("naq", g)][:, T0:T0 + 256].rearrange("(h d) t -> d h t", d=64), reads=[("fm", "naq", g, T0 // TT)], writes=[qk])
                    S.dma("pool", kt[:], self.fm[("nak", g)][:, T0:T0 + 256].rearrange("(h d) t -> d h t", d=64), reads=[("fm", "nak", g, T0 // TT)], writes=[kk])
                    S.dma("pool", vt[:], self.tm[("nav", g)][T0:T0 + 256, :].rearrange("(c p) f -> p c f", p=128),
                          reads=[("tm", "nav", g, T0 // 128), ("tm", "nav", g, T0 // 128 + 1)], writes=[vk])
                    return dict(qt=qt, qk=qk, kt=kt, kk=kk, vt=vt, vk=vk, T0=T0)

                def unit_p(sq, qb, h, onab):
                    def f():
                        if "c" not in sq:
                            sq["c"] = make_seq(sq["s"])
                        c = sq["c"]
                        if "o" not in onab:
                            onab["o"] = onr.next()
                        ona, onak = onab["o"]
                        ps, pk = self.bank()
                        S.op("pe", lambda e: e.matmul(ps[:, 0:256], lhsT=c["qt"][:, h, qb * 128:(qb + 1) * 128], rhs=c["kt"][:, h, :], start=True, stop=True),
                             reads=[c["qk"], c["kk"]], writes=[pk])
                        sbt, sbk = sbr.next()
                        S.op("act", lambda e: e.activation(out=sbt[:, 0:256], in_=ps[:, 0:256], func=AF.Copy, scale=sc), reads=[pk], writes=[sbk])
                        chunks = [(0, 128, c["vt"][:, 0, h * 64:(h + 1) * 64]), (128, 128, c["vt"][:, 1, h * 64:(h + 1) * 64])]
                        u = dict(sbt=sbt, sbks=[sbk], nk=256, chunks=chunks, vk_list=[c["vk"]], ona=ona, onak=onak, h=h, post=None)
                        if h == 3:
                            tk0 = c["T0"] + qb * 128
                            u["post"] = lambda: S.dma("act", self.tm[("o", g)][tk0:tk0 + 128, 256:512], ona[:], reads=[onak], writes=[("tm", "o", g, tk0 // 128, 1)])
                        return u
                    return f
                for s in range(4):
                    sq = {"s": s}
                    for qb in range(2):
                        onab = {}
                        for h in range(4):
                            units.append(unit_p(sq, qb, h, onab))
            else:
                ckt = self.sb(ph, "ckt", (128, 2, 256))
                ckT = self.sb(ph, "ckT", (64, 4, 256), F32R)
                cvt = self.sb(ph, "cvt2", (128, 2, 256), F32R)
                S.dma("sp", ckt[:], self.ck[l].rearrange("(c p) f -> p c f", p=128), writes=["ckt"])
                S.dma("pool", cvt[:], self.cv[l].rearrange("(c p) f -> p c f", p=128), writes=["cvt2"])
                for h in range(4):
                    ps, pk = self.bank()
                    for c in range(2):
                        S.op("pe", lambda e, h=h, c=c, ps=ps: e.transpose(ps[0:64, c * 128:(c + 1) * 128], ckt[:, c, h * 64:(h + 1) * 64], self.ident[:]),
                             reads=["ckt", "ident"], writes=[pk])
                    self.evac(ckT[:, h, :], ps[0:64, 0:256], [pk], ["ckT"])
                qr = Rot(nc, ph, "aq", (64, 4, 128), F32R, 3)
                kr = Rot(nc, ph, "ak", (64, 4, 576), F32R, 3)
                vr = Rot(nc, ph, "av", (128, 5, 256), F32R, 3)
                br = Rot(nc, ph, "ab", (128, 4, 576), F32, 3)

                def make_l2(l2):
                    base = _na_base(l2)
                    var = _na_var(l2)
                    T0 = l2 * 128
                    K0 = base * 64
                    qt, qk = qr.next()
                    kt, kk = kr.next()
                    vt, vk = vr.next()
                    bt, bk = br.next()
                    kreads = [("fm", "nak", g, t) for t in range(K0 // TT, min((K0 + 575) // TT, NSM // TT - 1) + 1)] + [("fm", "nak", "s", "pad")]
                    vreads = [("tm", "nav", g, t) for t in range(K0 // 128, min((K0 + 639) // 128, NSM // 128 - 1) + 1)] + [("tm", "nav", "s", "pad")]
                    S.dma("pool", qt[:], self.fm[("naq", g)][:, T0:T0 + 128].rearrange("(h d) t -> d h t", d=64), reads=[("fm", "naq", g, T0 // TT)], writes=[qk])
                    S.dma("pool", kt[:], self.fm[("nak", g)][:, K0:K0 + 576].rearrange("(h d) t -> d h t", d=64), reads=kreads, writes=[kk])
                    S.dma("pool", vt[:], self.tm[("nav", g)][K0:K0 + 640, :].rearrange("(c p) f -> p c f", p=128), reads=vreads, writes=[vk])
                    S.dma("sp", bt[:], self.nabias[l, var].rearrange("h q k -> q h k"), writes=[bk])
                    return dict(qt=qt, qk=qk, kt=kt, kk=kk, vt=vt, vk=vk, bt=bt, bk=bk, T0=T0)

                def unit_s(lq, h, onab):
                    def f():
                        if "c" not in lq:
                            lq["c"] = make_l2(lq["l2"])
                        c = lq["c"]
                        if "o" not in onab:
                            onab["o"] = onr.next()
                        ona, onak = onab["o"]
                        qt, kt, vt, bt = c["qt"], c["kt"], c["vt"], c["bt"]
                        qk, kk, vk, bk = c["qk"], c["kk"], c["vk"], c["bk"]
                        psA, pkA = self.bank()
                        psB, pkB = self.bank()
                        S.op("pe", lambda e: e.matmul(psA[:, 0:512], lhsT=qt[:, h, :], rhs=kt[:, h, 0:512], start=True, stop=True), reads=[qk, kk], writes=[pkA])
                        S.op("pe", lambda e: e.matmul(psB[:, 0:64], lhsT=qt[:, h, :], rhs=kt[:, h, 512:576], start=True, stop=True), reads=[qk, kk], writes=[pkB])
                        S.op("pe", lambda e: e.matmul(psB[:, 64:320], lhsT=qt[:, h, :], rhs=ckT[:, h, :], start=True, stop=True), reads=[qk, "ckT"], writes=[pkB])
                        sbt, sbk = sbr.next()
                        S.op("dve", lambda e: e.scalar_tensor_tensor(out=sbt[:, 0:512], in0=psA[:, 0:512], scalar=sc, in1=bt[:, h, 0:512], op0=ALU.mult, op1=ALU.add),
                             reads=[pkA, bk], writes=[sbk + "a"])
                        S.op("dve", lambda e: e.scalar_tensor_tensor(out=sbt[:, 512:576], in0=psB[:, 0:64], scalar=sc, in1=bt[:, h, 512:576], op0=ALU.mult, op1=ALU.add),
                             reads=[pkB, bk], writes=[sbk + "b"])
                        S.op("dve", lambda e: e.tensor_scalar(out=sbt[:, 640:896], in0=psB[:, 64:320], scalar1=sc, scalar2=None, op0=ALU.mult), reads=[pkB], writes=[sbk + "c"])
                        chunks = [(cc * 128, 128, vt[:, cc, h * 64:(h + 1) * 64]) for cc in range(5)]
                        chunks.append((640, 128, cvt[:, 0, h * 64:(h + 1) * 64]))
                        chunks.append((768, 128, cvt[:, 1, h * 64:(h + 1) * 64]))
                        u = dict(sbt=sbt, sbks=[sbk + "a", sbk + "b", sbk + "c", sbk + "pad"], nk=896, chunks=chunks, vk_list=[vk, "cvt2"],
                                 ona=ona, onak=onak, h=h, post=None)
                        if h == 3:
                            T0 = c["T0"]
                            u["post"] = lambda: S.dma("act", self.tm[("o", g)][T0:T0 + 128, 256:512], ona[:], reads=[onak], writes=[("tm", "o", g, T0 // 128, 1)])
                        return u
                    return f
                for l2 in range(32):
                    lq = {"l2": l2}
                    onab = {}
                    for h in range(4):
                        units.append(unit_s(lq, h, onab))
            prev = None
            for mk in units:
                u = mk()
                self.attn_A(u, bufs)
                if prev is not None:
                    self.attn_B(prev, bufs)
                prev = u
            self.attn_B(prev, bufs)
            S.finish()

    def mix_lin_gen(self, l, g, kind, ph):
        nc, S = self.nc, self.S
        ret = kind == "ret"
        dk = 64 if ret else 32
        HD = 4 * dk
        qscale = dk ** -0.5
        N = self.N[g]
        seqlen = 256 if g == "p" else NSM
        nseq = N // seqlen
        nblk = seqlen // 128
        qn, kn = ("rq", "rk") if ret else ("aq", "ak")
        rope = ret and g == "s"
        X = kind + "_"
        if True:
            if ret:
                lg = self.sb(ph, X + "lg", (128, 8))
                gconst = self.sb(ph, X + "gconst", (128, 2, 256))
                S.dma("sp", lg[:], self.ret_logit[l].partition_broadcast(128), writes=[X + "lg"])
                S.op("act", lambda e: e.activation(out=lg[:], in_=lg[:], func=AF.Exp, scale=-1.0), reads=[X + "lg"], writes=[X + "lg"])
                S.op("act", lambda e: e.activation(out=lg[:], in_=lg[:], func=AF.Ln, bias=self.ones_f[:, 0:1], scale=1.0), reads=[X + "lg", "ones_f"], writes=[X + "lg"])
                S.op("dve", lambda e: e.tensor_scalar(out=gconst[:].rearrange("p a (h d) -> p (a h) d", d=64), in0=lg[:].unsqueeze(2).to_broadcast([128, 8, 64]),
                                                      scalar1=-1.0, scalar2=None, op0=ALU.mult), reads=[X + "lg"], writes=[X + "gconst"])
            else:
                gup = self.sb(ph, X + "gup", (17, 2, 128), F32R)
                S.dma("pool", gup[0:16, :, :], self.gate_up[l].rearrange("a r f -> r a f"), writes=[X + "gup"])
                S.dma("pool", gup[16:17, :, :], self.gate_b[l:l + 1, :, :], writes=[X + "gup"])
                ng = self.sb(ph, X + "ng", (128, 64))
                S.dma("sp", ng[:], self.norm_g[l].partition_broadcast(128), writes=[X + "ng"])
                lrr = Rot(nc, ph, X + "lrr", (17, 128), F32R, 3)
                for t_, k_ in zip(lrr.t, lrr.k):
                    S.op("dve", lambda e, t_=t_: e.tensor_copy(out=t_[:], in_=self.ones_f[0:17, :]), reads=["ones_f"], writes=[k_])
                gtr = Rot(nc, ph, X + "gtr", (128, 128), F32, 3)
                lsr = Rot(nc, ph, X + "lsr", (16, 128), F32, 3)
            if rope:
                cosT = self.sb(ph, X + "cosT", (64, NSM))
                sinT = self.sb(ph, X + "sinT", (64, NSM))
                S.dma("sp", cosT[:], self.c_cos[:, :], writes=[X + "cosT"])
                S.dma("sp", sinT[:], self.c_sin[:, :], writes=[X + "sinT"])
                rt = Rot(nc, ph, X + "rt", (64, 4, 128), F32, 5)
                qkr = Rot(nc, ph, X + "qkr", (64, 4, 128), F32, 3)
            epsg = self.sb(ph, X + "epsg", (128, 1))
            S.op("dve", lambda e: e.memset(epsg[:], 1e-5 if ret else 1e-6), writes=[X + "epsg"])
            sts = [self.sb(ph, "state%d" % d, (dk, 4, 64), F32R) for d in range(2)]
            qr = Rot(nc, ph, X + "lq", (dk, 4, 128), F32, 3)
            kr = Rot(nc, ph, X + "lk", (dk, 4, 128), F32, 3)
            vr = Rot(nc, ph, X + "lv", (128, 256), F32R, 3)
            vsr = Rot(nc, ph, X + "lvs", (128, 256), F32, 3)
            gr = Rot(nc, ph, X + "lgt", (128, 256), F32, 2)
            ofr = Rot(nc, ph, X + "lof", (128, 256), F32, 3)
            epr = Rot(nc, ph, X + "lep", (dk, 512), F32, 3)
            enr = Rot(nc, ph, X + "len", (dk, 512), F32, 3)
            qgr = Rot(nc, ph, X + "lqg", (dk, 4, 128), F32R, 3)
            kgr = Rot(nc, ph, X + "lkg", (dk, 4, 128), F32R, 3)
            ktr = Rot(nc, ph, X + "lkt", (128, HD), F32R, 3)
            amr = Rot(nc, ph, X + "lam", (128, 512), F32R, 3)
            tsr = Rot(nc, ph, X + "lts", (dk, 256), F32, 2)
            osr = Rot(nc, ph, X + "los", (128, 256), F32, 2)
            o2r = Rot(nc, ph, X + "lo2", (128, 256), F32, 2)
            sqr = Rot(nc, ph, X + "lsq", (128, 256), F32, 2)
            str_ = Rot(nc, ph, X + "lst", (128, 16), F32, 2)
            sgr = Rot(nc, ph, X + "lsg", (128, 256), F32, 2)
            col0 = 512 if ret else 768

            def stage_a(s, d, n):
                U = self.uf if d == 0 else self.ub
                Uk = "uf" if d == 0 else "ub"
                if ret:
                    Ug, Ugk = U, Uk
                else:
                    Ug = self.ufg if d == 0 else self.ubg
                    Ugk = "ufg" if d == 0 else "ubg"
                T0 = s * seqlen + n * 128
                qt, qk = qr.next()
                kt, kk = kr.next()
                vt, vk = vr.next()
                S.dma("sp", qt[:], self.fm[(qn, g)][:, T0:T0 + 128].rearrange("(h d) t -> d h t", d=dk), reads=[("fm", qn, g, T0 // TT)], writes=[qk])
                S.dma("sp", kt[:], self.fm[(kn, g)][:, T0:T0 + 128].rearrange("(h d) t -> d h t", d=dk), reads=[("fm", kn, g, T0 // TT)], writes=[kk])
                vsrc = self.tm[("rvg", g)][T0:T0 + 128, 0:256] if ret else self.tm[("av", g)][T0:T0 + 128, :]
                vs_, vsk = vsr.next()
                S.dma("sp", vs_[:], vsrc, reads=[("tm", "rvg" if ret else "av", g, T0 // 128)], writes=[vsk])
                S.op("act", lambda e, vt=vt, vs_=vs_: e.activation(out=vt[:], in_=vs_[:], func=AF.Copy), reads=[vsk], writes=[vk])
                qf, kf, qfk, kfk = qt[:], kt[:], qk, kk
                if rope:
                    outs = []
                    pos0 = n * 128
                    cb = cosT[:, pos0:pos0 + 128].unsqueeze(1).to_broadcast([64, 4, 128])
                    sbb = sinT[:, pos0:pos0 + 128].unsqueeze(1).to_broadcast([64, 4, 128])
                    for (src_t, src_k, nm_) in ((qt, qk, qn), (kt, kk, kn)):
                        rr, rrk = rt.next()
                        srcv = self.fm[(nm_, g)][:, T0:T0 + 128].rearrange("(h b two s) t -> two b s h t", h=4, b=2, two=2, s=16)
                        for bb_ in range(2):
                            for tw in range(2):
                                p0 = bb_ * 32 + tw * 16
                                S.dma("sp", rr[p0:p0 + 16, :, :], srcv[1 - tw, bb_], reads=[("fm", nm_, g, T0 // TT)], writes=[rrk])
                        a1, a1k = rt.next()
                        S.op("pool", lambda e, a1=a1, src_t=src_t: e.tensor_tensor(out=a1[:], in0=src_t[:], in1=cb, op=ALU.mult), reads=[src_k, X + "cosT"], writes=[a1k])
                        S.op("dve", lambda e, rr=rr: e.tensor_tensor(out=rr[:], in0=rr[:], in1=sbb, op=ALU.mult), reads=[rrk, X + "sinT"], writes=[rrk])
                        o_, ok_ = qkr.next()
                        S.op("dve", lambda e, o_=o_, a1=a1, rr=rr: e.tensor_tensor(out=o_[:], in0=a1[:], in1=rr[:], op=ALU.add), reads=[a1k, rrk], writes=[ok_])
                        outs.append((o_, ok_))
                    qf, qfk = outs[0][0][:], outs[0][1]
                    kf, kfk = outs[1][0][:], outs[1][1]
                if ret:
                    gate_ap, gate_k = gconst[:, d, :], X + "gconst"
                else:
                    lrt, lrk = lrr.next()
                    ls_, lsk = lsr.next()
                    S.dma("sp", ls_[:], self.fm[("alr", g)][:, T0:T0 + 128], reads=[("fm", "alr", g, T0 // TT)], writes=[lsk])
                    S.op("dve", lambda e, lrt=lrt, ls_=ls_: e.tensor_copy(out=lrt[0:16, :], in_=ls_[:]), reads=[lsk], writes=[lrk])
                    ps, pk = self.bank()
                    S.op("pe", lambda e, ps=ps, lrt=lrt: e.matmul(ps[:, 0:128], lhsT=lrt[:, :], rhs=gup[:, d, :], start=True, stop=True), reads=[lrk, X + "gup"], writes=[pk])
                    gt_, gtk = gtr.next()
                    S.op("act", lambda e, ps=ps, gt_=gt_: e.activation(out=gt_[:], in_=ps[:, 0:128], func=AF.Exp, scale=-1.0), reads=[pk], writes=[gtk])
                    S.op("act", lambda e, gt_=gt_: e.activation(out=gt_[:], in_=gt_[:], func=AF.Ln, bias=self.ones_f[:, 0:1], scale=1.0), reads=[gtk, "ones_f"], writes=[gtk])
                    gate_ap, gate_k = gt_[:], gtk
                bps, bpk = self.bank()
                for h in range(4):
                    S.op("pe", lambda e, h=h, bps=bps, gate_ap=gate_ap: e.matmul(bps[0:dk, h * 128:(h + 1) * 128], lhsT=gate_ap[:, h * dk:(h + 1) * dk], rhs=Ug[:], start=True, stop=True),
                         reads=[gate_k, Ugk], writes=[bpk])
                ep, epk = epr.next()
                en, enk = enr.next()
                S.op("act", lambda e, ep=ep, bps=bps: e.activation(out=ep[:], in_=bps[0:dk, :], func=AF.Exp), reads=[bpk], writes=[epk])
                S.op("act", lambda e, en=en, bps=bps: e.activation(out=en[:], in_=bps[0:dk, :], func=AF.Exp, scale=-1.0), reads=[bpk], writes=[enk])
                qg, qgk = qgr.next()
                kg, kgk = kgr.next()
                S.op("dve", lambda e, qg=qg, ep=ep, qf=qf: e.scalar_tensor_tensor(out=qg[:].rearrange("p h t -> p (h t)"), in0=qf.rearrange("p h t -> p (h t)"), scalar=qscale, in1=ep[:], op0=ALU.mult, op1=ALU.mult),
                     reads=[qfk, epk], writes=[qgk])
                S.op("dve", lambda e, kg=kg, en=en, kf=kf: e.tensor_tensor(out=kg[:].rearrange("p h t -> p (h t)"), in0=kf.rearrange("p h t -> p (h t)"), in1=en[:], op=ALU.mult),
                     reads=[kfk, enk], writes=[kgk])
                tps, tpk = self.bank()
                for h in range(4):
                    S.op("pe", lambda e, h=h, tps=tps, kg=kg: e.transpose(tps[:, h * dk:(h + 1) * dk], kg[:, h, :].bitcast(F32), self.ident[0:dk, 0:dk]),
                         reads=[kgk, "ident"], writes=[tpk])
                ktok, ktk = ktr.next()
                self.evac(ktok[:], tps[:, 0:HD], [tpk], [ktk])
                aps, apk = self.bank()
                for h in range(4):
                    S.op("pe", lambda e, h=h, aps=aps, kg=kg, qg=qg: e.matmul(aps[:, h * 128:(h + 1) * 128], lhsT=kg[:, h, :], rhs=qg[:, h, :], start=True, stop=True), reads=[kgk, qgk], writes=[apk])
                am, amk = amr.next()
                S.op("dve", lambda e, am=am, aps=aps: e.tensor_tensor(out=am[:].rearrange("p (h t) -> p h t", t=128), in0=aps[:, :].rearrange("p (h t) -> p h t", t=128),
                                                                      in1=U[:].unsqueeze(1).to_broadcast([128, 4, 128]), op=ALU.mult), reads=[apk, Uk], writes=[amk])
                ams = [(am, amk)] * 4
                return dict(s=s, d=d, n=n, T0=T0, vt=vt, vk=vk, ep=ep, epk=epk, qg=qg, qgk=qgk, ktok=ktok, ktk=ktk, ams=ams)

            def stage_b(c):
                s, d, n, T0 = c["s"], c["d"], c["n"], c["T0"]
                vt, vk, ep, epk, qg, qgk, ktok, ktk, ams = c["vt"], c["vk"], c["ep"], c["epk"], c["qg"], c["qgk"], c["ktok"], c["ktk"], c["ams"]
                st_ = sts[d]
                stk = X + "state%d" % d
                last = 127 if d == 0 else 0
                first_blk = (n == 0) if d == 0 else (n == nblk - 1)
                last_blk = (n == nblk - 1) if d == 0 else (n == 0)
                if first_blk:
                    if g == "p":
                        S.op("dve", lambda e: e.tensor_copy(out=st_[:], in_=self.zeros_f[0:dk, :].rearrange("p (h v) -> p h v", v=64)), reads=["zeros_f"], writes=[stk])
                    else:
                        src = (self.sret if ret else self.sgla)[l, d].rearrange("h d v -> d h v")
                        S.dma("pool", st_[:], src, writes=[stk])
                ops_, opk = self.bank()
                for h in range(4):
                    am, amk = ams[h]
                    S.op("pe", lambda e, h=h, am=am: e.matmul(ops_[:, h * 64:(h + 1) * 64], lhsT=am[:, h * 128:(h + 1) * 128], rhs=vt[:, h * 64:(h + 1) * 64], start=True, stop=False), reads=[amk, vk], writes=[opk])
                    S.op("pe", lambda e, h=h: e.matmul(ops_[:, h * 64:(h + 1) * 64], lhsT=qg[:, h, :], rhs=st_[:, h, :], start=False, stop=True), reads=[qgk, stk], writes=[opk])
                sps, spk = self.bank()
                for h in range(4):
                    S.op("pe", lambda e, h=h: e.matmul(sps[0:dk, h * 64:(h + 1) * 64], lhsT=ktok[:, h * dk:(h + 1) * dk], rhs=vt[:, h * 64:(h + 1) * 64], start=True, stop=True),
                         reads=[ktk, vk], writes=[spk])
                ts, tsk = tsr.next()
                S.op("dve", lambda e: e.tensor_tensor(out=ts[:], in0=sps[0:dk, 0:256], in1=st_[:].bitcast(F32).rearrange("p h v -> p (h v)"), op=ALU.add), reads=[spk, stk], writes=[tsk])
                eb = ep[:].rearrange("p (h t) -> p h t", t=128)[:, :, last:last + 1].to_broadcast([dk, 4, 64])
                S.op("dve", lambda e: e.tensor_tensor(out=st_[:], in0=ts[:].rearrange("p (h v) -> p h v", v=64), in1=eb, op=ALU.mult), reads=[tsk, epk], writes=[stk])
                if d == 0:
                    of, ofk = ofr.next()
                    self.evac(of[:], ops_[:, 0:256], [opk], [ofk])
                    S.dma("pool", self.tm[("of" + kind, g)][T0:T0 + 128, :], of[:], reads=[ofk], writes=[("tm", "of" + kind, g, T0 // 128)])
                else:
                    of, ofk = ofr.next()
                    S.dma("sp", of[:], self.tm[("of" + kind, g)][T0:T0 + 128, :], reads=[("tm", "of" + kind, g, T0 // 128)], writes=[ofk])
                    gt2, g2k = gr.next()
                    gsrc = self.tm[("rvg", g)][T0:T0 + 128, 256:512] if ret else self.tm[("ag", g)][T0:T0 + 128, :]
                    S.dma("sp", gt2[:], gsrc, reads=[("tm", "rvg" if ret else "ag", g, T0 // 128)], writes=[g2k])
                    osum, osk = osr.next()
                    S.op("dve", lambda e: e.tensor_tensor(out=osum[:], in0=ops_[:, 0:256], in1=of[:], op=ALU.add), reads=[opk, ofk], writes=[osk])
                    o3 = osum[:].rearrange("p (h v) -> p h v", v=64)
                    sq_, sqk = sqr.next()
                    stt, stk2 = str_.next()
                    S.op("act", lambda e: e.activation(out=sq_[:], in_=osum[:], func=AF.Square), reads=[osk], writes=[sqk])
                    S.op("dve", lambda e: e.tensor_reduce(out=stt[:, 4:8], in_=sq_[:].rearrange("p (h v) -> p h v", v=64), axis=AX.X, op=ALU.add), reads=[sqk], writes=[stk2 + "b"])
                    o2, o2k = o2r.next()
                    if ret:
                        S.op("dve", lambda e: e.tensor_reduce(out=stt[:, 0:4], in_=o3, axis=AX.X, op=ALU.add), reads=[osk], writes=[stk2 + "a"])
                        S.op("dve", lambda e: e.tensor_scalar(out=stt[:, 0:4], in0=stt[:, 0:4], scalar1=1.0 / 64, scalar2=None, op0=ALU.mult), reads=[stk2 + "a"], writes=[stk2 + "a"])
                        S.op("dve", lambda e: e.tensor_tensor(out=stt[:, 8:12], in0=stt[:, 0:4], in1=stt[:, 0:4], op=ALU.mult), reads=[stk2 + "a"], writes=[stk2 + "c"])
                        S.op("dve", lambda e: e.scalar_tensor_tensor(out=stt[:, 12:16], in0=stt[:, 4:8], scalar=1.0 / 64, in1=stt[:, 8:12], op0=ALU.mult, op1=ALU.subtract), reads=[stk2 + "b", stk2 + "c"], writes=[stk2 + "d"])
                    else:
                        S.op("dve", lambda e: e.tensor_scalar(out=stt[:, 12:16], in0=stt[:, 4:8], scalar1=1.0 / 64, scalar2=None, op0=ALU.mult), reads=[stk2 + "b"], writes=[stk2 + "d"])
                    S.op("act", lambda e: e.activation(out=stt[:, 12:16], in_=stt[:, 12:16], func=AF.Sqrt, bias=epsg[:, 0:1], scale=1.0), reads=[stk2 + "d", X + "epsg"], writes=[stk2 + "d"])
                    S.op("dve", lambda e: e.reciprocal(out=stt[:, 12:16], in_=stt[:, 12:16]), reads=[stk2 + "d"], writes=[stk2 + "d"])
                    rb = stt[:, 12:16].unsqueeze(2).to_broadcast([128, 4, 64])
                    o23 = o2[:].rearrange("p (h v) -> p h v", v=64)
                    if ret:
                        mb = stt[:, 0:4].unsqueeze(2).to_broadcast([128, 4, 64])
                        S.op("dve", lambda e: e.tensor_tensor(out=o23, in0=o3, in1=mb, op=ALU.subtract), reads=[osk, stk2 + "a"], writes=[o2k])
                        S.op("dve", lambda e: e.tensor_tensor(out=o23, in0=o23, in1=rb, op=ALU.mult), reads=[o2k, stk2 + "d"], writes=[o2k])
                    else:
                        S.op("dve", lambda e: e.tensor_tensor(out=o23, in0=o3, in1=rb, op=ALU.mult), reads=[osk, stk2 + "d"], writes=[o2k])
                        nb = ng[:].unsqueeze(1).to_broadcast([128, 4, 64])
                        S.op("dve", lambda e: e.tensor_tensor(out=o23, in0=o23, in1=nb, op=ALU.mult), reads=[o2k, X + "ng"], writes=[o2k])
                    sg_, sgk = sgr.next()
                    S.op("act", lambda e: e.activation(out=sg_[:], in_=gt2[:], func=AF.Silu), reads=[g2k], writes=[sgk])
                    S.op("pool", lambda e: e.tensor_tensor(out=o2[:], in0=o2[:], in1=sg_[:], op=ALU.mult), reads=[o2k, sgk], writes=[o2k])
                    S.dma("pool", self.tm[("o", g)][T0:T0 + 128, col0:col0 + 256], o2[:], reads=[o2k], writes=[("tm", "o", g, T0 // 128, 2 if ret else 3)])
                if last_blk and g == "p":
                    dst = (self.o_sret if ret else self.o_sgla)[s, l, d].rearrange("h d v -> d h v")
                    S.dma("pool", dst, st_[:].bitcast(F32), reads=[stk])

            seqn = []
            for s in range(nseq):
                for d in range(2):
                    order = range(nblk) if d == 0 else range(nblk - 1, -1, -1)
                    for n in order:
                        seqn.append((s, d, n))
            prev = None
            for (s, d, n) in seqn:
                c = stage_a(s, d, n)
                if prev is not None:
                    stage_b(prev)
                prev = c
                yield
            stage_b(prev)
            yield


    def mix_lin_pair(self, l, g):
        with ExitStack() as ph:
            gens = [self.mix_lin_gen(l, g, "ret", ph), self.mix_lin_gen(l, g, "gla", ph)]
            while gens:
                for gen in list(gens):
                    try:
                        next(gen)
                    except StopIteration:
                        gens.remove(gen)
            self.S.finish()


def _alloc_zeros(b):
    pass


_CACHE = {}


def _build(depth=DEPTH):
    if depth not in _CACHE:
        _CACHE[depth] = Builder(depth)
    return _CACHE[depth]


def kernel(x_prompt, x_sample, cache_na_k, cache_na_v, state_ret, state_gla, c, c_ctx,
           w_mod, b_mod, g_pre_mix, g_post_mix, g_pre_ffn, g_post_ffn, w_in, w_out,
           pool_w, pool_scale, na_rpb, ret_decay_logit, gla_gate_up, gla_gate_b, gla_norm_g,
           w_ffn_gate, w_ffn_up, w_ffn_down, _depth=DEPTH):
    f = lambda a: np.ascontiguousarray(np.asarray(a, dtype=np.float32))
    L = _depth
    b = _build(L)

    def fm_(a, nch):
        return np.ascontiguousarray(a.reshape(a.shape[0], nch, 128).transpose(2, 0, 1))
    cos, sin, perm = _rope_consts()
    idx = np.arange(128)
    shared = {
        "w_mod": f(w_mod)[:L], "b_mod": fm_(f(b_mod)[:L], 48),
        "g_pre_mix": fm_(f(g_pre_mix)[:L], 8), "g_post_mix": fm_(f(g_post_mix)[:L], 8), "g_pre_ffn": fm_(f(g_pre_ffn)[:L], 8), "g_post_ffn": fm_(f(g_post_ffn)[:L], 8),
        "w_in": f(w_in)[:L], "w_out": f(w_out)[:L], "pool_w": f(pool_w)[:L], "pool_scale": f(pool_scale)[:L],
        "ret_decay_logit": f(ret_decay_logit)[:L].reshape(L, 8), "gla_gate_up": f(gla_gate_up)[:L], "gla_gate_b": f(gla_gate_b)[:L],
        "gla_norm_g": f(gla_norm_g)[:L], "w_ffn_gate": f(w_ffn_gate)[:L], "w_ffn_up": f(w_ffn_up)[:L], "w_ffn_down": f(w_ffn_down)[:L],
        "nabias": _na_bias_tables(f(na_rpb)[:L]),
        "c_ident": np.eye(128, dtype=np.float32),
        "c_uf": (idx[:, None] <= idx[None, :]).astype(np.float32),
        "c_ub": (idx[:, None] >= idx[None, :]).astype(np.float32),
        "c_ufg": (idx[:, None] <= idx[None, :]).astype(np.float32) * np.float32(-1.0 / 16.0),
        "c_ubg": (idx[:, None] >= idx[None, :]).astype(np.float32) * np.float32(-1.0 / 16.0),
        "c_poolm": _pool_consts(), "c_cos": cos, "c_sin": sin, "c_perm": perm,
        "c_zero": np.zeros((128, 256), np.float32),
    }
    xp = f(x_prompt)
    xs = f(x_sample)
    cc = f(c)
    cctx = f(c_ctx)
    in_maps = []
    for core in range(8):
        bb = core % 2
        m = dict(shared)
        m["xp"] = xp[core * 4:(core + 1) * 4].reshape(NPR, D)
        m["xs"] = xs[bb]
        m["cvec"] = np.ascontiguousarray(np.stack([cctx, cc[bb]], axis=0).reshape(2, 8, 128).transpose(2, 1, 0))
        m["ck"] = f(cache_na_k)[bb, :L].reshape(L, 256, 256)
        m["cv"] = f(cache_na_v)[bb, :L].reshape(L, 256, 256)
        m["sret"] = f(state_ret)[bb, :L]
        m["sgla"] = f(state_gla)[bb, :L]
        in_maps.append(m)
    res = run_bass_kernel_spmd(b.nc, in_maps, core_ids=list(range(8)))
    R = res.results
    y_prompt = np.concatenate([R[i]["yp"].reshape(4, 256, D) for i in range(8)], axis=0)
    y_sample = np.stack([R[0]["ys"], R[1]["ys"]], axis=0)
    nk = np.concatenate([R[i]["o_ck"].reshape(4, L, 256, 4, 64) for i in range(8)], axis=0)
    nv = np.concatenate([R[i]["o_cv"].reshape(4, L, 256, 4, 64) for i in range(8)], axis=0)
    sr = np.concatenate([R[i]["o_sret"] for i in range(8)], axis=0)
    sg = np.concatenate([R[i]["o_sgla"] for i in range(8)], axis=0)
    return (y_prompt.astype(np.float32), y_sample.astype(np.float32), nk.astype(np.float32), nv.astype(np.float32),
            sr.astype(np.float32), sg.astype(np.float32))
```

```python
import numpy as np
from contextlib import ExitStack
import concourse.bass as bass
import concourse.mybir as mybir
from concourse.bass_utils import run_bass_kernel_spmd

F32 = mybir.dt.float32
F32R = mybir.dt.float32r
BF16 = mybir.dt.bfloat16
AF = mybir.ActivationFunctionType
ALU = mybir.AluOpType
AX = mybir.AxisListType

D = 1024
DEPTH = 4
NPR = 1024
NSM = 4096
NSP = NSM + 128
TT = 512
P_IN = 2832
DFF = 2816
COMPUTE = ("pe", "dve", "act", "pool")
NEG = -30000.0


class Sched:
    def __init__(self, nc, stack, n_dma=60):
        self.nc = nc
        self.eng = {"pe": nc.tensor, "dve": nc.vector, "act": nc.scalar,
                    "pool": nc.gpsimd, "sp": nc.sync}
        self.sem = {e: stack.enter_context(nc.semaphore("s_" + e)) for e in COMPUTE}
        self.cnt = {e: 0 for e in COMPUTE}
        self.dsem = [stack.enter_context(nc.semaphore("d%d" % i)) for i in range(n_dma)]
        self.dval = [0] * n_dma
        self.dq = [0, 0, 0]
        self.ndma = n_dma
        self.seen = {e: {} for e in self.eng}
        self.last_w = {}
        self.readers = {}

    def _wait(self, e, tok):
        kind, key, val = tok
        if kind == "c" and key == e and e == "pe":
            return
        k = (kind, key)
        if self.seen[e].get(k, 0) >= val:
            return
        sem = self.sem[key] if kind == "c" else self.dsem[key]
        self.eng[e].wait_ge(sem, val)
        self.seen[e][k] = val

    def _deps(self, reads, writes):
        deps = {}
        for r in reads:
            t = self.last_w.get(r)
            if t is not None and deps.get((t[0], t[1]), 0) < t[2]:
                deps[(t[0], t[1])] = t[2]
        for w in writes:
            t = self.last_w.get(w)
            if t is not None and deps.get((t[0], t[1]), 0) < t[2]:
                deps[(t[0], t[1])] = t[2]
            for k, v in self.readers.get(w, {}).items():
                if deps.get(k, 0) < v:
                    deps[k] = v
        return deps

    def _commit(self, tok, reads, writes):
        for w in writes:
            self.last_w[w] = tok
            self.readers[w] = {}
        k = (tok[0], tok[1])
        for r in reads:
            d = self.readers.setdefault(r, {})
            if d.get(k, 0) < tok[2]:
                d[k] = tok[2]

    def op(self, e, fn, reads=(), writes=()):
        for k, v in self._deps(reads, writes).items():
            self._wait(e, (k[0], k[1], v))
        ins = fn(self.eng[e])
        self.cnt[e] += 1
        ins.then_inc(self.sem[e], 1)
        self._commit(("c", e, self.cnt[e]), reads, writes)

    def dma(self, q, out, in_, reads=(), writes=()):
        half = self.ndma // 3
        qi = {"sp": 0, "pool": 1, "act": 2}[q]
        slot = qi * half + self.dq[qi]
        self.dq[qi] = (self.dq[qi] + 1) % half
        if self.dval[slot] > 0:
            self._wait(q, ("d", slot, self.dval[slot]))
        for k, v in self._deps(reads, writes).items():
            self._wait(q, (k[0], k[1], v))
        self.dval[slot] += 16
        self.eng[q].dma_start(out=out, in_=in_).then_inc(self.dsem[slot], 16)
        self._commit(("d", slot, self.dval[slot]), reads, writes)

    def finish(self, e=None):
        for en in (list(self.eng) if e is None else [e]):
            for slot in range(self.ndma):
                if self.dval[slot] > 0:
                    self._wait(en, ("d", slot, self.dval[slot]))
            for c in COMPUTE:
                if self.cnt[c] > 0 and c != en:
                    self._wait(en, ("c", c, self.cnt[c]))


class Rot:
    uid = [0]

    def __init__(self, nc, stack, name, shape, dtype, n):
        Rot.uid[0] += 1
        self.t = [stack.enter_context(nc.sbuf_tensor("%s_%d_r%d" % (name, i, Rot.uid[0]), list(shape), dtype)) for i in range(n)]
        self.k = ["%s_%d" % (name, i) for i in range(n)]
        self.i = 0

    def next(self):
        i = self.i
        self.i = (i + 1) % len(self.t)
        return self.t[i], self.k[i]


def _pool_consts():
    wins = (2, 4, 8, 16)

    def mat(L):
        t = np.arange(L)
        M = np.zeros((4, L, L), np.float64)
        for gi, win in enumerate(wins):
            left = win // 2
            lo = np.clip(t - left, 0, L)
            hi = np.clip(t - left + win, 0, L)
            for tt in range(L):
                M[gi, lo[tt]:hi[tt], tt] = 1.0 / (hi[tt] - lo[tt])
                M[gi, tt, tt] -= 1.0
        return M
    m64 = mat(64)
    m256 = mat(256)
    out = np.zeros((128, 4, 5, 128), np.float32)
    for g in range(4):
        out[0:64, g, 0, 0:64] = m64[g]
        out[64:128, g, 0, 64:128] = m64[g]
        for a in range(2):
            for b in range(2):
                out[:, g, 1 + a * 2 + b, :] = m256[g, b * 128:(b + 1) * 128, a * 128:(a + 1) * 128]
    return out


def _rope_consts():
    nf = 16
    inv = (10000.0 ** (-np.arange(nf, dtype=np.float32) / nf)).astype(np.float32)
    t = np.arange(NSM)
    cos = np.zeros((64, NSM), np.float32)
    sin = np.zeros((64, NSM), np.float32)
    perm = np.zeros((64, 64), np.float32)
    for d in range(64):
        blk = d // 32
        dd = d % 32
        pos = (t // 64) if blk == 0 else (t % 64)
        ang = pos.astype(np.float32) * inv[dd % nf]
        cos[d] = np.cos(ang).astype(np.float32)
        if dd < nf:
            sin[d] = -np.sin(ang).astype(np.float32)
            partner = d + nf
        else:
            sin[d] = np.sin(ang).astype(np.float32)
            partner = d - nf
        perm[partner, d] = 1.0
    return cos, sin, perm


NA_VARIANTS = (0, 1, 2, 30, 31)


def _na_base(l2):
    return int(np.clip(2 * l2 - 4, 0, 56))


def _na_var(l2):
    if l2 <= 1:
        return l2
    if l2 >= 30:
        return l2 - 27
    return 2


def _na_bias_tables(rpb):
    L = rpb.shape[0]
    out = np.full((L, 5, 4, 128, 576), NEG, np.float32)
    cols = np.arange(64)
    c0 = np.clip(cols - 8, 0, 48)
    for vi, l2 in enumerate(NA_VARIANTS):
        base = _na_base(l2)
        for a in range(2):
            r = 2 * l2 + a
            r0 = int(np.clip(r - 4, 0, 56))
            for kr in range(r0, r0 + 8):
                kl = kr - base
                assert 0 <= kl < 9
                dr = kr - r + 7
                for c in range(64):
                    d0 = int(c0[c]) - c + 15
                    out[:, vi, :, a * 64 + c, kl * 64 + int(c0[c]):kl * 64 + int(c0[c]) + 16] = rpb[:, :, dr, d0:d0 + 16]
    return out


class Builder:
    def __init__(self, depth=DEPTH):
        self.depth = depth
        nc = bass.Bass("TRN2", target_bir_lowering=False)
        self.nc = nc
        L = depth

        def din(name, shape):
            return nc.dram_tensor(name, list(shape), F32, kind="ExternalInput").ap()

        def dout(name, shape):
            return nc.dram_tensor(name, list(shape), F32, kind="ExternalOutput").ap()

        def dscr(name, shape):
            return nc.dram_tensor(name, list(shape), F32, kind="Internal").ap()

        self.xin = {"p": din("xp", (NPR, D)), "s": din("xs", (NSM, D))}
        self.cvec = din("cvec", (128, 8, 2))
        self.w_mod = din("w_mod", (L, D, 6 * D))
        self.b_mod = din("b_mod", (128, L, 48))
        self.gvec = [din(n, (128, L, 8)) for n in ("g_pre_mix", "g_post_mix", "g_pre_ffn", "g_post_ffn")]
        self.w_in = din("w_in", (L, D, P_IN))
        self.w_out = din("w_out", (L, D, D))
        self.pool_w = din("pool_w", (L, 4, 64, 64))
        self.pool_scale = din("pool_scale", (L, 256))
        self.ret_logit = din("ret_decay_logit", (L, 8))
        self.gate_up = din("gla_gate_up", (L, 2, 16, 128))
        self.gate_b = din("gla_gate_b", (L, 2, 128))
        self.norm_g = din("gla_norm_g", (L, 64))
        self.w_gate = din("w_ffn_gate", (L, D, DFF))
        self.w_up = din("w_ffn_up", (L, D, DFF))
        self.w_down = din("w_ffn_down", (L, DFF, D))
        self.ck = din("ck", (L, 256, 256))
        self.cv = din("cv", (L, 256, 256))
        self.sret = din("sret", (L, 2, 4, 64, 64))
        self.sgla = din("sgla", (L, 2, 4, 32, 64))
        self.nabias = din("nabias", (L, 5, 4, 128, 576))
        self.c_ident = din("c_ident", (128, 128))
        self.c_uf = din("c_uf", (128, 128))
        self.c_ub = din("c_ub", (128, 128))
        self.c_ufg = din("c_ufg", (128, 128))
        self.c_ubg = din("c_ubg", (128, 128))
        self.c_poolm = din("c_poolm", (128, 4, 5, 128))
        self.c_cos = din("c_cos", (64, NSM))
        self.c_sin = din("c_sin", (64, NSM))
        self.c_perm = din("c_perm", (64, 64))
        self.c_zero = din("c_zero", (128, 256))

        self.yout = {"p": dout("yp", (NPR, D)), "s": dout("ys", (NSM, D))}
        self.o_ck = dout("o_ck", (4, L, 256, 256))
        self.o_cv = dout("o_cv", (4, L, 256, 256))
        self.o_sret = dout("o_sret", (4, L, 2, 4, 64, 64))
        self.o_sgla = dout("o_sgla", (4, L, 2, 4, 32, 64))

        self.N = {"p": NPR, "s": NSM}
        self.NP = {"p": NPR, "s": NSP}
        self.xT = {g: dscr("xT_" + g, (8, 128, self.N[g])) for g in "ps"}
        self.fm = {}
        self.tm = {}
        for g in "ps":
            n = self.NP[g]
            for nm, rows in (("naq", 256), ("nak", 256), ("rq", 256), ("rk", 256), ("aq", 128), ("ak", 128), ("alr", 16)):
                self.fm[(nm, g)] = dscr("fm_%s_%s" % (nm, g), (rows, n))
            for nm, cols in (("vpool", 256), ("nav", 256), ("rvg", 512), ("av", 256), ("ag", 256), ("ofret", 256), ("ofgla", 256), ("o", 1024)):
                self.tm[(nm, g)] = dscr("tm_%s_%s" % (nm, g), (n, cols))

        with ExitStack() as st:
            self.st = st
            self.S = Sched(nc, st)
            self.build()

    def sb(self, stack, name, shape, dtype=F32):
        Rot.uid[0] += 1
        return stack.enter_context(self.nc.sbuf_tensor("%s_u%d" % (name, Rot.uid[0]), list(shape), dtype))

    def bank(self):
        i = self.bi
        self.bi = (i + 1) % 8
        return self.banks[i], "bank%d" % i

    def evac(self, out, in_, reads, writes):
        self.ev = 1 - self.ev
        if self.ev:
            self.S.op("act", lambda e: e.activation(out=out, in_=in_, func=AF.Copy), reads, writes)
        else:
            self.S.op("dve", lambda e: e.tensor_copy(out=out, in_=in_), reads, writes)

    def build(self):
        nc, S, st = self.nc, self.S, self.st
        L = self.depth
        self.bi = 0
        self.ev = 0
        self.banks = [st.enter_context(nc.psum_tensor("bank%d" % i, [128, 512], F32)) for i in range(8)]

        self.ident = self.sb(st, "ident", (128, 128))
        self.uf = self.sb(st, "uf", (128, 128))
        self.ub = self.sb(st, "ub", (128, 128))
        self.ones_f = self.sb(st, "ones_f", (128, 128))
        self.ones_r = self.sb(st, "ones_r", (128, 128), F32R)
        S.dma("sp", self.ident[:], self.c_ident[:, :], writes=["ident"])
        S.dma("sp", self.uf[:], self.c_uf[:, :], writes=["uf"])
        S.dma("sp", self.ub[:], self.c_ub[:, :], writes=["ub"])
        self.ufg = self.sb(st, "ufg", (128, 128))
        self.ubg = self.sb(st, "ubg", (128, 128))
        S.dma("sp", self.ufg[:], self.c_ufg[:, :], writes=["ufg"])
        S.dma("sp", self.ubg[:], self.c_ubg[:, :], writes=["ubg"])
        S.op("dve", lambda e: e.memset(self.ones_f[:], 1.0), writes=["ones_f"])
        self.zeros_f = self.sb(st, "zeros_f", (128, 256))
        S.op("dve", lambda e: e.memset(self.zeros_f[:], 0.0), writes=["zeros_f"])
        S.op("dve", lambda e: e.tensor_copy(out=self.ones_r[:], in_=self.ones_f[:]), reads=["ones_f"], writes=["ones_r"])
        self.ident_b = self.sb(st, "ident_b", (128, 128), BF16)
        S.op("dve", lambda e: e.tensor_copy(out=self.ident_b[:], in_=self.ident[:]), reads=["ident"], writes=["ident_b"])
        self.ones_b = self.sb(st, "ones_b", (128, 128), BF16)
        S.op("dve", lambda e: e.tensor_copy(out=self.ones_b[:], in_=self.ones_f[:]), reads=["ones_f"], writes=["ones_b"])

        self.gv = []
        for i, gsrc in enumerate(self.gvec):
            t = self.sb(st, "gv%d" % i, (128, L, 8))
            S.dma("sp", t[:], gsrc[:, :, :], writes=["gv%d" % i])
            self.gv.append(t)
        self.modT = self.sb(st, "modT", (128, L, 48, 2))
        self.bmod = self.sb(st, "bmod", (128, L, 48))
        S.dma("sp", self.bmod[:], self.b_mod[:, :, :], writes=["bmod"])
        self.gm = [self.sb(st, "gm%d" % i, (128, L, 8, 2)) for i in range(4)]

        with ExitStack() as ph:
            z = self.sb(ph, "ztile", (128, 512))
            S.op("dve", lambda e: e.memset(z[:], 0.0), writes=["ztile"])
            for nm in ("nak",):
                S.dma("sp", self.fm[(nm, "s")][0:128, NSM:NSP], z[:, 0:128], reads=["ztile"], writes=[("fm", nm, "s", "pad")])
                S.dma("sp", self.fm[(nm, "s")][128:256, NSM:NSP], z[:, 0:128], reads=["ztile"], writes=[("fm", nm, "s", "pad")])
            S.dma("sp", self.tm[("nav", "s")][NSM:NSP, :], z[:, 0:256], reads=["ztile"], writes=[("tm", "nav", "s", "pad")])
            S.finish()

        import os
        stop = os.environ.get("KSTOP", "")
        grps = os.environ.get("KGRPS", "ps")
        self.phase_mod()
        if stop != "mod":
            self.phase_x()
        if stop not in ("mod", "x"):
            for l in range(L):
                self.phase_a(l, grps)
                if stop == "a":
                    break
                for g in grps:
                    self.mix_pool(l, g)
                    if stop == "pool":
                        continue
                    self.mix_attn(l, g)
                    if stop == "attn":
                        continue
                    self.mix_lin_pair(l, g)
                if stop in ("pool", "attn", "ret", "gla"):
                    break
                self.phase_b(l, grps, last=(l == L - 1))
        S.finish()

    def phase_mod(self):
        nc, S = self.nc, self.S
        L = self.depth
        with ExitStack() as ph:
            cv = self.sb(ph, "cvt", (128, 8, 2))
            scv = self.sb(ph, "scv", (128, 8, 2), F32R)
            S.dma("sp", cv[:], self.cvec[:, :, :], writes=["cvt"])
            S.op("act", lambda e: e.activation(out=scv[:], in_=cv[:], func=AF.Silu), reads=["cvt"], writes=["scv"])
            wr = Rot(nc, ph, "wm", (128, 8, 512), F32R, 3)
            for l in range(L):
                wv = self.w_mod[l].rearrange("(c p) f -> p c f", p=128)
                for jb in range(12):
                    wt, wk = wr.next()
                    S.dma("pool", wt[:], wv[:, :, jb * 512:(jb + 1) * 512], writes=[wk])
                    ps, pk = self.bank()
                    for j in range(4):
                        for k in range(8):
                            S.op("pe", lambda e, j=j, k=k: e.matmul(ps[:, j * 2:j * 2 + 2], lhsT=wt[:, k, j * 128:(j + 1) * 128],
                                                                   rhs=scv[:, k, :], start=(k == 0), stop=(k == 7)),
                                 reads=[wk, "scv"], writes=[pk])
                    S.op("dve", lambda e, jb=jb, l=l, ps=ps: e.tensor_tensor(
                        out=self.modT[:, l, jb * 4:(jb + 1) * 4, :], in0=ps[:, 0:8].rearrange("p (j v) -> p j v", v=2),
                        in1=self.bmod[:, l, jb * 4:(jb + 1) * 4].unsqueeze(2).to_broadcast([128, 4, 2]), op=ALU.add),
                        reads=[pk, "bmod"], writes=["modT"])
            for l in range(L):
                for i, (gi, sci) in enumerate(((0, 1), (1, 2), (2, 4), (3, 5))):
                    gsrc = self.gv[gi][:, l, :].unsqueeze(2).to_broadcast([128, 8, 2])
                    mod = self.modT[:, l, sci * 8:(sci + 1) * 8, :]
                    if i % 2 == 0:
                        S.op("dve", lambda e, mod=mod, gsrc=gsrc, i=i, l=l: e.scalar_tensor_tensor(
                            out=self.gm[i][:, l, :, :], in0=mod, scalar=1.0, in1=gsrc, op0=ALU.add, op1=ALU.mult),
                            reads=["modT", "gv%d" % gi], writes=["gm%d" % i])
                    else:
                        S.op("dve", lambda e, mod=mod, gsrc=gsrc, i=i, l=l: e.tensor_tensor(
                            out=self.gm[i][:, l, :, :], in0=mod, in1=gsrc, op=ALU.mult),
                            reads=["modT", "gv%d" % gi], writes=["gm%d" % i])
            S.finish()

    def phase_x(self):
        nc, S = self.nc, self.S
        with ExitStack() as ph:
            xi = Rot(nc, ph, "xi", (128, 1024), F32, 2)
            xo = Rot(nc, ph, "xo", (128, 8, 128), F32, 2)
            for g in "ps":
                xv = self.xT[g].rearrange("c p t -> p c t")
                for sub in range(self.N[g] // 128):
                    a, ak = xi.next()
                    S.dma("sp", a[:], self.xin[g][sub * 128:(sub + 1) * 128, :], writes=[ak])
                    o, ok = xo.next()
                    for half in range(2):
                        ps, pk = self.bank()
                        for j in range(4):
                            c = half * 4 + j
                            S.op("pe", lambda e, ps=ps, j=j, c=c, a=a: e.transpose(ps[:, j * 128:(j + 1) * 128], a[:, c * 128:(c + 1) * 128], self.ident[:]),
                                 reads=[ak, "ident"], writes=[pk])
                        self.evac(o[:, half * 4:(half + 1) * 4, :], ps[:, :].rearrange("p (j t) -> p j t", t=128), [pk], [ok + "h%d" % half])
                    S.dma("sp", xv[:, :, sub * 128:(sub + 1) * 128], o[:], reads=[ok + "h0", ok + "h1"],
                          writes=[("xT", g, sub // 4)])
            S.finish()

    def norm_mod(self, src, srck, dst, dstk, sq, rstd, tmp, l, gmi, shi, v):
        S = self.S
        S.op("act", lambda e: e.activation(out=sq[:], in_=src[:], func=AF.Square), reads=[srck], writes=["sq"])
        ps, pk = self.bank()
        for c in range(8):
            S.op("pe", lambda e, c=c: e.matmul(ps[:, :], lhsT=self.ones_b[:], rhs=sq[:, c, :], start=(c == 0), stop=(c == 7)),
                 reads=["sq", "ones_b"], writes=[pk])
        S.op("act", lambda e: e.activation(out=rstd[:], in_=ps[:, :], func=AF.Sqrt, bias=self.epsb[:, 0:1], scale=1.0 / D), reads=[pk, "epsb"], writes=["rstd0", "rstd"])
        S.op("dve", lambda e: e.reciprocal(out=rstd[:], in_=rstd[:]), reads=["rstd0"], writes=["rstd0", "rstd"])
        for c in range(8):
            t, tk = tmp.next()
            S.op("dve", lambda e, c=c, t=t: e.scalar_tensor_tensor(out=t[:], in0=src[:, c, :], scalar=self.gm[gmi][:, l, c, v:v + 1],
                                                               in1=rstd[:], op0=ALU.mult, op1=ALU.mult),
                 reads=[srck, "rstd", "gm%d" % gmi], writes=[tk])
            S.op("act", lambda e, c=c, t=t: e.activation(out=dst[:, c, :], in_=t[:], func=AF.Identity,
                                                         bias=self.modT[:, l, shi * 8 + c, v:v + 1], scale=1.0),
                 reads=[tk, "modT"], writes=[dstk])

    def post_norm_res(self, yraw, xt, sq, rstd, tmp, l, ggi, v, yk="yraw", xk="xt"):
        S = self.S
        S.op("act", lambda e: e.activation(out=sq[:], in_=yraw[:], func=AF.Square), reads=[yk], writes=["sq"])
        ps, pk = self.bank()
        for c in range(8):
            S.op("pe", lambda e, c=c: e.matmul(ps[:, :], lhsT=self.ones_b[:], rhs=sq[:, c, :], start=(c == 0), stop=(c == 7)),
                 reads=["sq", "ones_b"], writes=[pk])
        S.op("act", lambda e: e.activation(out=rstd[:], in_=ps[:, :], func=AF.Sqrt, bias=self.epsb[:, 0:1], scale=1.0 / D), reads=[pk, "epsb"], writes=["rstd0", "rstd"])
        S.op("dve", lambda e: e.reciprocal(out=rstd[:], in_=rstd[:]), reads=["rstd0"], writes=["rstd0", "rstd"])
        for c in range(8):
            t, tk = tmp.next()
            S.op("dve", lambda e, c=c, t=t: e.scalar_tensor_tensor(out=t[:], in0=yraw[:, c, :], scalar=self.gm[ggi][:, l, c, v:v + 1],
                                                               in1=rstd[:], op0=ALU.mult, op1=ALU.mult),
                 reads=[yk, "rstd", "gm%d" % ggi], writes=[tk])
            S.op("dve", lambda e, c=c, t=t: e.tensor_tensor(out=xt[:, c, :], in0=xt[:, c, :], in1=t[:], op=ALU.add),
                 reads=[tk, xk], writes=[xk])

    def dense_common(self, ph):
        nc = self.nc
        self.xt = self.sb(ph, "xt", (128, 8, TT))
        self.hT = self.sb(ph, "hT", (128, 8, TT), BF16)
        self.sq = self.sb(ph, "sq", (128, 8, TT), BF16)
        self.rstd = self.sb(ph, "rstd", (128, TT))
        self.tmp = Rot(nc, ph, "tmp", (128, TT), F32, 3)
        self.wr = Rot(nc, ph, "wr", (128, 4096), BF16, 4)
        self.stg = Rot(nc, ph, "stg", (128, TT), F32, 3)
        self.epsb = self.sb(ph, "epsb", (128, 1))
        self.S.op("dve", lambda e: e.memset(self.epsb[:], 1e-6), writes=["epsb"])

    def phase_a(self, l, groups):
        nc, S = self.nc, self.S

        def plan_for(g):
            return [
                (0, 512, [(256, 128, "naq", 0), (384, 128, "naq", 128)], [(0, 256, "vpool", 0)]),
                (512, 512, [(0, 128, "nak", 0), (128, 128, "nak", 128)], [(256, 256, "nav", 0)] + ([(0, 256, "ck", 0)] if g == "p" else [])),
                (1024, 512, [(0, 128, "rq", 0), (128, 128, "rq", 128), (256, 128, "rk", 0), (384, 128, "rk", 128)], []),
                (1536, 512, [], [(0, 512, "rvg", 0)]),
                (2048, 512, [(0, 128, "aq", 0), (128, 128, "ak", 0)], [(256, 256, "av", 0)]),
                (2560, 272, [(256, 16, "alr", 0)], [(0, 256, "ag", 0)]),
            ]
        wv = self.w_in[l].rearrange("(c p) f -> p c f", p=128)
        tiles = [(g, t) for g in groups for t in range(self.N[g] // TT)]
        with ExitStack() as ph:
            self.dense_common(ph)
            xts = [self.xt, self.sb(ph, "xtb", (128, 8, TT))]
            xks = ["xt", "xtb"]
            hTs = [self.hT, self.sb(ph, "hTb", (128, 8, TT), BF16)]
            hks = ["hT", "hTb"]

            def prep(i):
                g, t = tiles[i]
                v = 0 if g == "p" else 1
                t0 = t * TT
                xv = self.xT[g].rearrange("c p t -> p c t")
                xt, xk, hT, hk = xts[i % 2], xks[i % 2], hTs[i % 2], hks[i % 2]
                S.dma("sp", xt[:], xv[:, :, t0:t0 + TT], reads=[("xT", g, t)], writes=[xk])
                self.norm_mod(xt, xk, hT, hk, self.sq, self.rstd, self.tmp, l, 0, 0, v)
                return dict(g=g, t=t, t0=t0, hT=hT, hk=hk)

            def proj(c, blocks):
                g, t, t0, hT, hk = c["g"], c["t"], c["t0"], c["hT"], c["hk"]
                for (c0, ncol, fms, tms) in blocks:
                    wt, wk = self.wr.next()
                    w3 = wt[:, 0:8 * ncol].rearrange("p (c f) -> p c f", f=ncol)
                    S.dma("pool", w3, wv[:, :, c0:c0 + ncol], writes=[wk])
                    for (off, m, nm, r0) in fms:
                        ps, pk = self.bank()
                        for k in range(8):
                            S.op("pe", lambda e, k=k, off=off, m=m, ps=ps, w3=w3: e.matmul(ps[0:m, :], lhsT=w3[:, k, off:off + m], rhs=hT[:, k, :],
                                                                                         start=(k == 0), stop=(k == 7)),
                                 reads=[wk, hk], writes=[pk])
                        sg, sk = self.stg.next()
                        self.evac(sg[0:m, :], ps[0:m, :], [pk], [sk])
                        S.dma("sp", self.fm[(nm, g)][r0:r0 + m, t0:t0 + TT], sg[0:m, :], reads=[sk], writes=[("fm", nm, g, t)])
                    for (off, n, nm, cc0) in tms:
                        for sub in range(TT // 128):
                            ps, pk = self.bank()
                            for k in range(8):
                                S.op("pe", lambda e, k=k, off=off, n=n, ps=ps, w3=w3, sub=sub: e.matmul(
                                    ps[:, 0:n], lhsT=hT[:, k, sub * 128:(sub + 1) * 128], rhs=w3[:, k, off:off + n],
                                    start=(k == 0), stop=(k == 7)), reads=[wk, hk], writes=[pk])
                            sg, sk = self.stg.next()
                            self.evac(sg[:, 0:n], ps[:, 0:n], [pk], [sk])
                            tok0 = t0 + sub * 128
                            if nm == "ck":
                                seq, tt0 = tok0 // 256, tok0 % 256
                                S.dma("sp", self.o_ck[seq, l, tt0:tt0 + 128, :], sg[:, 0:n], reads=[sk])
                            else:
                                S.dma("sp", self.tm[(nm, g)][tok0:tok0 + 128, cc0:cc0 + n], sg[:, 0:n], reads=[sk],
                                      writes=[("tm", nm, g, tok0 // 128)])
                                if nm == "nav" and g == "p":
                                    seq, tt0 = tok0 // 256, tok0 % 256
                                    S.dma("sp", self.o_cv[seq, l, tt0:tt0 + 128, :], sg[:, 0:n], reads=[sk])

            ctx = prep(0)
            for i in range(len(tiles)):
                plan = plan_for(ctx["g"])
                proj(ctx, plan[0:3])
                nxt = prep(i + 1) if i + 1 < len(tiles) else None
                proj(ctx, plan[3:6])
                ctx = nxt
            S.finish()

    def phase_b(self, l, groups, last):
        nc, S = self.nc, self.S
        wo = self.w_out[l].rearrange("(c p) f -> p c f", p=128)
        wg = self.w_gate[l].rearrange("(c p) f -> p c f", p=128)
        wu = self.w_up[l].rearrange("(c p) f -> p c f", p=128)
        wd = self.w_down[l].rearrange("(c p) f -> p c f", p=128)
        tiles = [(g, t) for g in groups for t in range(self.N[g] // TT)]
        ntile = len(tiles)
        with ExitStack() as ph:
            self.dense_common(ph)
            xts = [self.xt, self.sb(ph, "xtb", (128, 8, TT))]
            xks = ["xt", "xtb"]
            yr1 = self.sb(ph, "yraw1", (128, 8, TT))
            yr2 = self.sb(ph, "yraw2", (128, 8, TT))
            hid = self.sb(ph, "hid", (128, 22, TT), BF16)
            oin = Rot(nc, ph, "oin", (128, 1024), F32, 4)
            yo = Rot(nc, ph, "yo", (128, 1024), F32, 2)
            sgt = Rot(nc, ph, "sgt", (128, TT), F32, 2)

            def load(i):
                g, t = tiles[i]
                t0 = t * TT
                xv = self.xT[g].rearrange("c p t -> p c t")
                xt, xk = xts[i % 2], xks[i % 2]
                S.dma("sp", xt[:], xv[:, :, t0:t0 + TT], reads=[("xT", g, t)], writes=[xk])
                os_ = []
                for sub in range(TT // 128):
                    a, ak = oin.next()
                    tok0 = t0 + sub * 128
                    S.dma("sp", a[:], self.tm[("o", g)][tok0:tok0 + 128, :], reads=[("tm", "o", g, tok0 // 128, q) for q in range(4)], writes=[ak])
                    os_.append((a, ak))
                return dict(g=g, v=(0 if g == "p" else 1), xv=xv, t=t, t0=t0, xt=xt, xk=xk, os=os_)

            def s1(c):
                for sub, (a, ak) in enumerate(c["os"]):
                    for half in range(2):
                        ps, pk = self.bank()
                        for j in range(4):
                            cc = half * 4 + j
                            S.op("pe", lambda e, ps=ps, j=j, cc=cc, a=a: e.transpose(ps[:, j * 128:(j + 1) * 128], a[:, cc * 128:(cc + 1) * 128], self.ident[:]),
                                 reads=[ak, "ident"], writes=[pk])
                        self.evac(self.hT[:, half * 4:(half + 1) * 4, sub * 128:(sub + 1) * 128],
                                  ps[:, :].rearrange("p (j t) -> p j t", t=128), [pk], ["hT"])
                for ob in range(2):
                    wt, wk = self.wr.next()
                    w3 = wt[:, :].rearrange("p (c f) -> p c f", f=512)
                    S.dma("pool", w3, wo[:, :, ob * 512:(ob + 1) * 512], writes=[wk])
                    for j in range(4):
                        ps, pk = self.bank()
                        for k in range(8):
                            S.op("pe", lambda e, k=k, j=j, ps=ps, w3=w3: e.matmul(ps[:, :], lhsT=w3[:, k, j * 128:(j + 1) * 128], rhs=self.hT[:, k, :],
                                                                              start=(k == 0), stop=(k == 7)), reads=[wk, "hT"], writes=[pk])
                        self.evac(yr1[:, ob * 4 + j, :], ps[:, :], [pk], ["yraw1"])

            def s2(c):
                self.post_norm_res(yr1, c["xt"], self.sq, self.rstd, self.tmp, l, 1, c["v"], yk="yraw1", xk=c["xk"])
                self.norm_mod(c["xt"], c["xk"], self.hT, "hT", self.sq, self.rstd, self.tmp, l, 2, 3, c["v"])

            def ffn_gu(c):
                for fb in range(6):
                    ncol = 512 if fb < 5 else 256
                    c0 = fb * 512
                    wtg, wkg = self.wr.next()
                    g3 = wtg[:, 0:8 * ncol].rearrange("p (c f) -> p c f", f=ncol)
                    S.dma("pool", g3, wg[:, :, c0:c0 + ncol], writes=[wkg])
                    wtu, wku = self.wr.next()
                    u3 = wtu[:, 0:8 * ncol].rearrange("p (c f) -> p c f", f=ncol)
                    S.dma("pool", u3, wu[:, :, c0:c0 + ncol], writes=[wku])
                    for j in range(ncol // 128):
                        psg, pkg = self.bank()
                        for k in range(8):
                            S.op("pe", lambda e, k=k, j=j, psg=psg, g3=g3: e.matmul(psg[:, :], lhsT=g3[:, k, j * 128:(j + 1) * 128], rhs=self.hT[:, k, :],
                                                                                 start=(k == 0), stop=(k == 7)), reads=[wkg, "hT"], writes=[pkg])
                        psu, pku = self.bank()
                        for k in range(8):
                            S.op("pe", lambda e, k=k, j=j, psu=psu, u3=u3: e.matmul(psu[:, :], lhsT=u3[:, k, j * 128:(j + 1) * 128], rhs=self.hT[:, k, :],
                                                                                 start=(k == 0), stop=(k == 7)), reads=[wku, "hT"], writes=[pku])
                        sg, sk = sgt.next()
                        S.op("act", lambda e, sg=sg, psg=psg: e.activation(out=sg[:], in_=psg[:, :], func=AF.Silu), reads=[pkg], writes=[sk])
                        S.op("dve", lambda e, sg=sg, psu=psu, idx=fb * 4 + j: e.tensor_tensor(out=hid[:, idx, :], in0=psu[:, :], in1=sg[:], op=ALU.mult),
                             reads=[pku, sk], writes=["hid"])

            def ffn_down(c):
                for ob in range(8):
                    wt, wk = self.wr.next()
                    d3 = wt[:, 0:22 * 128].rearrange("p (c f) -> p c f", f=128)
                    S.dma("pool", d3, wd[:, :, ob * 128:(ob + 1) * 128], writes=[wk])
                    ps, pk = self.bank()
                    for k in range(22):
                        S.op("pe", lambda e, k=k, ps=ps, d3=d3: e.matmul(ps[:, :], lhsT=d3[:, k, :], rhs=hid[:, k, :], start=(k == 0), stop=(k == 21)),
                             reads=[wk, "hid"], writes=[pk])
                    self.evac(yr2[:, ob, :], ps[:, :], [pk], ["yraw2"])

            def s4(c):
                xt, xk, t0, g, xv = c["xt"], c["xk"], c["t0"], c["g"], c["xv"]
                self.post_norm_res(yr2, xt, self.sq, self.rstd, self.tmp, l, 3, c["v"], yk="yraw2", xk=xk)
                if not last:
                    S.dma("sp", xv[:, :, t0:t0 + TT], xt[:], reads=[xk], writes=[("xT", g, c["t"])])
                else:
                    for sub in range(TT // 128):
                        a, ak = yo.next()
                        for half in range(2):
                            ps, pk = self.bank()
                            for j in range(4):
                                cc = half * 4 + j
                                S.op("pe", lambda e, ps=ps, j=j, cc=cc, sub=sub: e.transpose(ps[:, j * 128:(j + 1) * 128], xt[:, cc, sub * 128:(sub + 1) * 128], self.ident[:]),
                                     reads=[xk, "ident"], writes=[pk])
                            self.evac(a[:, half * 512:(half + 1) * 512], ps[:, :], [pk], [ak])
                        tok0 = t0 + sub * 128
                        S.dma("sp", self.yout[g][tok0:tok0 + 128, :], a[:], reads=[ak])

            ctx = load(0)
            s1(ctx)
            for t in range(ntile):
                s2(ctx)
                ffn_gu(ctx)
                nxt = load(t + 1) if t + 1 < ntile else None
                ffn_down(ctx)
                if nxt is not None:
                    s1(nxt)
                s4(ctx)
                ctx = nxt
            S.finish()

    def mixers(self, l, g):
        self.mix_pool(l, g)
        self.mix_attn(l, g)
        self.mix_lin_pair(l, g)

    def mix_pool(self, l, g):
        nc, S = self.nc, self.S
        N = self.N[g]
        with ExitStack() as ph:
            pm = self.sb(ph, "pm", (128, 4, 5, 128), F32R)
            wp = self.sb(ph, "wp", (64, 4, 64), F32R)
            psc = self.sb(ph, "psc", (128, 256))
            S.dma("pool", pm[:], self.c_poolm[:, :, :, :], writes=["pm"])
            S.dma("pool", wp[:], self.pool_w[l].rearrange("g c d -> c g d"), writes=["wp"])
            S.dma("sp", psc[:], self.pool_scale[l].partition_broadcast(128), writes=["psc"])
            vr = Rot(nc, ph, "pv", (128, 2, 256), F32R, 2)
            dtr = Rot(nc, ph, "pdt", (64, 4, 128), F32R, 2)
            opr = Rot(nc, ph, "pop", (128, 256), F32, 2)
            for pair in range(N // 256):
                tok0 = pair * 256
                vt, vk = vr.next()
                S.dma("pool", vt[:], self.tm[("vpool", g)][tok0:tok0 + 256, :].rearrange("(c p) f -> p c f", p=128),
                      reads=[("tm", "vpool", g, pair * 2), ("tm", "vpool", g, pair * 2 + 1)], writes=[vk])
                for a in range(2):
                    ps, pk = self.bank()
                    for gi in range(4):
                        if g == "s":
                            contrib = [(a, 0)]
                        else:
                            contrib = [(0, 1 + a * 2 + 0), (1, 1 + a * 2 + 1)]
                        for ci, (b, kind) in enumerate(contrib):
                            S.op("pe", lambda e, gi=gi, b=b, kind=kind, ci=ci, ps=ps, vt=vt, n=len(contrib): e.matmul(
                                ps[0:64, gi * 128:(gi + 1) * 128], lhsT=vt[:, b, gi * 64:(gi + 1) * 64], rhs=pm[:, gi, kind, :],
                                start=(ci == 0), stop=(ci == n - 1)), reads=[vk, "pm"], writes=[pk])
                    dt, dk_ = dtr.next()
                    self.evac(dt[:, :, :], ps[0:64, :].rearrange("p (g t) -> p g t", t=128), [pk], [dk_])
                    ps2, pk2 = self.bank()
                    for gi in range(4):
                        S.op("pe", lambda e, gi=gi, ps2=ps2, dt=dt: e.matmul(ps2[:, gi * 64:(gi + 1) * 64], lhsT=dt[:, gi, :], rhs=wp[:, gi, :],
                                                                         start=True, stop=True), reads=[dk_, "wp"], writes=[pk2])
                    op_, ok = opr.next()
                    S.op("dve", lambda e, op_=op_, ps2=ps2: e.tensor_tensor(out=op_[:], in0=ps2[:, 0:256], in1=psc[:], op=ALU.mult),
                         reads=[pk2, "psc"], writes=[ok])
                    tk0 = tok0 + a * 128
                    S.dma("sp", self.tm[("o", g)][tk0:tk0 + 128, 0:256], op_[:], reads=[ok], writes=[("tm", "o", g, tk0 // 128, 0)])
            S.finish()

    def attn_A(self, u, bufs):
        S = self.S
        sbt, sbks, nk = u["sbt"], u["sbks"], u["nk"]
        sm, smk = bufs["sm"].next()
        prob, prk = bufs["prob"].next()
        S.op("dve", lambda e: e.tensor_reduce(out=sm[:, 0:1], in_=sbt[:, 0:nk], axis=AX.X, op=ALU.max), reads=sbks, writes=[smk + "a"])
        S.op("dve", lambda e: e.tensor_scalar(out=sm[:, 1:2], in0=sm[:, 0:1], scalar1=-1.0, scalar2=None, op0=ALU.mult), reads=[smk + "a"], writes=[smk + "b"])
        S.op("dve", lambda e: e.memset(sm[:, 2:3], 0.0), writes=[smk + "c"])
        S.op("act", lambda e: e.activation(out=prob[:, 0:nk], in_=sbt[:, 0:nk], func=AF.Exp, bias=sm[:, 1:2], scale=1.0, accum_out=sm[:, 2:3]),
             reads=sbks + [smk + "b"], writes=[prk, smk + "c"])
        S.op("dve", lambda e: e.reciprocal(out=sm[:, 3:4], in_=sm[:, 2:3]), reads=[smk + "c"], writes=[smk + "d"])
        u.update(sm=sm, smk=smk, prob=prob, prk=prk)

    def attn_B(self, u, bufs):
        S = self.S
        sm, smk, prob, prk = u["sm"], u["smk"], u["prob"], u["prk"]
        chunks, vk_list, ona, onak, h = u["chunks"], u["vk_list"], u["ona"], u["onak"], u["h"]
        pt, ptk = bufs["pt"].next()
        nch = len(chunks)
        for c4 in range(0, nch, 4):
            ps, pk = self.bank()
            grp = chunks[c4:c4 + 4]
            for j, (col0, kc, vap) in enumerate(grp):
                S.op("pe", lambda e, j=j, col0=col0, kc=kc, ps=ps: e.transpose(ps[:, :].bitcast(BF16)[0:kc, j * 128:(j + 1) * 128], prob[:, col0:col0 + kc], self.ident_b[:]),
                     reads=[prk, "ident_b"], writes=[pk])
            self.evac(pt[:, c4:c4 + len(grp), :], ps[:, :].bitcast(BF16)[:, 0:128 * len(grp)].rearrange("p (j t) -> p j t", t=128), [pk], [ptk + "g%d" % (c4 // 4)])
        ps, pk = self.bank()
        for ci, (col0, kc, vap) in enumerate(chunks):
            S.op("pe", lambda e, ci=ci, kc=kc, vap=vap, ps=ps: e.matmul(ps[:, 0:64], lhsT=pt[0:kc, ci, :], rhs=vap, start=(ci == 0), stop=(ci == nch - 1)),
                 reads=[ptk + "g%d" % (ci // 4)] + vk_list, writes=[pk])
        S.op("act", lambda e, ps=ps: e.activation(out=ona[:, h * 64:(h + 1) * 64], in_=ps[:, 0:64], func=AF.Copy, scale=sm[:, 3:4]),
             reads=[pk, smk + "d"], writes=[onak])
        if u.get("post") is not None:
            u["post"]()

    def mix_attn(self, l, g):
        nc, S = self.nc, self.S
        sc = 0.125
        with ExitStack() as ph:
            bufs = {"sm": Rot(nc, ph, "asm", (128, 4), F32, 4), "pt": Rot(nc, ph, "apt", (128, 7, 128), BF16, 2),
                    "prob": Rot(nc, ph, "aprob", (128, 896), BF16, 3)}
            sbr = Rot(nc, ph, "asb", (128, 896), F32, 3)
            for t_, k_ in zip(sbr.t, sbr.k):
                S.op("dve", lambda e, t_=t_: e.memset(t_[:], NEG), writes=[k_ + "pad"])
            onr = Rot(nc, ph, "aon", (128, 256), F32, 3)
            units = []

            if g == "p":
                qr = Rot(nc, ph, "aq", (64, 4, 256), F32R, 2)
                kr = Rot(nc, ph, "ak", (64, 4, 256), F32R, 2)
                vr = Rot(nc, ph, "av", (128, 2, 256), BF16, 2)

                def make_seq(s):
                    T0 = s * 256
                    qt, qk = qr.next()
                    kt, kk = kr.next()
                    vt, vk = vr.next()
                    S.dma("pool", qt[:], self.fm[("naq", g)][:, T0:T0 + 256].rearrange("(h d) t -> d h t", d=64), reads=[("fm", "naq", g, T0 // TT)], writes=[qk])
                    S.dma("pool", kt[:], self.fm[("nak", g)][:, T0:T0 + 256].rearrange("(h d) t -> d h t", d=64), reads=[("fm", "nak", g, T0 // TT)], writes=[kk])
                    S.dma("pool", vt[:], self.tm[("nav", g)][T0:T0 + 256, :].rearrange("(c p) f -> p c f", p=128),
                          reads=[("tm", "nav", g, T0 // 128), ("tm", "nav", g, T0 // 128 + 1)], writes=[vk])
                    return dict(qt=qt, qk=qk, kt=kt, kk=kk, vt=vt, vk=vk, T0=T0)

                def unit_p(sq, qb, h, onab):
                    def f():
                        if "c" not in sq:
                            sq["c"] = make_seq(sq["s"])
                        c = sq["c"]
                        if "o" not in onab:
                            onab["o"] = onr.next()
                        ona, onak = onab["o"]
                        ps, pk = self.bank()
                        S.op("pe", lambda e: e.matmul(ps[:, 0:256], lhsT=c["qt"][:, h, qb * 128:(qb + 1) * 128], rhs=c["kt"][:, h, :], start=True, stop=True),
                             reads=[c["qk"], c["kk"]], writes=[pk])
                        sbt, sbk = sbr.next()
                        S.op("act", lambda e: e.activation(out=sbt[:, 0:256], in_=ps[:, 0:256], func=AF.Copy, scale=sc), reads=[pk], writes=[sbk])
                        chunks = [(0, 128, c["vt"][:, 0, h * 64:(h + 1) * 64]), (128, 128, c["vt"][:, 1, h * 64:(h + 1) * 64])]
                        u = dict(sbt=sbt, sbks=[sbk], nk=256, chunks=chunks, vk_list=[c["vk"]], ona=ona, onak=onak, h=h, post=None)
                        if h == 3:
                            tk0 = c["T0"] + qb * 128
                            u["post"] = lambda: S.dma("act", self.tm[("o", g)][tk0:tk0 + 128, 256:512], ona[:], reads=[onak], writes=[("tm", "o", g, tk0 // 128, 1)])
                        return u
                    return f
                for s in range(4):
                    sq = {"s": s}
                    for qb in range(2):
                        onab = {}
                        for h in range(4):
                            units.append(unit_p(sq, qb, h, onab))
            else:
                ckt = self.sb(ph, "ckt", (128, 2, 256))
                ckT = self.sb(ph, "ckT", (64, 4, 256), F32R)
                cvt = self.sb(ph, "cvt2", (128, 2, 256), BF16)
                S.dma("sp", ckt[:], self.ck[l].rearrange("(c p) f -> p c f", p=128), writes=["ckt"])
                S.dma("pool", cvt[:], self.cv[l].rearrange("(c p) f -> p c f", p=128), writes=["cvt2"])
                for h in range(4):
                    ps, pk = self.bank()
                    for c in range(2):
                        S.op("pe", lambda e, h=h, c=c, ps=ps: e.transpose(ps[0:64, c * 128:(c + 1) * 128], ckt[:, c, h * 64:(h + 1) * 64], self.ident[:]),
                             reads=["ckt", "ident"], writes=[pk])
                    self.evac(ckT[:, h, :], ps[0:64, 0:256], [pk], ["ckT"])
                qr = Rot(nc, ph, "aq", (64, 4, 128), F32R, 3)
                kr = Rot(nc, ph, "ak", (64, 4, 576), F32R, 3)
                vr = Rot(nc, ph, "av", (128, 5, 256), BF16, 3)
                br = Rot(nc, ph, "ab", (128, 4, 576), F32, 3)

                def make_l2(l2):
                    base = _na_base(l2)
                    var = _na_var(l2)
                    T0 = l2 * 128
                    K0 = base * 64
                    qt, qk = qr.next()
                    kt, kk = kr.next()
                    vt, vk = vr.next()
                    bt, bk = br.next()
                    kreads = [("fm", "nak", g, t) for t in range(K0 // TT, min((K0 + 575) // TT, NSM // TT - 1) + 1)] + [("fm", "nak", "s", "pad")]
                    vreads = [("tm", "nav", g, t) for t in range(K0 // 128, min((K0 + 639) // 128, NSM // 128 - 1) + 1)] + [("tm", "nav", "s", "pad")]
                    S.dma("pool", qt[:], self.fm[("naq", g)][:, T0:T0 + 128].rearrange("(h d) t -> d h t", d=64), reads=[("fm", "naq", g, T0 // TT)], writes=[qk])
                    S.dma("pool", kt[:], self.fm[("nak", g)][:, K0:K0 + 576].rearrange("(h d) t -> d h t", d=64), reads=kreads, writes=[kk])
                    S.dma("pool", vt[:], self.tm[("nav", g)][K0:K0 + 640, :].rearrange("(c p) f -> p c f", p=128), reads=vreads, writes=[vk])
                    S.dma("sp", bt[:], self.nabias[l, var].rearrange("h q k -> q h k"), writes=[bk])
                    return dict(qt=qt, qk=qk, kt=kt, kk=kk, vt=vt, vk=vk, bt=bt, bk=bk, T0=T0)

                def unit_s(lq, h, onab):
                    def f():
                        if "c" not in lq:
                            lq["c"] = make_l2(lq["l2"])
                        c = lq["c"]
                        if "o" not in onab:
                            onab["o"] = onr.next()
                        ona, onak = onab["o"]
                        qt, kt, vt, bt = c["qt"], c["kt"], c["vt"], c["bt"]
                        qk, kk, vk, bk = c["qk"], c["kk"], c["vk"], c["bk"]
                        psA, pkA = self.bank()
                        psB, pkB = self.bank()
                        S.op("pe", lambda e: e.matmul(psA[:, 0:512], lhsT=qt[:, h, :], rhs=kt[:, h, 0:512], start=True, stop=True), reads=[qk, kk], writes=[pkA])
                        S.op("pe", lambda e: e.matmul(psB[:, 0:64], lhsT=qt[:, h, :], rhs=kt[:, h, 512:576], start=True, stop=True), reads=[qk, kk], writes=[pkB])
                        S.op("pe", lambda e: e.matmul(psB[:, 64:320], lhsT=qt[:, h, :], rhs=ckT[:, h, :], start=True, stop=True), reads=[qk, "ckT"], writes=[pkB])
                        sbt, sbk = sbr.next()
                        S.op("dve", lambda e: e.scalar_tensor_tensor(out=sbt[:, 0:512], in0=psA[:, 0:512], scalar=sc, in1=bt[:, h, 0:512], op0=ALU.mult, op1=ALU.add),
                             reads=[pkA, bk], writes=[sbk + "a"])
                        S.op("dve", lambda e: e.scalar_tensor_tensor(out=sbt[:, 512:576], in0=psB[:, 0:64], scalar=sc, in1=bt[:, h, 512:576], op0=ALU.mult, op1=ALU.add),
                             reads=[pkB, bk], writes=[sbk + "b"])
                        S.op("dve", lambda e: e.tensor_scalar(out=sbt[:, 640:896], in0=psB[:, 64:320], scalar1=sc, scalar2=None, op0=ALU.mult), reads=[pkB], writes=[sbk + "c"])
                        chunks = [(cc * 128, 128, vt[:, cc, h * 64:(h + 1) * 64]) for cc in range(5)]
                        chunks.append((640, 128, cvt[:, 0, h * 64:(h + 1) * 64]))
                        chunks.append((768, 128, cvt[:, 1, h * 64:(h + 1) * 64]))
                        u = dict(sbt=sbt, sbks=[sbk + "a", sbk + "b", sbk + "c", sbk + "pad"], nk=896, chunks=chunks, vk_list=[vk, "cvt2"],
                                 ona=ona, onak=onak, h=h, post=None)
                        if h == 3:
                            T0 = c["T0"]
                            u["post"] = lambda: S.dma("act", self.tm[("o", g)][T0:T0 + 128, 256:512], ona[:], reads=[onak], writes=[("tm", "o", g, T0 // 128, 1)])
                        return u
                    return f
                for l2 in range(32):
                    lq = {"l2": l2}
                    onab = {}
                    for h in range(4):
                        units.append(unit_s(lq, h, onab))
            prev = None
            for mk in units:
                u = mk()
                self.attn_A(u, bufs)
                if prev is not None:
                    self.attn_B(prev, bufs)
                prev = u
            self.attn_B(prev, bufs)
            S.finish()

    def mix_lin_gen(self, l, g, kind, ph):
        nc, S = self.nc, self.S
        ret = kind == "ret"
        dk = 64 if ret else 32
        HD = 4 * dk
        qscale = dk ** -0.5
        N = self.N[g]
        seqlen = 256 if g == "p" else NSM
        nseq = N // seqlen
        nblk = seqlen // 128
        qn, kn = ("rq", "rk") if ret else ("aq", "ak")
        rope = ret and g == "s"
        X = kind + "_"
        if True:
            if ret:
                lg = self.sb(ph, X + "lg", (128, 8))
                gconst = self.sb(ph, X + "gconst", (128, 2, 256))
                S.dma("sp", lg[:], self.ret_logit[l].partition_broadcast(128), writes=[X + "lg"])
                S.op("act", lambda e: e.activation(out=lg[:], in_=lg[:], func=AF.Exp, scale=-1.0), reads=[X + "lg"], writes=[X + "lg"])
                S.op("act", lambda e: e.activation(out=lg[:], in_=lg[:], func=AF.Ln, bias=self.ones_f[:, 0:1], scale=1.0), reads=[X + "lg", "ones_f"], writes=[X + "lg"])
                S.op("dve", lambda e: e.tensor_scalar(out=gconst[:].rearrange("p a (h d) -> p (a h) d", d=64), in0=lg[:].unsqueeze(2).to_broadcast([128, 8, 64]),
                                                      scalar1=-1.0, scalar2=None, op0=ALU.mult), reads=[X + "lg"], writes=[X + "gconst"])
            else:
                gup = self.sb(ph, X + "gup", (17, 2, 128), F32R)
                S.dma("pool", gup[0:16, :, :], self.gate_up[l].rearrange("a r f -> r a f"), writes=[X + "gup"])
                S.dma("pool", gup[16:17, :, :], self.gate_b[l:l + 1, :, :], writes=[X + "gup"])
                ng = self.sb(ph, X + "ng", (128, 64))
                S.dma("sp", ng[:], self.norm_g[l].partition_broadcast(128), writes=[X + "ng"])
                lrr = Rot(nc, ph, X + "lrr", (17, 128), F32R, 3)
                for t_, k_ in zip(lrr.t, lrr.k):
                    S.op("dve", lambda e, t_=t_: e.tensor_copy(out=t_[:], in_=self.ones_f[0:17, :]), reads=["ones_f"], writes=[k_])
                gtr = Rot(nc, ph, X + "gtr", (128, 128), F32, 3)
                lsr = Rot(nc, ph, X + "lsr", (16, 128), F32, 3)
            if rope:
                cosT = self.sb(ph, X + "cosT", (64, NSM))
                sinT = self.sb(ph, X + "sinT", (64, NSM))
                S.dma("sp", cosT[:], self.c_cos[:, :], writes=[X + "cosT"])
                S.dma("sp", sinT[:], self.c_sin[:, :], writes=[X + "sinT"])
                rt = Rot(nc, ph, X + "rt", (64, 4, 128), F32, 5)
                qkr = Rot(nc, ph, X + "qkr", (64, 4, 128), F32, 3)
            epsg = self.sb(ph, X + "epsg", (128, 1))
            S.op("dve", lambda e: e.memset(epsg[:], 1e-5 if ret else 1e-6), writes=[X + "epsg"])
            sts = [self.sb(ph, "state%d" % d, (dk, 4, 64), F32) for d in range(2)]
            stb = [self.sb(ph, "stateb%d" % d, (dk, 4, 64), BF16) for d in range(2)]
            qr = Rot(nc, ph, X + "lq", (dk, 4, 128), F32, 3)
            kr = Rot(nc, ph, X + "lk", (dk, 4, 128), F32, 3)
            vr = Rot(nc, ph, X + "lv", (128, 256), BF16, 3)
            vsr = Rot(nc, ph, X + "lvs", (128, 256), F32, 3)
            gr = Rot(nc, ph, X + "lgt", (128, 256), F32, 2)
            ofr = Rot(nc, ph, X + "lof", (128, 256), F32, 3)
            epr = Rot(nc, ph, X + "lep", (dk, 512), F32, 3)
            enr = Rot(nc, ph, X + "len", (dk, 512), F32, 3)
            qgr = Rot(nc, ph, X + "lqg", (dk, 4, 128), BF16, 3)
            kgr = Rot(nc, ph, X + "lkg", (dk, 4, 128), BF16, 3)
            ktr = Rot(nc, ph, X + "lkt", (128, HD), BF16, 3)
            amr = Rot(nc, ph, X + "lam", (128, 512), BF16, 3)
            tsr = Rot(nc, ph, X + "lts", (dk, 256), F32, 2)
            osr = Rot(nc, ph, X + "los", (128, 256), F32, 2)
            o2r = Rot(nc, ph, X + "lo2", (128, 256), F32, 2)
            sqr = Rot(nc, ph, X + "lsq", (128, 256), F32, 2)
            str_ = Rot(nc, ph, X + "lst", (128, 16), F32, 2)
            sgr = Rot(nc, ph, X + "lsg", (128, 256), F32, 2)
            col0 = 512 if ret else 768

            def stage_a(s, d, n):
                U = self.uf if d == 0 else self.ub
                Uk = "uf" if d == 0 else "ub"
                if ret:
                    Ug, Ugk = U, Uk
                else:
                    Ug = self.ufg if d == 0 else self.ubg
                    Ugk = "ufg" if d == 0 else "ubg"
                T0 = s * seqlen + n * 128
                qt, qk = qr.next()
                kt, kk = kr.next()
                vt, vk = vr.next()
                S.dma("sp", qt[:], self.fm[(qn, g)][:, T0:T0 + 128].rearrange("(h d) t -> d h t", d=dk), reads=[("fm", qn, g, T0 // TT)], writes=[qk])
                S.dma("sp", kt[:], self.fm[(kn, g)][:, T0:T0 + 128].rearrange("(h d) t -> d h t", d=dk), reads=[("fm", kn, g, T0 // TT)], writes=[kk])
                vsrc = self.tm[("rvg", g)][T0:T0 + 128, 0:256] if ret else self.tm[("av", g)][T0:T0 + 128, :]
                vs_, vsk = vsr.next()
                S.dma("sp", vs_[:], vsrc, reads=[("tm", "rvg" if ret else "av", g, T0 // 128)], writes=[vsk])
                S.op("act", lambda e, vt=vt, vs_=vs_: e.activation(out=vt[:], in_=vs_[:], func=AF.Copy), reads=[vsk], writes=[vk])
                qf, kf, qfk, kfk = qt[:], kt[:], qk, kk
                if rope:
                    outs = []
                    pos0 = n * 128
                    cb = cosT[:, pos0:pos0 + 128].unsqueeze(1).to_broadcast([64, 4, 128])
                    sbb = sinT[:, pos0:pos0 + 128].unsqueeze(1).to_broadcast([64, 4, 128])
                    for (src_t, src_k, nm_) in ((qt, qk, qn), (kt, kk, kn)):
                        rr, rrk = rt.next()
                        srcv = self.fm[(nm_, g)][:, T0:T0 + 128].rearrange("(h b two s) t -> two b s h t", h=4, b=2, two=2, s=16)
                        for bb_ in range(2):
                            for tw in range(2):
                                p0 = bb_ * 32 + tw * 16
                                S.dma("sp", rr[p0:p0 + 16, :, :], srcv[1 - tw, bb_], reads=[("fm", nm_, g, T0 // TT)], writes=[rrk])
                        a1, a1k = rt.next()
                        S.op("pool", lambda e, a1=a1, src_t=src_t: e.tensor_tensor(out=a1[:], in0=src_t[:], in1=cb, op=ALU.mult), reads=[src_k, X + "cosT"], writes=[a1k])
                        S.op("dve", lambda e, rr=rr: e.tensor_tensor(out=rr[:], in0=rr[:], in1=sbb, op=ALU.mult), reads=[rrk, X + "sinT"], writes=[rrk])
                        o_, ok_ = qkr.next()
                        S.op("dve", lambda e, o_=o_, a1=a1, rr=rr: e.tensor_tensor(out=o_[:], in0=a1[:], in1=rr[:], op=ALU.add), reads=[a1k, rrk], writes=[ok_])
                        outs.append((o_, ok_))
                    qf, qfk = outs[0][0][:], outs[0][1]
                    kf, kfk = outs[1][0][:], outs[1][1]
                if ret:
                    gate_ap, gate_k = gconst[:, d, :], X + "gconst"
                else:
                    lrt, lrk = lrr.next()
                    ls_, lsk = lsr.next()
                    S.dma("sp", ls_[:], self.fm[("alr", g)][:, T0:T0 + 128], reads=[("fm", "alr", g, T0 // TT)], writes=[lsk])
                    S.op("dve", lambda e, lrt=lrt, ls_=ls_: e.tensor_copy(out=lrt[0:16, :], in_=ls_[:]), reads=[lsk], writes=[lrk])
                    ps, pk = self.bank()
                    S.op("pe", lambda e, ps=ps, lrt=lrt: e.matmul(ps[:, 0:128], lhsT=lrt[:, :], rhs=gup[:, d, :], start=True, stop=True), reads=[lrk, X + "gup"], writes=[pk])
                    gt_, gtk = gtr.next()
                    S.op("act", lambda e, ps=ps, gt_=gt_: e.activation(out=gt_[:], in_=ps[:, 0:128], func=AF.Exp, scale=-1.0), reads=[pk], writes=[gtk])
                    S.op("act", lambda e, gt_=gt_: e.activation(out=gt_[:], in_=gt_[:], func=AF.Ln, bias=self.ones_f[:, 0:1], scale=1.0), reads=[gtk, "ones_f"], writes=[gtk])
                    gate_ap, gate_k = gt_[:], gtk
                bps, bpk = self.bank()
                for h in range(4):
                    S.op("pe", lambda e, h=h, bps=bps, gate_ap=gate_ap: e.matmul(bps[0:dk, h * 128:(h + 1) * 128], lhsT=gate_ap[:, h * dk:(h + 1) * dk], rhs=Ug[:], start=True, stop=True),
                         reads=[gate_k, Ugk], writes=[bpk])
                ep, epk = epr.next()
                en, enk = enr.next()
                S.op("act", lambda e, ep=ep, bps=bps: e.activation(out=ep[:], in_=bps[0:dk, :], func=AF.Exp), reads=[bpk], writes=[epk])
                S.op("act", lambda e, en=en, bps=bps: e.activation(out=en[:], in_=bps[0:dk, :], func=AF.Exp, scale=-1.0), reads=[bpk], writes=[enk])
                qg, qgk = qgr.next()
                kg, kgk = kgr.next()
                S.op("dve", lambda e, qg=qg, ep=ep, qf=qf: e.scalar_tensor_tensor(out=qg[:].rearrange("p h t -> p (h t)"), in0=qf.rearrange("p h t -> p (h t)"), scalar=qscale, in1=ep[:], op0=ALU.mult, op1=ALU.mult),
                     reads=[qfk, epk], writes=[qgk])
                S.op("dve", lambda e, kg=kg, en=en, kf=kf: e.tensor_tensor(out=kg[:].rearrange("p h t -> p (h t)"), in0=kf.rearrange("p h t -> p (h t)"), in1=en[:], op=ALU.mult),
                     reads=[kfk, enk], writes=[kgk])
                tps, tpk = self.bank()
                for h in range(4):
                    S.op("pe", lambda e, h=h, tps=tps, kg=kg: e.transpose(tps[:, :].bitcast(BF16)[:, h * dk:(h + 1) * dk], kg[:, h, :], self.ident_b[0:dk, 0:dk]),
                         reads=[kgk, "ident_b"], writes=[tpk])
                ktok, ktk = ktr.next()
                self.evac(ktok[:], tps[:, :].bitcast(BF16)[:, 0:HD], [tpk], [ktk])
                aps, apk = self.bank()
                for h in range(4):
                    S.op("pe", lambda e, h=h, aps=aps, kg=kg, qg=qg: e.matmul(aps[:, h * 128:(h + 1) * 128], lhsT=kg[:, h, :], rhs=qg[:, h, :], start=True, stop=True), reads=[kgk, qgk], writes=[apk])
                am, amk = amr.next()
                S.op("dve", lambda e, am=am, aps=aps: e.tensor_tensor(out=am[:].rearrange("p (h t) -> p h t", t=128), in0=aps[:, :].rearrange("p (h t) -> p h t", t=128),
                                                                      in1=U[:].unsqueeze(1).to_broadcast([128, 4, 128]), op=ALU.mult), reads=[apk, Uk], writes=[amk])
                ams = [(am, amk)] * 4
                return dict(s=s, d=d, n=n, T0=T0, vt=vt, vk=vk, ep=ep, epk=epk, qg=qg, qgk=qgk, ktok=ktok, ktk=ktk, ams=ams)

            def stage_b(c):
                s, d, n, T0 = c["s"], c["d"], c["n"], c["T0"]
                vt, vk, ep, epk, qg, qgk, ktok, ktk, ams = c["vt"], c["vk"], c["ep"], c["epk"], c["qg"], c["qgk"], c["ktok"], c["ktk"], c["ams"]
                st_ = sts[d]
                stk = X + "state%d" % d
                last = 127 if d == 0 else 0
                first_blk = (n == 0) if d == 0 else (n == nblk - 1)
                last_blk = (n == nblk - 1) if d == 0 else (n == 0)
                sb_ = stb[d]
                sbk_ = stk + "b"
                if first_blk:
                    if g == "p":
                        S.op("dve", lambda e: e.tensor_copy(out=st_[:], in_=self.zeros_f[0:dk, :].rearrange("p (h v) -> p h v", v=64)), reads=["zeros_f"], writes=[stk])
                    else:
                        src = (self.sret if ret else self.sgla)[l, d].rearrange("h d v -> d h v")
                        S.dma("sp", st_[:], src, writes=[stk])
                    S.op("dve", lambda e: e.tensor_copy(out=sb_[:], in_=st_[:]), reads=[stk], writes=[sbk_])
                ops_, opk = self.bank()
                for h in range(4):
                    am, amk = ams[h]
                    S.op("pe", lambda e, h=h, am=am: e.matmul(ops_[:, h * 64:(h + 1) * 64], lhsT=am[:, h * 128:(h + 1) * 128], rhs=vt[:, h * 64:(h + 1) * 64], start=True, stop=False), reads=[amk, vk], writes=[opk])
                    S.op("pe", lambda e, h=h: e.matmul(ops_[:, h * 64:(h + 1) * 64], lhsT=qg[:, h, :], rhs=sb_[:, h, :], start=False, stop=True), reads=[qgk, sbk_], writes=[opk])
                sps, spk = self.bank()
                for h in range(4):
                    S.op("pe", lambda e, h=h: e.matmul(sps[0:dk, h * 64:(h + 1) * 64], lhsT=ktok[:, h * dk:(h + 1) * dk], rhs=vt[:, h * 64:(h + 1) * 64], start=True, stop=True),
                         reads=[ktk, vk], writes=[spk])
                ts, tsk = tsr.next()
                S.op("dve", lambda e: e.tensor_tensor(out=ts[:], in0=sps[0:dk, 0:256], in1=st_[:].rearrange("p h v -> p (h v)"), op=ALU.add), reads=[spk, stk], writes=[tsk])
                eb = ep[:].rearrange("p (h t) -> p h t", t=128)[:, :, last:last + 1].to_broadcast([dk, 4, 64])
                S.op("dve", lambda e: e.tensor_tensor(out=st_[:], in0=ts[:].rearrange("p (h v) -> p h v", v=64), in1=eb, op=ALU.mult), reads=[tsk, epk], writes=[stk])
                S.op("pool", lambda e: e.tensor_copy(out=sb_[:], in_=st_[:]), reads=[stk], writes=[sbk_])
                if d == 0:
                    of, ofk = ofr.next()
                    self.evac(of[:], ops_[:, 0:256], [opk], [ofk])
                    S.dma("pool", self.tm[("of" + kind, g)][T0:T0 + 128, :], of[:], reads=[ofk], writes=[("tm", "of" + kind, g, T0 // 128)])
                else:
                    of, ofk = ofr.next()
                    S.dma("sp", of[:], self.tm[("of" + kind, g)][T0:T0 + 128, :], reads=[("tm", "of" + kind, g, T0 // 128)], writes=[ofk])
                    gt2, g2k = gr.next()
                    gsrc = self.tm[("rvg", g)][T0:T0 + 128, 256:512] if ret else self.tm[("ag", g)][T0:T0 + 128, :]
                    S.dma("sp", gt2[:], gsrc, reads=[("tm", "rvg" if ret else "ag", g, T0 // 128)], writes=[g2k])
                    osum, osk = osr.next()
                    S.op("dve", lambda e: e.tensor_tensor(out=osum[:], in0=ops_[:, 0:256], in1=of[:], op=ALU.add), reads=[opk, ofk], writes=[osk])
                    o3 = osum[:].rearrange("p (h v) -> p h v", v=64)
                    sq_, sqk = sqr.next()
                    stt, stk2 = str_.next()
                    S.op("act", lambda e: e.activation(out=sq_[:], in_=osum[:], func=AF.Square), reads=[osk], writes=[sqk])
                    S.op("dve", lambda e: e.tensor_reduce(out=stt[:, 4:8], in_=sq_[:].rearrange("p (h v) -> p h v", v=64), axis=AX.X, op=ALU.add), reads=[sqk], writes=[stk2 + "b"])
                    o2, o2k = o2r.next()
                    if ret:
                        S.op("dve", lambda e: e.tensor_reduce(out=stt[:, 0:4], in_=o3, axis=AX.X, op=ALU.add), reads=[osk], writes=[stk2 + "a"])
                        S.op("dve", lambda e: e.tensor_scalar(out=stt[:, 0:4], in0=stt[:, 0:4], scalar1=1.0 / 64, scalar2=None, op0=ALU.mult), reads=[stk2 + "a"], writes=[stk2 + "a"])
                        S.op("dve", lambda e: e.tensor_tensor(out=stt[:, 8:12], in0=stt[:, 0:4], in1=stt[:, 0:4], op=ALU.mult), reads=[stk2 + "a"], writes=[stk2 + "c"])
                        S.op("dve", lambda e: e.scalar_tensor_tensor(out=stt[:, 12:16], in0=stt[:, 4:8], scalar=1.0 / 64, in1=stt[:, 8:12], op0=ALU.mult, op1=ALU.subtract), reads=[stk2 + "b", stk2 + "c"], writes=[stk2 + "d"])
                    else:
                        S.op("dve", lambda e: e.tensor_scalar(out=stt[:, 12:16], in0=stt[:, 4:8], scalar1=1.0 / 64, scalar2=None, op0=ALU.mult), reads=[stk2 + "b"], writes=[stk2 + "d"])
                    S.op("act", lambda e: e.activation(out=stt[:, 12:16], in_=stt[:, 12:16], func=AF.Sqrt, bias=epsg[:, 0:1], scale=1.0), reads=[stk2 + "d", X + "epsg"], writes=[stk2 + "d"])
                    S.op("dve", lambda e: e.reciprocal(out=stt[:, 12:16], in_=stt[:, 12:16]), reads=[stk2 + "d"], writes=[stk2 + "d"])
                    rb = stt[:, 12:16].unsqueeze(2).to_broadcast([128, 4, 64])
                    o23 = o2[:].rearrange("p (h v) -> p h v", v=64)
                    if ret:
                        mb = stt[:, 0:4].unsqueeze(2).to_broadcast([128, 4, 64])
                        S.op("dve", lambda e: e.tensor_tensor(out=o23, in0=o3, in1=mb, op=ALU.subtract), reads=[osk, stk2 + "a"], writes=[o2k])
                        S.op("dve", lambda e: e.tensor_tensor(out=o23, in0=o23, in1=rb, op=ALU.mult), reads=[o2k, stk2 + "d"], writes=[o2k])
                    else:
                        S.op("dve", lambda e: e.tensor_tensor(out=o23, in0=o3, in1=rb, op=ALU.mult), reads=[osk, stk2 + "d"], writes=[o2k])
                        nb = ng[:].unsqueeze(1).to_broadcast([128, 4, 64])
                        S.op("dve", lambda e: e.tensor_tensor(out=o23, in0=o23, in1=nb, op=ALU.mult), reads=[o2k, X + "ng"], writes=[o2k])
                    sg_, sgk = sgr.next()
                    S.op("act", lambda e: e.activation(out=sg_[:], in_=gt2[:], func=AF.Silu), reads=[g2k], writes=[sgk])
                    S.op("pool", lambda e: e.tensor_tensor(out=o2[:], in0=o2[:], in1=sg_[:], op=ALU.mult), reads=[o2k, sgk], writes=[o2k])
                    S.dma("pool", self.tm[("o", g)][T0:T0 + 128, col0:col0 + 256], o2[:], reads=[o2k], writes=[("tm", "o", g, T0 // 128, 2 if ret else 3)])
                if last_blk and g == "p":
                    dst = (self.o_sret if ret else self.o_sgla)[s, l, d].rearrange("h d v -> d h v")
                    S.dma("pool", dst, st_[:], reads=[stk])

            seqn = []
            for s in range(nseq):
                for d in range(2):
                    order = range(nblk) if d == 0 else range(nblk - 1, -1, -1)
                    for n in order:
                        seqn.append((s, d, n))
            prev = None
            for (s, d, n) in seqn:
                c = stage_a(s, d, n)
                if prev is not None:
                    stage_b(prev)
                prev = c
                yield
            stage_b(prev)
            yield


    def mix_lin_pair(self, l, g):
        with ExitStack() as ph:
            gens = [self.mix_lin_gen(l, g, "ret", ph), self.mix_lin_gen(l, g, "gla", ph)]
            while gens:
                for gen in list(gens):
                    try:
                        next(gen)
                    except StopIteration:
                        gens.remove(gen)
            self.S.finish()


def _alloc_zeros(b):
    pass


_CACHE = {}


def _build(depth=DEPTH):
    if depth not in _CACHE:
        _CACHE[depth] = Builder(depth)
    return _CACHE[depth]


def kernel(x_prompt, x_sample, cache_na_k, cache_na_v, state_ret, state_gla, c, c_ctx,
           w_mod, b_mod, g_pre_mix, g_post_mix, g_pre_ffn, g_post_ffn, w_in, w_out,
           pool_w, pool_scale, na_rpb, ret_decay_logit, gla_gate_up, gla_gate_b, gla_norm_g,
           w_ffn_gate, w_ffn_up, w_ffn_down, _depth=DEPTH):
    f = lambda a: np.ascontiguousarray(np.asarray(a, dtype=np.float32))
    L = _depth
    b = _build(L)

    def fm_(a, nch):
        return np.ascontiguousarray(a.reshape(a.shape[0], nch, 128).transpose(2, 0, 1))
    cos, sin, perm = _rope_consts()
    idx = np.arange(128)
    shared = {
        "w_mod": f(w_mod)[:L], "b_mod": fm_(f(b_mod)[:L], 48),
        "g_pre_mix": fm_(f(g_pre_mix)[:L], 8), "g_post_mix": fm_(f(g_post_mix)[:L], 8), "g_pre_ffn": fm_(f(g_pre_ffn)[:L], 8), "g_post_ffn": fm_(f(g_post_ffn)[:L], 8),
        "w_in": f(w_in)[:L], "w_out": f(w_out)[:L], "pool_w": f(pool_w)[:L], "pool_scale": f(pool_scale)[:L],
        "ret_decay_logit": f(ret_decay_logit)[:L].reshape(L, 8), "gla_gate_up": f(gla_gate_up)[:L], "gla_gate_b": f(gla_gate_b)[:L],
        "gla_norm_g": f(gla_norm_g)[:L], "w_ffn_gate": f(w_ffn_gate)[:L], "w_ffn_up": f(w_ffn_up)[:L], "w_ffn_down": f(w_ffn_down)[:L],
        "nabias": _na_bias_tables(f(na_rpb)[:L]),
        "c_ident": np.eye(128, dtype=np.float32),
        "c_uf": (idx[:, None] <= idx[None, :]).astype(np.float32),
        "c_ub": (idx[:, None] >= idx[None, :]).astype(np.float32),
        "c_ufg": (idx[:, None] <= idx[None, :]).astype(np.float32) * np.float32(-1.0 / 16.0),
        "c_ubg": (idx[:, None] >= idx[None, :]).astype(np.float32) * np.float32(-1.0 / 16.0),
        "c_poolm": _pool_consts(), "c_cos": cos, "c_sin": sin, "c_perm": perm,
        "c_zero": np.zeros((128, 256), np.float32),
    }
    xp = f(x_prompt)
    xs = f(x_sample)
    cc = f(c)
    cctx = f(c_ctx)
    in_maps = []
    for core in range(8):
        bb = core % 2
        m = dict(shared)
        m["xp"] = xp[core * 4:(core + 1) * 4].reshape(NPR, D)
        m["xs"] = xs[bb]
        m["cvec"] = np.ascontiguousarray(np.stack([cctx, cc[bb]], axis=0).reshape(2, 8, 128).transpose(2, 1, 0))
        m["ck"] = f(cache_na_k)[bb, :L].reshape(L, 256, 256)
        m["cv"] = f(cache_na_v)[bb, :L].reshape(L, 256, 256)
        m["sret"] = f(state_ret)[bb, :L]
        m["sgla"] = f(state_gla)[bb, :L]
        in_maps.append(m)
    res = run_bass_kernel_spmd(b.nc, in_maps, core_ids=list(range(8)))
    R = res.results
    y_prompt = np.concatenate([R[i]["yp"].reshape(4, 256, D) for i in range(8)], axis=0)
    y_sample = np.stack([R[0]["ys"], R[1]["ys"]], axis=0)
    nk = np.concatenate([R[i]["o_ck"].reshape(4, L, 256, 4, 64) for i in range(8)], axis=0)
    nv = np.concatenate([R[i]["o_cv"].reshape(4, L, 256, 4, 64) for i in range(8)], axis=0)
    sr = np.concatenate([R[i]["o_sret"] for i in range(8)], axis=0)
    sg = np.concatenate([R[i]["o_sgla"] for i in range(8)], axis=0)
    return (y_prompt.astype(np.float32), y_sample.astype(np.float32), nk.astype(np.float32), nv.astype(np.float32),
            sr.astype(np.float32), sg.astype(np.float32))
```
